# Optimizing a Trainium2 kernel written in Bass

```python
import math
import jax, jax.numpy as jnp
from jax import lax
import numpy as np

D_MODEL = 1024
BATCH = 8
SEQ = 4096
DEPTH = 1

CHUNK = 64
Q_BLOCK = 128
EPS = 1e-6
GLA_HEADS = 4
GLA_DK = D_MODEL // 2 // GLA_HEADS
GLA_DV = D_MODEL // GLA_HEADS
GLA_RANK = 16
GLA_TAU = 16.0
DIFF_HEADS = 8
DIFF_DH = 64
DIFF_DV = 2 * DIFF_DH
D_FF = 2816
CONV_W = 3
IN_SIZES = (
    GLA_HEADS * GLA_DK,
    GLA_HEADS * GLA_DK,
    GLA_HEADS * GLA_DV,
    GLA_HEADS * GLA_DV,
    GLA_RANK,
    DIFF_HEADS * 2 * DIFF_DH,
    DIFF_HEADS * 2 * DIFF_DH,
    DIFF_HEADS * DIFF_DV,
    D_MODEL,
    D_MODEL,
)
IN_DIM = sum(IN_SIZES)

kernel_name = "hybrid_gla_diffattn_convffn_block"


def rms_norm(x, g):
    xf = x.astype(jnp.float32)
    y = xf * lax.rsqrt(jnp.mean(xf * xf, axis=-1, keepdims=True) + EPS)
    return (y * g.astype(jnp.float32)).astype(x.dtype)


def gla_mixer(q, k, v, g, a_code, w_alpha_up, b_alpha, norm_g):
    dt = q.dtype
    f32 = jnp.float32
    B, S, _ = q.shape
    nc = S // CHUNK
    log_a = jax.nn.log_sigmoid((a_code @ w_alpha_up + b_alpha).astype(f32)) / GLA_TAU

    def heads(t, d):
        return t.astype(f32).reshape(B, nc, CHUNK, GLA_HEADS, d).transpose(0, 3, 1, 2, 4)

    qh = heads(q, GLA_DK) * (GLA_DK ** -0.5)
    kh = heads(k, GLA_DK)
    vh = heads(v, GLA_DV)
    bcum = jnp.cumsum(heads(log_a, GLA_DK), axis=3)
    b_last = bcum[..., -1:, :]

    q_fwd = qh * jnp.exp(bcum)
    k_fwd = kh * jnp.exp(-bcum)
    q_bwd = qh * jnp.exp(-bcum)
    k_bwd = kh * jnp.exp(bcum)
    s_fwd = jnp.einsum('bhncd,bhnsd->bhncs', q_fwd, k_fwd)
    s_bwd = jnp.einsum('bhncd,bhnsd->bhncs', q_bwd, k_bwd)
    lower = jnp.tril(jnp.ones((CHUNK, CHUNK), dtype=bool))
    scores = jnp.where(lower, s_fwd, s_bwd)
    o_intra = jnp.einsum('bhncs,bhnse->bhnce', scores, vh)

    delta = jnp.einsum('bhncd,bhnce->bhnde', kh * jnp.exp(b_last - bcum), vh)
    decay = jnp.exp(b_last[..., 0, :])

    def step(state, inp):
        dec, dlt = inp
        return dec[..., None] * state + dlt, state

    init = jnp.zeros((B, GLA_HEADS, GLA_DK, GLA_DV), f32)
    _, s_prev = lax.scan(step, init, (jnp.moveaxis(decay, 2, 0), jnp.moveaxis(delta, 2, 0)))
    s_prev = jnp.moveaxis(s_prev, 0, 2)
    o_inter = jnp.einsum('bhncd,bhnde->bhnce', q_fwd, s_prev)

    o = rms_norm(o_intra + o_inter, norm_g)
    o = o.transpose(0, 2, 3, 1, 4).reshape(B, S, GLA_HEADS * GLA_DV).astype(dt)
    return o * jax.nn.silu(g)


def diff_attention(q, k, v, q_norm_g, k_norm_g, lam_q1, lam_k1, lam_q2, lam_k2,
                   subln_g, lambda_init):
    B, S, _ = q.shape
    f32 = jnp.float32
    qh = rms_norm(q.reshape(B, S, DIFF_HEADS, 2, DIFF_DH), q_norm_g)
    kh = rms_norm(k.reshape(B, S, DIFF_HEADS, 2, DIFF_DH), k_norm_g)
    vh = v.reshape(B, S, DIFF_HEADS, DIFF_DV)
    lam = (jnp.exp(jnp.sum(lam_q1 * lam_k1).astype(f32))
           - jnp.exp(jnp.sum(lam_q2 * lam_k2).astype(f32)) + lambda_init)
    slopes = 2.0 ** (-8.0 * (jnp.arange(DIFF_HEADS, dtype=f32) + 1.0) / DIFF_HEADS)
    kpos = jnp.arange(S)
    kchunk = kpos // CHUNK
    nb = S // Q_BLOCK
    qb = jnp.moveaxis(qh.reshape(B, nb, Q_BLOCK, DIFF_HEADS, 2, DIFF_DH), 1, 0)
    scale = DIFF_DH ** -0.5

    def block(args):
        qblk, i = args
        qpos = i * Q_BLOCK + jnp.arange(Q_BLOCK)
        logits = jnp.einsum('bqhjd,bshjd->bhjqs', qblk, kh).astype(f32) * scale
        dist = jnp.abs(qpos[:, None] - kpos[None, :]).astype(f32)
        logits = logits - (slopes[:, None, None] * dist)[None, :, None]
        allowed = kchunk[None, :] <= (qpos // CHUNK)[:, None]
        logits = jnp.where(allowed, logits, -jnp.inf)
        p = jax.nn.softmax(logits, axis=-1)
        attn = p[:, :, 0] - lam * p[:, :, 1]
        return jnp.einsum('bhqs,bshe->bqhe', attn.astype(vh.dtype), vh)

    o = lax.map(block, (qb, jnp.arange(nb)))
    o = jnp.moveaxis(o, 0, 1).reshape(B, S, DIFF_HEADS, DIFF_DV)
    o = rms_norm(o, subln_g) * (1.0 - lambda_init)
    return o.reshape(B, S, DIFF_HEADS * DIFF_DV)


def conv_ffn(h, w_up, conv_w, conv_b, w_down):
    S = h.shape[1]
    u = h @ w_up
    up = jnp.pad(u, ((0, 0), (CONV_W - 1, 0), (0, 0)))
    u = sum(conv_w[j] * up[:, j:j + S] for j in range(CONV_W)) + conv_b
    a, b = jnp.split(u, 2, axis=-1)
    return (jax.nn.silu(a) * b) @ w_down


def setup_inputs(seed: int = 0) -> dict:
    key = jax.random.key(seed)
    ks = jax.random.split(key, 32)
    L, D, F = DEPTH, D_MODEL, D_FF
    nrm = lambda k, shape, fan: jax.random.normal(k, shape, jnp.float32) * (fan ** -0.5)
    gain = lambda k, shape: 1.0 + 0.05 * jax.random.normal(k, shape, jnp.float32)
    return {
        "x": jax.random.normal(ks[0], (BATCH, SEQ, D), jnp.float32),
        "c": jax.random.normal(ks[1], (BATCH, D), jnp.float32),
        "w_ada": nrm(ks[2], (L, D, 6 * D), D),
        "b_ada": 0.02 * jax.random.normal(ks[3], (L, 6 * D), jnp.float32),
        "norm1_g": gain(ks[4], (L, D)),
        "w_in": nrm(ks[5], (L, D, IN_DIM), D),
        "w_alpha_up": nrm(ks[6], (L, GLA_RANK, GLA_HEADS * GLA_DK), GLA_RANK),
        "b_alpha": 0.5 + 0.1 * jax.random.normal(ks[7], (L, GLA_HEADS * GLA_DK), jnp.float32),
        "gla_norm_g": gain(ks[8], (L, GLA_DV)),
        "q_norm_g": gain(ks[9], (L, DIFF_DH)),
        "k_norm_g": gain(ks[10], (L, DIFF_DH)),
        "lam_q1": 0.1 * jax.random.normal(ks[11], (L, DIFF_DH), jnp.float32),
        "lam_k1": 0.1 * jax.random.normal(ks[12], (L, DIFF_DH), jnp.float32),
        "lam_q2": 0.1 * jax.random.normal(ks[13], (L, DIFF_DH), jnp.float32),
        "lam_k2": 0.1 * jax.random.normal(ks[14], (L, DIFF_DH), jnp.float32),
        "diff_norm_g": gain(ks[15], (L, DIFF_DV)),
        "w_gla_o": nrm(ks[16], (L, GLA_HEADS * GLA_DV, D), GLA_HEADS * GLA_DV),
        "w_diff_o": nrm(ks[17], (L, DIFF_HEADS * DIFF_DV, D), DIFF_HEADS * DIFF_DV),
        "w_out": nrm(ks[18], (L, D, D), D),
        "norm2_g": gain(ks[19], (L, D)),
        "w_up": nrm(ks[20], (L, D, 2 * F), D),
        "conv_w": nrm(ks[21], (L, CONV_W, 2 * F), CONV_W),
        "conv_b": 0.02 * jax.random.normal(ks[22], (L, 2 * F), jnp.float32),
        "w_down": nrm(ks[23], (L, F, D), F),
    }


def reference(x, c, w_ada, b_ada, norm1_g, w_in, w_alpha_up, b_alpha, gla_norm_g,
              q_norm_g, k_norm_g, lam_q1, lam_k1, lam_q2, lam_k2, diff_norm_g,
              w_gla_o, w_diff_o, w_out, norm2_g, w_up, conv_w, conv_b, w_down):
    split_idx = [int(v) for v in np.cumsum(IN_SIZES)[:-1]]
    for layer in range(DEPTH):
        lambda_init = 0.8 - 0.6 * math.exp(-0.3 * layer)
        mod = jax.nn.silu(c) @ w_ada[layer] + b_ada[layer]
        sh1, sc1, gt1, sh2, sc2, gt2 = [m[:, None, :] for m in jnp.split(mod, 6, axis=-1)]

        h = rms_norm(x, norm1_g[layer]) * (1.0 + sc1) + sh1
        proj = h @ w_in[layer]
        (g_q, g_k, g_v, g_g, g_a, d_q, d_k, d_v,
         gate_a, gate_b) = jnp.split(proj, split_idx, axis=-1)
        o_a = gla_mixer(g_q, g_k, g_v, g_g, g_a, w_alpha_up[layer], b_alpha[layer],
                        gla_norm_g[layer])
        o_b = diff_attention(d_q, d_k, d_v, q_norm_g[layer], k_norm_g[layer],
                             lam_q1[layer], lam_k1[layer], lam_q2[layer], lam_k2[layer],
                             diff_norm_g[layer], lambda_init)
        merged = (jax.nn.sigmoid(gate_a) * (o_a @ w_gla_o[layer])
                  + jax.nn.sigmoid(gate_b) * (o_b @ w_diff_o[layer]))
        x = x + gt1 * (merged @ w_out[layer])

        h2 = rms_norm(x, norm2_g[layer]) * (1.0 + sc2) + sh2
        x = x + gt2 * conv_ffn(h2, w_up[layer], conv_w[layer], conv_b[layer], w_down[layer])
    return x
```

```python
import contextlib
import numpy as np
import concourse.bass as bass
import concourse.mybir as mybir
from concourse.bass_utils import run_bass_kernel_spmd

F32 = mybir.dt.float32
BF16 = mybir.dt.bfloat16
AF = mybir.ActivationFunctionType
ALU = mybir.AluOpType

D = 1024
S = 4096
NT = S // 128
EPS = 1e-6
F = 2816
NFC = F // 128
IN_DIM = 8208
OFF_GQ, OFF_GK, OFF_GV, OFF_GG, OFF_GA = 0, 512, 1024, 2048, 3072
OFF_DQ, OFF_DK, OFF_DV, OFF_GATEA, OFF_GATEB = 3088, 4112, 5136, 6160, 7184
LAMBDA_INIT = 0.8 - 0.6 * 1.0
NEG = -30000.0
NO_MERGE = False


class T:
    __slots__ = ("name", "w", "r", "wcnt", "rcnt")

    def __init__(self, name):
        self.name = name
        self.w = None
        self.r = {}
        self.wcnt = 0
        self.rcnt = 0


class Prog:
    def __init__(self, nc):
        self.nc = nc
        self.ops = []
        self.engs = {"pe": nc.tensor, "act": nc.scalar, "dve": nc.vector,
                     "pool": nc.gpsimd, "sp": nc.sync}

    def op(self, eng, fn, R=(), W=(), dma=None, owner=None):
        self.ops.append((eng, fn, tuple(R), tuple(W), dma, owner))

    def barrier(self):
        self.ops.append(("barrier", None, (), (), None, None))

    def capture(self, fn):
        saved, self.ops = self.ops, []
        fn()
        out, self.ops = self.ops, saved
        return out

    def emit_merged(self, main, side):
        if not side or NO_MERGE:
            self.ops.extend(main)
            self.ops.extend(side)
            return
        nm, ns = len(main), len(side)
        j = 0
        for i, o in enumerate(main):
            self.ops.append(o)
            tgt = (i + 1) * ns // nm
            while j < tgt:
                self.ops.append(side[j])
                j += 1
        self.ops.extend(side[j:])

    def mm(self, out, lhsT, rhs, start, stop, R, W, **kw):
        nc = self.nc
        self.op("pe", lambda: nc.tensor.matmul(out, lhsT=lhsT, rhs=rhs, start=start,
                                               stop=stop, **kw), R, W)

    def tr(self, out, in_, ident, R, W):
        nc = self.nc
        self.op("pe", lambda: nc.tensor.transpose(out, in_, ident), R, W)

    def act(self, out, in_, func, R, W, bias=None, scale=None, accum_out=None):
        nc = self.nc
        kw = {}
        if bias is not None:
            kw["bias"] = bias
        if scale is not None:
            kw["scale"] = scale
        if accum_out is not None:
            kw["accum_out"] = accum_out
        self.op("act", lambda: nc.scalar.activation(out=out, in_=in_, func=func, **kw), R, W)

    def ts(self, eng, out, in0, s1, s2, op0, op1, R, W):
        e = self.engs[eng]
        if op1 is None:
            self.op(eng, lambda: e.tensor_scalar(out=out, in0=in0, scalar1=s1, scalar2=None,
                                                 op0=op0), R, W)
        else:
            self.op(eng, lambda: e.tensor_scalar(out=out, in0=in0, scalar1=s1, scalar2=s2,
                                                 op0=op0, op1=op1), R, W)

    def stt(self, out, in0, scalar, in1, op0, op1, R, W):
        nc = self.nc
        self.op("dve", lambda: nc.vector.scalar_tensor_tensor(out=out, in0=in0, scalar=scalar,
                                                              in1=in1, op0=op0, op1=op1), R, W)

    def tt(self, eng, out, in0, in1, op, R, W):
        e = self.engs[eng]
        self.op(eng, lambda: e.tensor_tensor(out=out, in0=in0, in1=in1, op=op), R, W)

    def copy(self, eng, out, in_, R, W):
        if eng == "act":
            nc = self.nc
            self.op("act", lambda: nc.scalar.copy(out=out, in_=in_), R, W)
        else:
            e = self.engs[eng]
            self.op(eng, lambda: e.tensor_copy(out=out, in_=in_), R, W)

    def recip(self, out, in_, R, W):
        nc = self.nc
        self.op("dve", lambda: nc.vector.reciprocal(out=out, in_=in_), R, W)

    def memset(self, eng, ap, val, W):
        e = self.engs[eng]
        self.op(eng, lambda: e.memset(ap, val), (), W)

    def load(self, eng, out, in_, W, owner=None, **kw):
        e = self.engs[eng]
        self.op(eng, lambda: e.dma_start(out=out, in_=in_, **kw), (), W, dma="w",
                owner=owner if owner is not None else W[0])

    def store(self, eng, out, in_, R, owner=None):
        e = self.engs[eng]
        self.op(eng, lambda: e.dma_start(out=out, in_=in_), R, (), dma="r",
                owner=owner if owner is not None else R[0])

    def finalize(self, es):
        nc = self.nc
        ops = self.ops
        n = len(ops)
        deps_ops = [None] * n
        deps_dma = [None] * n
        signal = [False] * n
        dma_ev = [None] * n
        last_op = {}
        dma_latest = {}
        pending_bar = {}
        semkeys = {}
        for i, (eng, fn, R, W, dma, owner) in enumerate(ops):
            d_ops = {}
            d_dma = {}
            if eng == "barrier":
                for e, idx in last_op.items():
                    if e != "sp":
                        d_ops[e] = idx
                d_dma = dict(dma_latest)
                deps_ops[i], deps_dma[i] = d_ops, d_dma
                ev = ("op", "sp", i)
                last_op["sp"] = i
                for e in self.engs:
                    if e != "sp":
                        pending_bar[e] = ev
                continue

            mykey = (id(owner), dma) if dma is not None else None

            def add(ev, eng=eng, d_ops=d_ops, d_dma=d_dma, mykey=mykey):
                if ev is None:
                    return
                if ev[0] == "dma" and ev[1] == mykey:
                    return
                if ev[0] == "op":
                    e, idx = ev[1], ev[2]
                    if e == "pe" and eng == "pe":
                        return
                    if d_ops.get(e, -1) < idx:
                        d_ops[e] = idx
                else:
                    k, v = ev[1], ev[2]
                    if d_dma.get(k, 0) < v:
                        d_dma[k] = v

            if eng in pending_bar:
                add(pending_bar.pop(eng))
            for t in R:
                add(t.w)
            for t in W:
                add(t.w)
                for ev in t.r.values():
                    add(ev)
            deps_ops[i], deps_dma[i] = d_ops, d_dma
            if dma is None:
                ev = ("op", eng, i)
                rkey = eng
            else:
                key = (id(owner), dma)
                if key not in semkeys:
                    semkeys[key] = None
                if dma == "w":
                    owner.wcnt += 1
                    val = 16 * owner.wcnt
                else:
                    owner.rcnt += 1
                    val = 16 * owner.rcnt
                ev = ("dma", key, val)
                dma_ev[i] = (key, val)
                dma_latest[key] = val
                rkey = key
            if dma is None:
                last_op[eng] = i
            for t in R:
                t.r[rkey] = ev
            for t in W:
                t.w = ev
                t.r = {}
        for i in range(n):
            for e, idx in deps_ops[i].items():
                signal[idx] = True
        esem = {e: es.enter_context(nc.semaphore("c_" + e)) for e in self.engs}
        for j, key in enumerate(semkeys):
            semkeys[key] = es.enter_context(nc.semaphore("d%d" % j))
        self.n_sems = len(esem) + len(semkeys)
        rank = [0] * n
        cnt = {e: 0 for e in self.engs}
        for i, o in enumerate(ops):
            eng = "sp" if o[0] == "barrier" else o[0]
            if o[4] is not None:
                signal[i] = False
            elif signal[i] or o[0] == "barrier":
                cnt[eng] += 1
                signal[i] = True
            rank[i] = cnt[eng]
        known = {e: {} for e in self.engs}
        for i, (eng, fn, R, W, dma, owner) in enumerate(ops):
            real = "sp" if eng == "barrier" else eng
            e = self.engs[real]
            kn = known[real]
            for pe_, idx in deps_ops[i].items():
                v = rank[idx]
                if kn.get(pe_, 0) < v:
                    e.wait_ge(esem[pe_], v)
                    kn[pe_] = v
            for key, v in deps_dma[i].items():
                if kn.get(key, 0) < v:
                    e.wait_ge(semkeys[key], v)
                    kn[key] = v
            if eng == "barrier":
                ins = nc.sync.nop()
                ins.then_inc(esem["sp"], 1)
                continue
            ins = fn()
            if dma is not None:
                ins.then_inc(semkeys[dma_ev[i][0]], 16)
            elif signal[i]:
                ins.then_inc(esem[real], 1)
        self.max_rank = dict(cnt)


def _host_consts():
    p = np.arange(128)
    same = (p[:, None] // 64) == (p[None, :] // 64)
    ident = np.eye(128, dtype=np.float32)
    tri = (same & (p[:, None] <= p[None, :])).astype(np.float32)
    upp = (same & (p[:, None] > p[None, :])).astype(np.float32)
    cf32 = np.concatenate([ident, tri, upp], axis=1)
    blockones = same.astype(np.float32)
    cm = []
    for h in range(8):
        slope = 2.0 ** (-(h + 1))
        s = p[:, None]
        q = p[None, :]
        c = np.zeros((128, 128), np.float32)
        c = np.where(same & (s > q), -2.0 * slope * (s - q), c)
        c = np.where((s // 64) > (q // 64), NEG, c)
        cm.append(c.astype(np.float32))
    cbf = np.concatenate([ident, blockones] + cm, axis=1)
    t = np.arange(S)
    aug = np.zeros((8, 2, 4, S), np.float32)
    for h in range(8):
        slope = 2.0 ** (-(h + 1))
        aug[h, 0, 0] = -slope * 64.0 * (t // 64)
        aug[h, 0, 1] = -slope * (t % 64)
        aug[h, 0, 2] = 1.0
        aug[h, 0, 3] = 1.0
        aug[h, 1, 0] = 1.0
        aug[h, 1, 1] = 1.0
        aug[h, 1, 2] = slope * 64.0 * (t // 64)
        aug[h, 1, 3] = slope * (t % 64)
    return cf32, cbf, aug


def _col(v):
    v = np.asarray(v, np.float32)
    return np.ascontiguousarray(v.reshape(-1, 128).T)


NCOL = 48 + 8 + 8 + 1 + 1 + 132 + 44 + 2 + 1
C_BADA, C_G1, C_G2, C_QG, C_KG, C_CW, C_CB, C_GLAG, C_DG = 0, 48, 56, 64, 65, 66, 198, 242, 244


def _host_inputs(b, inp, consts):
    cf32, cbf, aug = consts
    cols = np.zeros((128, NCOL), np.float32)
    cols[:, C_BADA:C_BADA + 48] = _col(inp["b_ada"][0])
    cols[:, C_G1:C_G1 + 8] = _col(inp["norm1_g"][0])
    cols[:, C_G2:C_G2 + 8] = _col(inp["norm2_g"][0])
    cols[:, C_QG] = np.tile(inp["q_norm_g"][0], 2)
    cols[:, C_KG] = np.tile(inp["k_norm_g"][0], 2)
    cw = inp["conv_w"][0]
    for j in range(3):
        cols[:, C_CW + 44 * j:C_CW + 44 * (j + 1)] = _col(cw[j])
    cols[:, C_CB:C_CB + 44] = _col(inp["conv_b"][0])
    cols[:, C_GLAG:C_GLAG + 2] = _col(inp["gla_norm_g"][0])
    cols[:, C_DG] = inp["diff_norm_g"][0]
    lam = np.stack([inp["lam_q1"][0], inp["lam_k1"][0], inp["lam_q2"][0], inp["lam_k2"][0]])
    walpha = np.zeros((33, 512), np.float32)
    walpha[0:16] = inp["w_alpha_up"][0]
    walpha[32] = inp["b_alpha"][0]
    return {
        "x": np.ascontiguousarray(inp["x"][b]),
        "c_col": _col(inp["c"][b]),
        "cols": cols,
        "lam": np.ascontiguousarray(lam.reshape(1, 256)),
        "walpha": walpha,
        "w_ada": np.ascontiguousarray(inp["w_ada"][0]),
        "w_in": np.ascontiguousarray(inp["w_in"][0]),
        "w_gla_o": np.ascontiguousarray(inp["w_gla_o"][0]),
        "w_diff_o": np.ascontiguousarray(inp["w_diff_o"][0]),
        "w_out": np.ascontiguousarray(inp["w_out"][0]),
        "w_up": np.ascontiguousarray(inp["w_up"][0]),
        "w_down": np.ascontiguousarray(inp["w_down"][0]),
        "cf32": cf32, "cbf": cbf, "aug": aug,
    }


def build(stage="full"):
    nc = bass.Bass("TRN2", target_bir_lowering=False)
    P = Prog(nc)
    dram = {}

    def din(name, shape):
        dram[name] = nc.dram_tensor(name, list(shape), F32, kind="ExternalInput").ap()
        return dram[name]

    x_d = din("x", [S, D])
    ccol_d = din("c_col", [128, 8])
    cols_d = din("cols", [128, NCOL])
    lam_d = din("lam", [1, 256])
    walpha_d = din("walpha", [33, 512])
    wada_d = din("w_ada", [D, 6 * D])
    win_d = din("w_in", [D, IN_DIM])
    wglao_d = din("w_gla_o", [D, D])
    wdiffo_d = din("w_diff_o", [D, D])
    wout_d = din("w_out", [D, D])
    wup_d = din("w_up", [D, 2 * F])
    wdown_d = din("w_down", [F, D])
    cf32_d = din("cf32", [128, 384])
    cbf_d = din("cbf", [128, 1280])
    aug_d = din("aug", [8, 2, 4, S])
    out_d = nc.dram_tensor("out", [S, D], F32, kind="ExternalOutput").ap()
    dbg = {}

    es = contextlib.ExitStack()

    def sb(name, shape, dt=F32, stack=None):
        return (stack or es).enter_context(nc.sbuf_tensor("s_" + name, list(shape), dt))

    def ps(name, shape, dt=F32, stack=None):
        return (stack or es).enter_context(nc.psum_tensor("p_" + name, list(shape), dt))

    cf32 = sb("cf32", [128, 384])
    cbf = sb("cbf", [128, 1280], BF16)
    cols = sb("cols", [128, NCOL])
    ccol = sb("ccol", [128, 8])
    lam_sb = sb("lam_sb", [1, 256])
    t_const = T("const")
    P.load("sp", cf32[:], cf32_d[:, :], [t_const])
    P.load("pool", cbf[:], cbf_d[:, :], [t_const], owner=t_const)
    P.load("sp", cols[:], cols_d[:, :], [t_const], owner=t_const)
    P.load("sp", ccol[:], ccol_d[:, :], [t_const], owner=t_const)
    P.load("sp", lam_sb[:], lam_d[:, :], [t_const], owner=t_const)
    ident = cf32[:, 0:128]
    tri = cf32[:, 128:256]
    upp = cf32[:, 256:384]
    ident_bf = cbf[:, 0:128]
    blockones = cbf[:, 128:256]

    if stage == "c0":
        dbg["cols"] = (cols, [128, NCOL], [t_const])
        pass
        return _finish(nc, P, es, dbg, out_d)
    modc = sb("modc", [128, 48])
    a1c = sb("a1c", [128, 8])
    a2c = sb("a2c", [128, 8])
    t_mod = T("mod")

    with contextlib.ExitStack() as s0:
        silc = sb("silc", [128, 8], stack=s0)
        t_silc = T("silc")
        P.act(silc[:], ccol[:], AF.Silu, [t_const], [t_silc])
        if stage == "c1":
            dbg["silc"] = (silc, [128, 8], [t_silc])
            return _finish(nc, P, es, dbg, out_d)
        wa = [sb("wa%d" % i, [128, 8, 512], stack=s0) for i in range(2)]
        t_wa = [T("wa0"), T("wa1")]
        pm = ps("pm", [128, 48], stack=s0)
        t_pm = T("pm")
        wada_v = wada_d.rearrange("(k p) n -> p k n", p=128)
        for nb in range(12):
            bi = nb % 2
            P.load("sp", wa[bi][:], wada_v[:, :, nb * 512:(nb + 1) * 512], [t_wa[bi]])
            for j in range(4):
                nchunk = nb * 4 + j
                for kc in range(8):
                    P.mm(pm[:, nchunk:nchunk + 1], wa[bi][:, kc, j * 128:(j + 1) * 128],
                         silc[:, kc:kc + 1], kc == 0, kc == 7, [t_wa[bi], t_silc], [t_pm])
        P.tt("dve", modc[:], pm[:], cols[:, C_BADA:C_BADA + 48], ALU.add, [t_pm, t_const], [t_mod])
        if stage == "c2":
            dbg["modc"] = (modc, [128, 48], [t_mod])
            return _finish(nc, P, es, dbg, out_d)
        P.stt(a1c[:], modc[:, 8:16], 1.0, cols[:, C_G1:C_G1 + 8], ALU.add, ALU.mult,
              [t_mod, t_const], [t_mod])
        P.stt(a2c[:], modc[:, 32:40], 1.0, cols[:, C_G2:C_G2 + 8], ALU.add, ALU.mult,
              [t_mod, t_const], [t_mod])
    if stage == "mod":
        dbg["modc"] = (modc, [128, 48], [t_mod])
        dbg["a1c"] = (a1c, [128, 8], [t_mod])
        return _finish(nc, P, es, dbg, out_d)

    nlam = sb("nlam", [128, 1])
    ones_row = sb("ones_row", [1, 128])
    qgs = sb("qgs", [128, 2])
    dgs = sb("dgs", [128, 1])
    mb_d = nc.dram_tensor("mb_scratch", [128, 8, S], BF16, kind="Internal").ap()
    s10 = contextlib.ExitStack()
    hT = sb("hT", [128, 8, S], BF16, stack=s10)
    t_hT = [T("hT%d" % i) for i in range(NT)]

    def norm_a(src_ap_dram, i, bufs, preloaded=False):
        xt, t_xt, ss, t_ss, xs, t_xs, ptr, t_ptr = bufs
        bi = i % len(xt)
        if not preloaded:
            P.load("sp", xt[bi][:], src_ap_dram[i * 128:(i + 1) * 128, :], [t_xt[bi]])
        b2 = i % 2
        P.act(xs[b2][:], xt[bi][:], AF.Square, [t_xt[bi]], [t_xs[b2], t_ss[b2]],
              accum_out=ss[b2][:, 0:1])
        P.act(ss[b2][:, 1:2], ss[b2][:, 0:1], AF.Ln, [t_ss[b2]], [t_ss[b2]],
              bias=EPS, scale=1.0 / D)
        P.act(ss[b2][:, 2:3], ss[b2][:, 1:2], AF.Exp, [t_ss[b2]], [t_ss[b2]], scale=-0.5)
        P.ts("dve", xs[b2][:], xt[bi][:], ss[b2][:, 2:3], None, ALU.mult, None,
             [t_xt[bi], t_ss[b2]], [t_xs[b2]])
        return bi

    def norm_b(i, dst, t_dst, ac, shc_off, bufs, out_is_tile=False, col=None):
        xt, t_xt, ss, t_ss, xs, t_xs, ptr, t_ptr = bufs
        b2 = i % 2
        for half in range(2):
            pt = ptr[half]
            for j in range(4):
                kc = half * 4 + j
                P.tr(pt[:, j * 128:(j + 1) * 128], xs[b2][:, kc * 128:(kc + 1) * 128], ident,
                     [t_xs[b2], t_const], [t_ptr[half]])
            for j in range(4):
                kc = half * 4 + j
                if out_is_tile:
                    o = dst[:, kc, :]
                elif col is not None:
                    o = dst[:, kc, col * 128:(col + 1) * 128]
                else:
                    o = dst[:, kc, i * 128:(i + 1) * 128]
                P.ts("dve", o, pt[:, j * 128:(j + 1) * 128], ac[:, kc:kc + 1],
                     modc[:, shc_off + kc:shc_off + kc + 1], ALU.mult, ALU.add,
                     [t_ptr[half], t_mod], [t_dst])

    def norm_tile(src_ap_dram, i, dst, t_dst, ac, shc_off, stk, bufs, out_is_tile=False, col=None,
                  preloaded=False):
        bi = norm_a(src_ap_dram, i, bufs, preloaded)
        norm_b(i, dst, t_dst, ac, shc_off, bufs, out_is_tile, col)
        return bi

    with contextlib.ExitStack() as s1:
        xt = [sb("xt%d" % i, [128, D], stack=s1) for i in range(3)]
        t_xt = [T("xt%d" % i) for i in range(3)]
        ss = [sb("ss%d" % i, [128, 4], stack=s1) for i in range(2)]
        t_ss = [T("ss0"), T("ss1")]
        xs = [sb("xs%d" % i, [128, D], stack=s1) for i in range(2)]
        t_xs = [T("xs0"), T("xs1")]
        ptr = [ps("ptr%d" % i, [128, 512], stack=s1) for i in range(2)]
        t_ptr = [T("ptr0"), T("ptr1")]
        bufs = (xt, t_xt, ss, t_ss, xs, t_xs, ptr, t_ptr)
        for i in range(NT):
            norm_tile(x_d, i, hT, t_hT[i], a1c, 0, s1, bufs)
        P.barrier()
    if stage == "hT":
        dbg["hT"] = (hT, [128, 8, S], t_hT)
        return _finish(nc, P, es, dbg, out_d)

    bank = [ps("bank%d" % i, [128, 512]) for i in range(8)]
    t_bank = [T("bank%d" % i) for i in range(8)]

    t_nlam = T("nlam")
    with contextlib.ExitStack() as sl:
        ltmp = sb("ltmp", [1, 136], stack=sl)
        t_l = T("ltmp")
        P.memset("dve", ones_row[:], 1.0, [t_l])
        P.tt("dve", ltmp[:, 0:64], lam_sb[:, 0:64], lam_sb[:, 64:128], ALU.mult, [t_const], [t_l])
        P.tt("dve", ltmp[:, 64:128], lam_sb[:, 128:192], lam_sb[:, 192:256], ALU.mult, [t_const], [t_l])
        for j in range(2):
            P.op("dve", (lambda j=j: nc.vector.reduce_sum(out=ltmp[:, 128 + j:129 + j],
                                                          in_=ltmp[:, 64 * j:64 * j + 64],
                                                          axis=mybir.AxisListType.X)), [t_l], [t_l])
        P.act(ltmp[:, 130:132], ltmp[:, 128:130], AF.Exp, [t_l], [t_l])
        P.tt("dve", ltmp[:, 132:133], ltmp[:, 130:131], ltmp[:, 131:132], ALU.subtract, [t_l], [t_l])
        P.ts("dve", ltmp[:, 133:134], ltmp[:, 132:133], LAMBDA_INIT, -1.0, ALU.add, ALU.mult, [t_l], [t_l])
        P.mm(bank[0][:, 0:1], ones_row[:, :], ltmp[:, 133:134], True, True, [t_l], [t_bank[0]])
        P.copy("dve", nlam[:], bank[0][:, 0:1], [t_bank[0]], [t_nlam])
        P.ts("dve", qgs[:, 0:1], cols[:, C_QG:C_QG + 1], 0.125, None, ALU.mult, None, [t_const], [t_nlam])
        P.copy("dve", qgs[:, 1:2], cols[:, C_KG:C_KG + 1], [t_const], [t_nlam])
        P.ts("dve", dgs[:], cols[:, C_DG:C_DG + 1], 1.0 - LAMBDA_INIT, None, ALU.mult, None,
             [t_const], [t_nlam])
        P.barrier()

    ob_d = nc.dram_tensor("ob_scratch", [128, 8, S], BF16, kind="Internal").ap()
    wg_s = nc.dram_tensor("wg_s", [D, 3088], BF16, kind="Internal").ap()
    wga_s = nc.dram_tensor("wga_s", [D, D], BF16, kind="Internal").ap()
    wgo_s = nc.dram_tensor("wgo_s", [D, D], BF16, kind="Internal").ap()
    wo_s = nc.dram_tensor("wo_s", [D, D], BF16, kind="Internal").ap()
    wup_s = nc.dram_tensor("wup_s", [D, 2 * F], BF16, kind="Internal").ap()
    wdn_s = nc.dram_tensor("wdn_s", [F, D], BF16, kind="Internal").ap()
    wgb_s = nc.dram_tensor("wgb_s", [D, D], BF16, kind="Internal").ap()
    wdo_s = nc.dram_tensor("wdo_s", [D, D], BF16, kind="Internal").ap()
    t_pre = T("precast")
    precast = [
        (wgb_s[:, :], win_d[:, OFF_GATEB:OFF_GATEB + D]), (wdo_s[:, :], wdiffo_d[:, :]),
        (wg_s[:, 0:1544], win_d[:, 0:1544]), (wg_s[:, 1544:3088], win_d[:, 1544:3088]),
        (wga_s[:, :], win_d[:, OFF_GATEA:OFF_GATEA + D]), (wgo_s[:, :], wglao_d[:, :]),
        (wo_s[:, :], wout_d[:, :]),
    ] + [(wup_s[:, q4 * 1408:(q4 + 1) * 1408], wup_d[:, q4 * 1408:(q4 + 1) * 1408]) for q4 in range(4)] + [
        (wdn_s[0:1024, :], wdown_d[0:1024, :]), (wdn_s[1024:2048, :], wdown_d[1024:2048, :]),
        (wdn_s[2048:2816, :], wdown_d[2048:2816, :]),
    ]

    def emit_precast(k):
        for o_, i_ in precast[:k]:
            P.op("pool", (lambda o_=o_, i_=i_: nc.gpsimd.dma_start(out=o_, in_=i_)), (), [t_pre],
                 dma="w", owner=t_pre)
        del precast[:k]
    win_v = win_d.rearrange("(k p) n -> p k n", p=128)
    with contextlib.ExitStack() as sa:
        QKb = [sb("QK%d" % i, [128, 4, S], BF16, stack=sa) for i in range(2)]
        t_q = [[T("q%d_%d" % (b_, i)) for i in range(8)] for b_ in range(2)]
        t_k = [[T("k%d_%d" % (b_, i)) for i in range(8)] for b_ in range(2)]
        t_aug = T("aug")
        Vb = [sb("Vaug%d" % i, [128, NT, 130], BF16, stack=sa) for i in range(2)]
        t_v = [[T("v%d_%d" % (b_, i)) for i in range(8)] for b_ in range(2)]
        wqkv = [sb("wqkv%d" % i, [128, 3, 8, 128], BF16, stack=sa) for i in range(2)]
        t_w = [T("wqkv0"), T("wqkv1")]
        NPT = 4
        PT = [sb("PT%d" % i, [128, 512], BF16, stack=sa) for i in range(NPT)]
        t_PT = [T("PT%d" % i) for i in range(NPT)]
        sq = [sb("sq%d" % i, [128, 512], BF16, stack=sa) for i in range(2)]
        t_sq = [T("sq0"), T("sq1")]
        rs = [sb("rs%d" % i, [128, 512], stack=sa) for i in range(2)]
        t_rs = [T("rs0"), T("rs1")]
        qraw = [sb("qraw%d" % i, [128, 512], stack=sa) for i in range(2)]
        t_qraw = [T("qraw0"), T("qraw1")]
        OTs = [[sb("OTs%d_%d" % (a, m_), [128, 512], stack=sa) for m_ in range(2)] for a in range(2)]
        SMs = [[sb("SMs%d_%d" % (a, m_), [128, 512], stack=sa) for m_ in range(2)] for a in range(2)]
        t_OTs = [[T("OTs%d_%d" % (a, m_)) for m_ in range(2)] for a in range(2)]
        t_SMs = [[T("SMs%d_%d" % (a, m_)) for m_ in range(2)] for a in range(2)]
        sqb = [sb("sqb%d" % i, [128, 512], BF16, stack=sa) for i in range(2)]
        t_sqb = [T("sqb0"), T("sqb1")]
        onb = [sb("onb%d" % i, [128, 512], BF16, stack=sa) for i in range(2)]
        t_onb = [T("onb0"), T("onb1")]
        rstd = sb("rstd", [128, 512], stack=sa)
        t_rstd = T("rstd")
        rscr = sb("rscr", [128, 512], stack=sa)
        t_rscr = T("rscr")
        ones_bf = sb("ones_bf", [128, 128], BF16, stack=sa)
        ones_f32 = sb("ones_f32", [128, 128], stack=sa)
        t_onesbf = T("ones_bf")
        P.memset("pool", ones_bf[:], 1.0, [t_onesbf])
        P.memset("pool", ones_f32[:], 1.0, [t_onesbf])
        Sacc = [sb("Sacc%d" % i, [128, 512], stack=sa) for i in range(2)]
        t_Sacc = [T("Sacc0"), T("Sacc1")]
        for b_ in range(2):
            P.memset("pool", QKb[b_][64:128, 0, :], 0.0, t_q[b_])
            P.memset("pool", QKb[b_][0:64, 1, :], 0.0, t_q[b_])
            P.memset("pool", QKb[b_][64:128, 2, :], 0.0, t_k[b_])
            P.memset("pool", QKb[b_][0:64, 3, :], 0.0, t_k[b_])
            P.memset("pool", Vb[b_][:, :, 128:130], 1.0, t_v[b_])

        def load_w(hd):
            hp = hd % 2
            wb, tw = wqkv[hp], t_w[hp]
            for j, off in enumerate((OFF_DQ, OFF_DK, OFF_DV)):
                P.load("pool", wb[:, j, :, :], win_v[:, :, off + hd * 128: off + (hd + 1) * 128],
                       [tw], owner=tw)

        def load_aug(hd):
            hp = hd % 2
            QK = QKb[hp]
            P.load("pool", QK[64:68, 0, :], aug_d[hd, 0, :, :], t_q[hp], owner=t_aug)
            P.load("pool", QK[0:4, 1, :], aug_d[hd, 0, :, :], t_q[hp], owner=t_aug)
            P.load("pool", QK[64:68, 2, :], aug_d[hd, 1, :, :], t_k[hp], owner=t_aug)
            P.load("pool", QK[0:4, 3, :], aug_d[hd, 1, :, :], t_k[hp], owner=t_aug)

        def projection(hd):
            hp = hd % 2
            QK, Vaug = QKb[hp], Vb[hp]
            wb, tw = wqkv[hp], t_w[hp]
            pq, tpq = bank[7], t_bank[7]
            it = 0
            for which in range(2):
                for tb in range(8):
                    b2 = it % 2
                    it += 1
                    tsl = slice(tb * 512, (tb + 1) * 512)
                    for kc in range(8):
                        P.mm(pq[:, :], wb[:, which, kc, :], hT[:, kc, tsl], kc == 0, kc == 7,
                             [tw] + t_hT[tb * 4:tb * 4 + 4], [tpq])
                    P.copy("dve", qraw[b2][:], pq[:, :], [tpq], [t_qraw[b2]])
                    P.tt("dve", sq[b2][:], qraw[b2][:], qraw[b2][:], ALU.mult, [t_qraw[b2]], [t_sq[b2]])
                    P.mm(pq[:, :], blockones, sq[b2][:], True, True, [t_sq[b2], t_const], [tpq])
                    P.act(rs[b2][:], pq[:, :], AF.Ln, [tpq], [t_rs[b2]], bias=EPS, scale=1.0 / 64)
                    P.act(rs[b2][:], rs[b2][:], AF.Exp, [t_rs[b2]], [t_rs[b2]], scale=-0.5)
                    tdst = t_q[hp][tb] if which == 0 else t_k[hp][tb]
                    P.stt(QK[0:64, 2 * which, tsl], qraw[b2][0:64, :], qgs[0:64, which:which + 1],
                          rs[b2][0:64, :], ALU.mult, ALU.mult, [t_qraw[b2], t_rs[b2], t_nlam], [tdst])
                    P.stt(QK[64:128, 2 * which + 1, tsl], qraw[b2][64:128, :], qgs[64:128, which:which + 1],
                          rs[b2][64:128, :], ALU.mult, ALU.mult, [t_qraw[b2], t_rs[b2], t_nlam], [tdst])
            for g in range(8):
                for tt_ in range(4):
                    ti = g * 4 + tt_
                    for kc in range(8):
                        P.mm(pq[:, tt_ * 128:(tt_ + 1) * 128], hT[:, kc, ti * 128:(ti + 1) * 128],
                             wb[:, 2, kc, :], kc == 0, kc == 7, [tw, t_hT[ti]], [tpq])
                for tt_ in range(4):
                    ti = g * 4 + tt_
                    P.copy("dve", Vaug[:, ti, 0:128], pq[:, tt_ * 128:(tt_ + 1) * 128], [tpq], [t_v[hp][g]])

        def attention(hd):
            hp = hd % 2
            QK, Vaug = QKb[hp], Vb[hp]
            cm_h = cbf[:, 256 + hd * 128: 256 + (hd + 1) * 128]
            items = []
            for qb in range(8):
                for mp in range(2):
                    for kt in range(4 * qb + 4):
                        items.append(("qk", qb, mp, kt))
                if qb >= 1:
                    items.insert(len(items) - 6, ("ssq", qb - 1))
            items.append(("ssq", 7))
            items.insert(len(items) - 1, ("nop",))
            items.insert(len(items) - 1, ("nop",))
            nst = len(items)

            def c0_of(it):
                _, qb, mp, kt = it
                jj = kt - 4 * qb
                return 0 if jj < 0 else jj * 128

            def stage1(n):
                it = items[n]
                pl, tpl = bank[4 + n % 3], t_bank[4 + n % 3]
                if it[0] == "qk":
                    _, qb, mp, kt = it
                    jj = kt - 4 * qb
                    c0 = c0_of(it)
                    P.mm(pl[:, c0:512], QK[:, 2 + mp, kt * 128:(kt + 1) * 128],
                         QK[:, mp, qb * 512 + c0:(qb + 1) * 512], True, jj < 0,
                         [t_k[hp][kt // 4], t_q[hp][qb]], [tpl])
                    if jj >= 0:
                        P.mm(pl[:, c0:c0 + 128], ident_bf, cm_h, False, True, [t_const], [tpl])
                elif it[0] == "ssq":
                    qb = it[1]
                    P.mm(pl[:, :], ones_bf[:], sqb[qb % 2][:], True, True, [t_sqb[qb % 2], t_onesbf], [tpl])

            def stage2(n):
                it = items[n]
                pl, tpl = bank[4 + n % 3], t_bank[4 + n % 3]
                if it[0] == "qk":
                    c0 = c0_of(it)
                    P.act(PT[n % NPT][:, c0:512], pl[:, c0:512], AF.Exp, [tpl], [t_PT[n % NPT]])
                elif it[0] == "ssq":
                    qb = it[1]
                    P.act(rstd[:], pl[:, :], AF.Ln, [tpl], [t_rstd], bias=EPS, scale=1.0 / 128)
                    P.act(rstd[:], rstd[:], AF.Exp, [t_rstd], [t_rstd], scale=-0.5)

            def stage3(n):
                it = items[n]
                if it[0] == "qk":
                    _, qb, mp, kt = it
                    c0 = c0_of(it)
                    last = kt == 4 * qb + 3
                    bo, tbo = bank[2 * mp], t_bank[2 * mp]
                    bs, tbs = bank[2 * mp + 1], t_bank[2 * mp + 1]
                    P.mm(bo[:, c0:512], Vaug[:, kt, 0:128], PT[n % NPT][:, c0:512], kt == 0, last,
                         [t_PT[n % NPT], t_v[hp][kt // 4]], [tbo])
                    if kt == 0:
                        P.memset("pool", Sacc[mp][:], 0.0, [t_Sacc[mp]])
                    if kt % 2 == 0:
                        P.mm(bs[:, c0:512], ones_bf[:], PT[n % NPT][:, c0:512], kt == 0, False,
                             [t_PT[n % NPT], t_onesbf], [tbs])
                    else:
                        P.tt("pool", Sacc[mp][:, c0:512], Sacc[mp][:, c0:512], PT[n % NPT][:, c0:512], ALU.add,
                             [t_Sacc[mp], t_PT[n % NPT]], [t_Sacc[mp]])
                    if last:
                        P.mm(bs[:, :], ones_f32[:], Sacc[mp][:, :], False, True, [t_Sacc[mp], t_onesbf], [tbs])
                        e_ = qb % 2
                        sm_, tsm_ = SMs[e_][mp], t_SMs[e_][mp]
                        P.act(sm_[:], bs[:, :], AF.Ln, [tbs], [tsm_])
                        P.act(sm_[:], sm_[:], AF.Exp, [tsm_], [tsm_], scale=-1.0)
                        oa, ta = OTs[e_][0], t_OTs[e_][0]
                        if mp == 0:
                            P.tt("dve", oa[:], bo[:, :], sm_[:], ALU.mult, [tbo, tsm_], [ta])
                        else:
                            obb, tb_ = OTs[e_][1], t_OTs[e_][1]
                            P.stt(obb[:], bo[:, :], nlam[:, 0:1], sm_[:], ALU.mult, ALU.mult, [tbo, tsm_, t_nlam], [tb_])
                            P.tt("dve", oa[:], oa[:], obb[:], ALU.add, [ta, tb_], [ta])
                            P.tt("dve", sqb[e_][:], oa[:], oa[:], ALU.mult, [ta], [t_sqb[e_]])
                elif it[0] == "ssq":
                    qb = it[1]
                    e_ = qb % 2
                    P.tt("dve", onb[e_][:], OTs[e_][0][:], rstd[:], ALU.mult, [t_OTs[e_][0], t_rstd], [t_onb[e_]])
                    P.store("sp", ob_d[:, hd, qb * 512:(qb + 1) * 512], onb[e_][:], [t_onb[e_]])

            for n in range(nst + 2):
                if n < nst:
                    stage1(n)
                if 0 <= n - 1 < nst:
                    stage2(n - 1)
                if n - 2 >= 0:
                    stage3(n - 2)

        load_w(0)
        load_aug(0)
        load_w(1)
        projection(0)
        for hd in range(8):
            main = P.capture(lambda: attention(hd))

            def side_fn():
                if hd + 1 < 8:
                    load_aug(hd + 1)
                if hd + 2 < 8:
                    load_w(hd + 2)
                emit_precast(2)
                if hd + 1 < 8:
                    projection(hd + 1)
            side = P.capture(side_fn)
            P.emit_merged(main, side)
        emit_precast(len(precast))
        P.barrier()

    with contextlib.ExitStack() as sb2:
        wgb = sb("wgb", [128, 8, D], BF16, stack=sb2)
        wdo = sb("wdo", [128, 8, D], BF16, stack=sb2)
        t_wgb, t_wdo = T("wgb"), T("wdo")
        sig = sb("sig", [128, 8, 512], stack=sb2)
        t_sig = [T("sig%d" % i) for i in range(8)]
        mst = [sb("mst%d" % i, [128, 8, 512], BF16, stack=sb2) for i in range(2)]
        t_mst = [T("mst0"), T("mst1")]
        obl = [sb("obl%d" % i, [128, 8, 512], BF16, stack=sb2) for i in range(2)]
        t_obl = [T("obl0"), T("obl1")]
        P.load("sp", wgb[:], wgb_s.rearrange("(k p) n -> p k n", p=128), [t_wgb])
        P.load("sp", wdo[:], wdo_s.rearrange("(k p) n -> p k n", p=128), [t_wdo])
        for kc in range(8):
            P.ts("dve", wdo[:, kc, :], wdo[:, kc, :], dgs[:, 0:1], None, ALU.mult, None, [t_wdo, t_nlam], [t_wdo])
        n_ = 0
        for tb in range(8):
            tsl = slice(tb * 512, (tb + 1) * 512)
            th = t_hT[tb * 4:tb * 4 + 4]
            ol, tol = obl[tb % 2], t_obl[tb % 2]
            P.load("sp", ol[:], ob_d[:, :, tsl], [tol])
            for dc in range(8):
                pg, tpg = bank[n_ % 4], t_bank[n_ % 4]
                n_ += 1
                for kc in range(8):
                    P.mm(pg[:, :], wgb[:, kc, dc * 128:(dc + 1) * 128], hT[:, kc, tsl], kc == 0, kc == 7,
                         [t_wgb] + th, [tpg])
                P.act(sig[:, dc, :], pg[:, :], AF.Sigmoid, [tpg], [t_sig[dc]])
            for dc in range(8):
                pg, tpg = bank[n_ % 4], t_bank[n_ % 4]
                n_ += 1
                for kc in range(8):
                    P.mm(pg[:, :], wdo[:, kc, dc * 128:(dc + 1) * 128], ol[:, kc, :], kc == 0, kc == 7,
                         [t_wdo, tol], [tpg])
                P.tt("dve", mst[tb % 2][:, dc, :], pg[:, :], sig[:, dc, :], ALU.mult,
                     [tpg, t_sig[dc]], [t_mst[tb % 2]])
            P.store("sp", mb_d[:, :, tsl], mst[tb % 2][:], [t_mst[tb % 2]])
        P.barrier()
    s10.close()
    if stage == "mb":
        with contextlib.ExitStack() as sd:
            mdbg = sb("mdbg", [128, 8, S], BF16, stack=sd)
            t_md = T("mdbg")
            P.load("sp", mdbg[:], mb_d[:, :, :], [t_md])
            dbg["mT"] = (mdbg, [128, 8, S], [t_md])
            return _finish(nc, P, es, dbg, out_d)

    s13 = contextlib.ExitStack()
    wg = sb("wg", [128, 8, 3088], BF16, stack=s13)
    wga = sb("wga", [128, 8, D], BF16, stack=s13)
    wgo = sb("wgo", [128, 8, D], BF16, stack=s13)
    wo = sb("wo", [128, 8, D], BF16, stack=s13)
    walpha_sb = sb("walpha_sb", [33, 512], stack=s13)
    t_wg, t_wga, t_wgo, t_wo, t_wal = T("wg"), T("wga"), T("wgo"), T("wo"), T("wal")
    P.load("sp", wg[:], wg_s.rearrange("(k p) n -> p k n", p=128), [t_wg])
    P.load("sp", walpha_sb[:], walpha_d[:, :], [t_wal])
    P.load("sp", wga[:], wga_s.rearrange("(k p) n -> p k n", p=128), [t_wga])
    P.load("sp", wgo[:], wgo_s.rearrange("(k p) n -> p k n", p=128), [t_wgo])
    P.load("sp", wo[:], wo_s.rearrange("(k p) n -> p k n", p=128), [t_wo])
    for kc in range(8):
        P.ts("dve", wgo[:, kc, :], wgo[:, kc, :], cols[:, C_GLAG + kc % 2:C_GLAG + kc % 2 + 1], None,
             ALU.mult, None, [t_wgo, t_const], [t_wgo])
    ones128 = sb("ones128", [128, 128], stack=s13)
    Gt = sb("Gt", [128, 128], stack=s13)
    gt_bc = sb("gt_bc", [128, D], stack=s13)
    t_ones, t_G, t_gtbc = T("ones128"), T("G"), T("gtbc")
    P.memset("dve", ones128[:], 1.0, [t_ones])

    def bcast_cols(col0, dst, t_dst):
        for j in range(8):
            P.ts("dve", Gt[:], ones128[:], modc[:, col0 + j:col0 + j + 1], None, ALU.mult, None,
                 [t_ones, t_mod], [t_G])
            P.mm(bank[j // 4][:, (j % 4) * 128:(j % 4 + 1) * 128], Gt[:], ident, True, True,
                 [t_G, t_const], [t_bank[j // 4]])
        for hf in range(2):
            P.copy("dve", dst[:, hf * 512:(hf + 1) * 512], bank[hf][:, :], [t_bank[hf]], [t_dst])

    bcast_cols(16, gt_bc, t_gtbc)
    for kc in range(8):
        P.tt("dve", wo[:, kc, :], wo[:, kc, :], gt_bc[:], ALU.mult, [t_wo, t_gtbc], [t_wo])

    Sst = sb("Sst", [128, 4, 256], stack=s13)
    Sbf = sb("Sbf", [128, 4, 256], BF16, stack=s13)
    t_S = [T("S%d" % h) for h in range(4)]
    t_Sbf = [T("Sbf%d" % h) for h in range(4)]
    P.memset("pool", Sst[:], 0.0, t_S)
    P.memset("pool", Sbf[:], 0.0, t_Sbf)
    MF4 = sb("MF4", [128, 512], BF16, stack=s13)
    MB4 = sb("MB4", [128, 512], BF16, stack=s13)
    t_msk = T("msk")
    for h in range(4):
        P.copy("dve", MF4[:, h * 128:(h + 1) * 128], tri, [t_const], [t_msk])
        P.copy("dve", MB4[:, h * 128:(h + 1) * 128], upp, [t_const], [t_msk])
    acT = sb("acT", [33, 128], stack=s13)
    t_acT = T("acT")
    P.memset("dve", acT[:], 1.0, [t_acT])
    xt3 = [sb("xt3_%d" % i, [128, D], stack=s13) for i in range(2)]
    t_xt3 = [T("xt3_0"), T("xt3_1")]
    ss3 = [sb("ss3_%d" % i, [128, 4], stack=s13) for i in range(2)]
    t_ss3 = [T("ss3_0"), T("ss3_1")]
    scrA = sb("scrA", [128, D], stack=s13)
    scrB = sb("scrB", [128, D], stack=s13)
    t_scrA, t_scrB = T("scrA"), T("scrB")
    hTt = [sb("hTt%d" % i, [128, 8, 128], BF16, stack=s13) for i in range(2)]
    t_hTt = [T("hTt0"), T("hTt1")]
    v_sb = sb("v_sb", [128, D], BF16, stack=s13)
    sg = sb("sg", [128, D], BF16, stack=s13)
    Ep = sb("Ep", [128, 512], stack=s13)
    Em = sb("Em", [128, 512], stack=s13)
    scrC = sb("scrC", [128, 512], stack=s13)
    la = sb("la", [128, 512], stack=s13)
    qf = sb("qf", [128, 512], BF16, stack=s13)
    qbw = sb("qbw", [128, 512], BF16, stack=s13)
    kf = sb("kf", [128, 512], BF16, stack=s13)
    kbw = sb("kbw", [128, 512], BF16, stack=s13)
    kd = sb("kd", [128, 512], BF16, stack=s13)
    tmpF = sb("tmpF", [128, 512], BF16, stack=s13)
    tmpB = sb("tmpB", [128, 512], BF16, stack=s13)
    scT = sb("scT", [128, 512], BF16, stack=s13)
    oaT = sb("oaT", [128, D], BF16, stack=s13)
    m_t = sb("m_t", [128, D], BF16, stack=s13)
    mbt = [sb("mbt%d" % i, [128, D], BF16, stack=s13) for i in range(2)]
    t_mbt = [T("mbt0"), T("mbt1")]
    scrD = [sb("scrD%d" % i, [128, 512], stack=s13) for i in range(2)]
    t_scrD = [T("scrD0"), T("scrD1")]
    junk = sb("junk", [128, 256], BF16, stack=s13)
    smg = [sb("smg%d" % i, [128, 4], stack=s13) for i in range(4)]
    t_smg = [T("smg%d" % i) for i in range(4)]
    (t_v, t_sg, t_Ep, t_Em, t_scrC, t_la, t_qf, t_qbw, t_kf, t_kbw, t_kd, t_tF, t_tB, t_scT, t_oaT,
     t_mt, t_junk) = [T(n) for n in ("v", "sg", "Ep", "Em", "scrC", "la", "qf", "qbw", "kf", "kbw", "kd",
                                     "tF", "tB", "scT", "oaT", "mt", "junk")]
    t_dl = [T("dl%d" % h) for h in range(4)]
    xs3 = sb("xs3", [128, D], stack=s13)
    t_xs3 = T("xs3")
    bufs13 = (xt3, t_xt3, ss3, t_ss3, [xs3, xs3], [t_xs3, t_xs3], [bank[0], bank[1]],
              [t_bank[0], t_bank[1]])
    QS = 128.0 ** -0.5
    NTILES = NT if stage != "x1s" else 2
    def gate_chain(hb, th):
        for kc in range(8):
            P.mm(bank[4][0:16, 0:128], wg[:, kc, OFF_GA:OFF_GA + 16], hb[:, kc, :], kc == 0, kc == 7,
                 [t_wg, th], [t_bank[4]])
        P.copy("dve", acT[0:16, :], bank[4][0:16, 0:128], [t_bank[4]], [t_acT])
        P.mm(bank[5][:, :], acT[:, :], walpha_sb[:, :], True, True, [t_acT, t_wal], [t_bank[5]])
        P.act(scrC[:], bank[5][:, :], AF.Exp, [t_bank[5]], [t_scrC], scale=-1.0)
        P.act(la[:], scrC[:], AF.Ln, [t_scrC], [t_la], bias=1.0)
        for h in range(4):
            P.mm(bank[4][:, h * 128:(h + 1) * 128], la[:, h * 128:(h + 1) * 128], tri, True, True,
                 [t_la, t_const], [t_bank[4]])
        P.mm(bank[5][:, :], upp, la[:, :], True, True, [t_const, t_la], [t_bank[5]])
        P.act(Ep[:], bank[4][:, :], AF.Exp, [t_bank[4]], [t_Ep], scale=-1.0 / 16)
        P.act(Em[:], bank[4][:, :], AF.Exp, [t_bank[4]], [t_Em], scale=1.0 / 16)
        P.act(scrC[:], bank[5][:, :], AF.Exp, [t_bank[5]], [t_scrC], scale=-1.0 / 16)

    def gla_loads(i):
        P.load("sp", xt3[i % 2][:], x_d[i * 128:(i + 1) * 128, :], [t_xt3[i % 2]])
        P.load("sp", mbt[i % 2][:, :].rearrange("p (a b) -> p a b", a=8), mb_d[:, :, i * 128:(i + 1) * 128],
               [t_mbt[i % 2]])

    gla_loads(0)
    for i in range(NTILES):
        if i + 1 < NTILES:
            gla_loads(i + 1)
        bi = i % 2
        if i == 0:
            norm_a(x_d, 0, bufs13, preloaded=True)
            norm_b(0, hTt[0], t_hTt[0], a1c, 0, bufs13, out_is_tile=True)
        hb, th = hTt[i % 2], t_hTt[i % 2]
        mb_ = mbt[i % 2]
        if i == 0:
            gate_chain(hb, th)
        for h in range(4):
            for kc in range(8):
                P.mm(bank[2][:, h * 128:(h + 1) * 128], wg[:, kc, OFF_GQ + h * 128:OFF_GQ + (h + 1) * 128],
                     hb[:, kc, :], kc == 0, kc == 7, [t_wg, th], [t_bank[2]])
        for h in range(4):
            for kc in range(8):
                P.mm(bank[3][:, h * 128:(h + 1) * 128], wg[:, kc, OFF_GK + h * 128:OFF_GK + (h + 1) * 128],
                     hb[:, kc, :], kc == 0, kc == 7, [t_wg, th], [t_bank[3]])
        for hf in range(2):
            for kc in range(8):
                P.mm(bank[6 + hf][:, :], hb[:, kc, :], wg[:, kc, OFF_GV + hf * 512:OFF_GV + (hf + 1) * 512],
                     kc == 0, kc == 7, [t_wg, th], [t_bank[6 + hf]])
        P.stt(qf[:], bank[2][:, :], QS, Ep[:], ALU.mult, ALU.mult, [t_bank[2], t_Ep], [t_qf])
        P.stt(qbw[:], bank[2][:, :], QS, Em[:], ALU.mult, ALU.mult, [t_bank[2], t_Em], [t_qbw])
        P.tt("dve", kf[:], bank[3][:, :], Em[:], ALU.mult, [t_bank[3], t_Em], [t_kf])
        P.tt("dve", kbw[:], bank[3][:, :], Ep[:], ALU.mult, [t_bank[3], t_Ep], [t_kbw])
        for hf in range(2):
            P.copy("act", v_sb[:, hf * 512:(hf + 1) * 512], bank[6 + hf][:, :], [t_bank[6 + hf]], [t_v])
        for dc in range(8):
            for kc in range(8):
                P.mm(bank[dc // 4][:, (dc % 4) * 128:(dc % 4 + 1) * 128], wga[:, kc, dc * 128:(dc + 1) * 128],
                     hb[:, kc, :], kc == 0, kc == 7, [t_wga, th], [t_bank[dc // 4]])
        for kc in range(8):
            P.mm(bank[5][:, :], hb[:, kc, :], wg[:, kc, OFF_GK:OFF_GK + 512], kc == 0, kc == 7,
                 [t_wg, th], [t_bank[5]])
        sga = scrA
        for hf in range(2):
            sl_ = sga[:, hf * 512:(hf + 1) * 512]
            P.act(sl_, bank[hf][:, :], AF.Exp, [t_bank[hf]], [t_scrA], scale=-1.0)
            P.act(sl_, sl_, AF.Ln, [t_scrA], [t_scrA], bias=1.0)
            P.act(sl_, sl_, AF.Exp, [t_scrA], [t_scrA], scale=-1.0)
        P.tt("dve", kd[:], bank[5][:, :], scrC[:], ALU.mult, [t_bank[5], t_scrC], [t_kd])
        for hf in range(2):
            for kc in range(8):
                P.mm(bank[6 + hf][:, :], hb[:, kc, :], wg[:, kc, OFF_GG + hf * 512:OFF_GG + (hf + 1) * 512],
                     kc == 0, kc == 7, [t_wg, th], [t_bank[6 + hf]])
            sd = scrD[hf]
            P.act(sd[:], bank[6 + hf][:, :], AF.Exp, [t_bank[6 + hf]], [t_scrD[hf]], scale=-1.0)
            P.act(sd[:], sd[:], AF.Ln, [t_scrD[hf]], [t_scrD[hf]], bias=1.0)
            P.act(sd[:], sd[:], AF.Exp, [t_scrD[hf]], [t_scrD[hf]], scale=-1.0)
        for h in range(4):
            hs = slice(h * 128, (h + 1) * 128)
            P.mm(bank[0][:, hs], kf[:, hs], qf[:, hs], True, True, [t_kf, t_qf], [t_bank[0]])
        for h in range(4):
            hs = slice(h * 128, (h + 1) * 128)
            P.mm(bank[1][:, hs], kbw[:, hs], qbw[:, hs], True, True, [t_kbw, t_qbw], [t_bank[1]])
        P.tt("dve", tmpF[:], bank[0][:, :], MF4[:], ALU.mult, [t_bank[0], t_msk], [t_tF])
        P.tt("dve", tmpB[:], bank[1][:, :], MB4[:], ALU.mult, [t_bank[1], t_msk], [t_tB])
        P.tt("dve", scT[:], tmpF[:], tmpB[:], ALU.add, [t_tF, t_tB], [t_scT])
        for hf in range(2):
            P.tt("dve", sg[:, hf * 512:(hf + 1) * 512], bank[6 + hf][:, :], scrD[hf][:], ALU.mult,
                 [t_bank[6 + hf], t_scrD[hf]], [t_sg])
        obanks = [2, 3, 6, 7]
        og = scrB
        for h in range(4):
            hs = slice(h * 128, (h + 1) * 128)
            es_ = slice(h * 256, (h + 1) * 256)
            ob_, tob = bank[obanks[h]], t_bank[obanks[h]]
            P.mm(ob_[:, 0:256], scT[:, hs], v_sb[:, es_], True, False, [t_scT, t_v], [tob])
        for c in range(2):
            cs = slice(c * 64, (c + 1) * 64)
            for h in range(4):
                hs = slice(h * 128, (h + 1) * 128)
                es_ = slice(h * 256, (h + 1) * 256)
                ob_, tob = bank[obanks[h]], t_bank[obanks[h]]
                P.mm(ob_[cs, 0:256], qf[:, h * 128 + c * 64:h * 128 + (c + 1) * 64], Sbf[:, h, :],
                     False, c == 1, [t_qf, t_Sbf[h]], [tob])
            for h in range(4):
                hs = slice(h * 128, (h + 1) * 128)
                es_ = slice(h * 256, (h + 1) * 256)
                db_ = bank[4 + h // 2][:, (h % 2) * 256:(h % 2 + 1) * 256]
                tdb = t_bank[4 + h // 2]
                P.mm(db_, kd[cs, hs], v_sb[cs, es_], True, True, [t_kd, t_v], [tdb])
            if c == 1:
                for h in range(4):
                    es_ = slice(h * 256, (h + 1) * 256)
                    ob_, tob = bank[obanks[h]], t_bank[obanks[h]]
                    smb, tsm = smg[h], t_smg[h]
                    P.act(junk[:], ob_[:, 0:256], AF.Square, [tob], [t_junk, tsm], accum_out=smb[:, 0:1])
                    P.act(smb[:, 1:2], smb[:, 0:1], AF.Ln, [tsm], [tsm], bias=EPS, scale=1.0 / 256)
                    P.act(smb[:, 2:3], smb[:, 1:2], AF.Exp, [tsm], [tsm], scale=-0.5)
                    P.stt(og[:, es_], ob_[:, 0:256], smb[:, 2:3], sg[:, es_], ALU.mult, ALU.mult,
                          [tob, tsm, t_sg], [t_scrB])
            for h in range(4):
                db_ = bank[4 + h // 2][:, (h % 2) * 256:(h % 2 + 1) * 256]
                tdb = t_bank[4 + h // 2]
                dcol = h * 128 + 63 + 64 * c
                P.stt(Sst[:, h, :], Sst[:, h, :], Ep[:, dcol:dcol + 1], db_, ALU.mult, ALU.add,
                      [t_S[h], t_Ep, tdb], [t_S[h]])
                P.copy("act", Sbf[:, h, :], Sst[:, h, :], [t_S[h]], [t_Sbf[h]])
        for kc in range(8):
            P.tr(bank[kc // 4][:, (kc % 4) * 128:(kc % 4 + 1) * 128], og[:, kc * 128:(kc + 1) * 128], ident,
                 [t_scrB, t_const], [t_bank[kc // 4]])
        for hf in range(2):
            P.copy("act", oaT[:, hf * 512:(hf + 1) * 512], bank[hf][:, :], [t_bank[hf]], [t_oaT])
        if i + 1 < NTILES:
            norm_a(x_d, i + 1, bufs13, preloaded=True)
            norm_b(i + 1, hTt[(i + 1) % 2], t_hTt[(i + 1) % 2], a1c, 0, bufs13, out_is_tile=True)
        for dc in range(8):
            for kc in range(8):
                P.mm(bank[2 + dc // 4][:, (dc % 4) * 128:(dc % 4 + 1) * 128], wgo[:, kc, dc * 128:(dc + 1) * 128],
                     oaT[:, kc * 128:(kc + 1) * 128], kc == 0, kc == 7, [t_wgo, t_oaT], [t_bank[2 + dc // 4]])
        if i + 1 < NTILES:
            gate_chain(hTt[(i + 1) % 2], t_hTt[(i + 1) % 2])
        for hf in range(2):
            P.tt("dve", sga[:, hf * 512:(hf + 1) * 512], bank[2 + hf][:, :], sga[:, hf * 512:(hf + 1) * 512],
                 ALU.mult, [t_bank[2 + hf], t_scrA], [t_scrA])
        P.tt("dve", m_t[:], sga[:], mb_[:], ALU.add, [t_scrA, t_mbt[i % 2]], [t_mt])
        for hf in range(2):
            for kc in range(8):
                P.mm(bank[6 + hf][:, :], m_t[:, kc * 128:(kc + 1) * 128], wo[:, kc, hf * 512:(hf + 1) * 512],
                     kc == 0, kc == 7, [t_mt, t_wo], [t_bank[6 + hf]])
        for hf in range(2):
            P.tt("dve", xt3[bi][:, hf * 512:(hf + 1) * 512], bank[6 + hf][:, :],
                 xt3[bi][:, hf * 512:(hf + 1) * 512], ALU.add, [t_bank[6 + hf], t_xt3[bi]], [t_xt3[bi]])
        P.store("sp", out_d[i * 128:(i + 1) * 128, :], xt3[bi][:], [t_xt3[bi]])
    P.barrier()
    s13.close()
    if stage in ("x1", "x1s"):
        return _finish(nc, P, es, dbg, out_d)

    s2 = contextlib.ExitStack()
    wup = sb("wup", [128, 8, 2 * F], BF16, stack=s2)
    wdn = sb("wdn", [128, NFC, D], BF16, stack=s2)
    t_wup, t_wdn = T("wup"), T("wdn")
    wup_v = wup_s.rearrange("(k p) n -> p k n", p=128)
    for q4 in range(2):
        P.load("sp", wup[:, :, q4 * 2816:(q4 + 1) * 2816], wup_v[:, :, q4 * 2816:(q4 + 1) * 2816], [t_wup])
    wdn_v = wdn_s.rearrange("(c p) n -> p c n", p=128)
    P.load("sp", wdn[:, 0:11, :], wdn_v[:, 0:11, :], [t_wdn])
    P.load("sp", wdn[:, 11:22, :], wdn_v[:, 11:22, :], [t_wdn])
    xt2 = [sb("xt2_%d" % i, [128, D], stack=s2) for i in range(2)]
    t_xt2 = [T("xt2_0"), T("xt2_1")]
    xr = [sb("xr%d" % i, [128, D], stack=s2) for i in range(2)]
    t_xr = [T("xr0"), T("xr1")]
    ss2 = [sb("ss2_%d" % i, [128, 4], stack=s2) for i in range(2)]
    t_ss2 = [T("ss2_0"), T("ss2_1")]
    xs2 = sb("xs2", [128, D], stack=s2)
    t_xs2 = T("xs2")
    h2T = sb("h2T", [128, 8, 512], BF16, stack=s2)
    t_h2T = [T("h2T%d" % i) for i in range(4)]
    Uab = [[sb("U%d_%d" % (p_, i), [128, 514], stack=s2) for p_ in range(2)] for i in range(2)]
    accab = [[sb("acc%d_%d" % (p_, i), [128, 512], stack=s2) for p_ in range(2)] for i in range(2)]
    t_Uab = [[T("U%d_%d" % (p_, i)) for p_ in range(2)] for i in range(2)]
    t_accab = [[T("acc%d_%d" % (p_, i)) for p_ in range(2)] for i in range(2)]
    t_Ucab = [[T("Uc%d_%d" % (p_, i)) for p_ in range(2)] for i in range(2)]
    actT = sb("actT", [128, NFC, 512], BF16, stack=s2)
    t_actT = [T("actT%d" % i) for i in range(NFC)]
    car = sb("car", [128, 2 * NFC, 2], stack=s2)
    t_car = [T("car%d" % i) for i in range(2 * NFC)]
    P.memset("dve", car[:], 0.0, t_car)
    ones2, t_o2 = Uab[0][0][:, 0:128], t_Uab[0][0]
    Gt2, t_G2 = Uab[0][1][:, 0:128], t_Uab[0][1]
    P.memset("dve", ones2, 1.0, [t_o2])
    for j in range(8):
        P.ts("dve", Gt2, ones2, modc[:, 40 + j:41 + j], None, ALU.mult, None, [t_o2, t_mod], [t_G2])
        P.mm(bank[j // 4][:, (j % 4) * 128:(j % 4 + 1) * 128], Gt2, ident, True, True,
             [t_G2, t_const], [t_bank[j // 4]])
    for hf in range(2):
        P.copy("dve", accab[0][hf][:], bank[hf][:, :], [t_bank[hf]], [t_accab[0][hf]])
    for fc in range(NFC):
        for hf in range(2):
            P.tt("dve", wdn[:, fc, hf * 512:(hf + 1) * 512], wdn[:, fc, hf * 512:(hf + 1) * 512],
                 accab[0][hf][:], ALU.mult, [t_wdn, t_accab[0][hf]], [t_wdn])
    bufs2 = (xt2, t_xt2, ss2, t_ss2, [xs2, xs2], [t_xs2, t_xs2], [bank[6], bank[7]], [t_bank[6], t_bank[7]])
    NB = 8 if stage != "ffns" else 1

    def norm_block(tb):
        for tt_ in range(4):
            ti = tb * 4 + tt_
            norm_tile(out_d, ti, h2T, t_h2T[tt_], a2c, 24, None, bufs2, col=tt_)

    cwc = lambda j, ch: cols[:, C_CW + 44 * j + ch:C_CW + 44 * j + ch + 1]
    cbc = lambda ch: cols[:, C_CB + ch:C_CB + ch + 1]
    norm_block(0)
    for tb in range(NB):
        for fc in range(NFC):
            db = fc % 2
            parts = tuple((Uab[db][p_], t_Uab[db][p_], accab[db][p_], t_accab[db][p_]) for p_ in range(2))
            acca, t_acca, accb, t_accb = parts[0][2], parts[0][3], parts[1][2], parts[1][3]
            for part, (U_, tU, acc_, tacc) in enumerate(parts):
                ch = part * NFC + fc
                pu, tpu = bank[(2 * fc + part) % 4], t_bank[(2 * fc + part) % 4]
                for kc in range(8):
                    P.mm(pu[:, :], wup[:, kc, ch * 128:(ch + 1) * 128], h2T[:, kc, :], kc == 0, kc == 7,
                         [t_wup] + t_h2T, [tpu])
            for part, (U_, tU, acc_, tacc) in enumerate(parts):
                ch = part * NFC + fc
                pu, tpu = bank[(2 * fc + part) % 4], t_bank[(2 * fc + part) % 4]
                P.act(acc_[:], pu[:, :], AF.Identity, [tpu, t_const], [tacc], scale=cwc(2, ch), bias=cbc(ch))
            for tap in (1, 0):
                sh = 2 - tap
                for part, (U_, tU, acc_, tacc) in enumerate(parts):
                    ch = part * NFC + fc
                    pu, tpu = bank[(2 * fc + part) % 4], t_bank[(2 * fc + part) % 4]
                    P.stt(acc_[:, sh:512], pu[:, 0:512 - sh], cwc(tap, ch), acc_[:, sh:512], ALU.mult, ALU.add,
                          [tpu, t_const, tacc], [tacc])
            for part, (U_, tU, acc_, tacc) in enumerate(parts):
                ch = part * NFC + fc
                pu, tpu = bank[(2 * fc + part) % 4], t_bank[(2 * fc + part) % 4]
                P.stt(acc_[:, 0:1], car[:, ch, 1:2], cwc(1, ch), acc_[:, 0:1], ALU.mult, ALU.add,
                      [t_car[ch], t_const, tacc], [tacc])
                P.stt(acc_[:, 0:2], car[:, ch, 0:2], cwc(0, ch), acc_[:, 0:2], ALU.mult, ALU.add,
                      [t_car[ch], t_const, tacc], [tacc])
                P.copy("dve", car[:, ch, :], pu[:, 510:512], [tpu], [t_car[ch]])
            P.act(acca[:], acca[:], AF.Silu, [t_acca], [t_acca])
            P.tt("pool", actT[:, fc, :], acca[:], accb[:], ALU.mult, [t_acca, t_accb], [t_actT[fc]])
        for tt_ in range(4):
            ti = tb * 4 + tt_
            r_ = xr[ti % 2]
            tr_ = t_xr[ti % 2]
            P.load("sp", r_[:], out_d[ti * 128:(ti + 1) * 128, :], [tr_])
            b0 = 4
            for hf in range(2):
                for fc in range(NFC):
                    P.mm(bank[b0 + hf][:, :], actT[:, fc, tt_ * 128:(tt_ + 1) * 128],
                         wdn[:, fc, hf * 512:(hf + 1) * 512], fc == 0, fc == NFC - 1,
                         [t_actT[fc], t_wdn], [t_bank[b0 + hf]])
            nti = (tb + 1) * 4 + tt_
            if tb + 1 < NB:
                norm_a(out_d, nti, bufs2)
            for hf in range(2):
                P.tt("dve", r_[:, hf * 512:(hf + 1) * 512], bank[b0 + hf][:, :], r_[:, hf * 512:(hf + 1) * 512],
                     ALU.add, [t_bank[b0 + hf], tr_], [tr_])
            P.store("sp", out_d[ti * 128:(ti + 1) * 128, :], r_[:], [tr_])
            if tb + 1 < NB:
                norm_b(nti, h2T, t_h2T[tt_], a2c, 24, bufs2, col=tt_)
    P.barrier()
    P.finalize(es)
    return nc, {}


def _finish(nc, P, es, dbg, out_d):
    outs = {}
    for name, (tile_, shape, deps) in dbg.items():
        d = nc.dram_tensor("dbg_" + name, list(shape), tile_.dtype if hasattr(tile_, "dtype") else F32,
                           kind="ExternalOutput").ap()
        outs[name] = d
        P.store("pool", d, tile_[:], list(deps))
    P.barrier()
    P.finalize(es)
    return nc, outs


_CONSTS = None


def _in_maps(inputs, cores):
    global _CONSTS
    if _CONSTS is None:
        _CONSTS = _host_consts()
    inp = {k: np.asarray(v) for k, v in inputs.items()}
    return [_host_inputs(b, inp, _CONSTS) for b in cores]


_NC_CACHE = {}


def kernel(**inputs):
    if "nc" not in _NC_CACHE:
        _NC_CACHE["nc"] = build("full")[0]
    nc = _NC_CACHE["nc"]
    maps = _in_maps(inputs, list(range(8)))
    res = run_bass_kernel_spmd(nc, maps, core_ids=list(range(8)))
    out = np.stack([np.asarray(res.results[b]["out"]) for b in range(8)], axis=0)
    return out.astype(np.float32)
```

```python
import contextlib
import numpy as np
import concourse.bass as bass
import concourse.mybir as mybir
from concourse.bass_utils import run_bass_kernel_spmd

F32 = mybir.dt.float32
BF16 = mybir.dt.bfloat16
AF = mybir.ActivationFunctionType
ALU = mybir.AluOpType

D = 1024
S = 4096
NT = S // 128
EPS = 1e-6
F = 2816
NFC = F // 128
IN_DIM = 8208
OFF_GQ, OFF_GK, OFF_GV, OFF_GG, OFF_GA = 0, 512, 1024, 2048, 3072
OFF_DQ, OFF_DK, OFF_DV, OFF_GATEA, OFF_GATEB = 3088, 4112, 5136, 6160, 7184
LAMBDA_INIT = 0.8 - 0.6 * 1.0
NEG = -30000.0
NO_MERGE = False


class T:
    __slots__ = ("name", "w", "r", "wcnt", "rcnt")

    def __init__(self, name):
        self.name = name
        self.w = None
        self.r = {}
        self.wcnt = 0
        self.rcnt = 0


class Prog:
    def __init__(self, nc):
        self.nc = nc
        self.ops = []
        self.engs = {"pe": nc.tensor, "act": nc.scalar, "dve": nc.vector,
                     "pool": nc.gpsimd, "sp": nc.sync}

    def op(self, eng, fn, R=(), W=(), dma=None, owner=None):
        self.ops.append((eng, fn, tuple(R), tuple(W), dma, owner))

    def barrier(self):
        self.ops.append(("barrier", None, (), (), None, None))

    def capture(self, fn):
        saved, self.ops = self.ops, []
        fn()
        out, self.ops = self.ops, saved
        return out

    def emit_merged(self, main, side):
        if not side or NO_MERGE:
            self.ops.extend(main)
            self.ops.extend(side)
            return
        nm, ns = len(main), len(side)
        j = 0
        for i, o in enumerate(main):
            self.ops.append(o)
            tgt = (i + 1) * ns // nm
            while j < tgt:
                self.ops.append(side[j])
                j += 1
        self.ops.extend(side[j:])

    def mm(self, out, lhsT, rhs, start, stop, R, W, **kw):
        nc = self.nc
        self.op("pe", lambda: nc.tensor.matmul(out, lhsT=lhsT, rhs=rhs, start=start,
                                               stop=stop, **kw), R, W)

    def tr(self, out, in_, ident, R, W):
        nc = self.nc
        self.op("pe", lambda: nc.tensor.transpose(out, in_, ident), R, W)

    def act(self, out, in_, func, R, W, bias=None, scale=None, accum_out=None):
        nc = self.nc
        kw = {}
        if bias is not None:
            kw["bias"] = bias
        if scale is not None:
            kw["scale"] = scale
        if accum_out is not None:
            kw["accum_out"] = accum_out
        self.op("act", lambda: nc.scalar.activation(out=out, in_=in_, func=func, **kw), R, W)

    def ts(self, eng, out, in0, s1, s2, op0, op1, R, W):
        e = self.engs[eng]
        if op1 is None:
            self.op(eng, lambda: e.tensor_scalar(out=out, in0=in0, scalar1=s1, scalar2=None,
                                                 op0=op0), R, W)
        else:
            self.op(eng, lambda: e.tensor_scalar(out=out, in0=in0, scalar1=s1, scalar2=s2,
                                                 op0=op0, op1=op1), R, W)

    def stt(self, out, in0, scalar, in1, op0, op1, R, W):
        nc = self.nc
        self.op("dve", lambda: nc.vector.scalar_tensor_tensor(out=out, in0=in0, scalar=scalar,
                                                              in1=in1, op0=op0, op1=op1), R, W)

    def tt(self, eng, out, in0, in1, op, R, W):
        e = self.engs[eng]
        self.op(eng, lambda: e.tensor_tensor(out=out, in0=in0, in1=in1, op=op), R, W)

    def copy(self, eng, out, in_, R, W):
        if eng == "act":
            nc = self.nc
            self.op("act", lambda: nc.scalar.copy(out=out, in_=in_), R, W)
        else:
            e = self.engs[eng]
            self.op(eng, lambda: e.tensor_copy(out=out, in_=in_), R, W)

    def recip(self, out, in_, R, W):
        nc = self.nc
        self.op("dve", lambda: nc.vector.reciprocal(out=out, in_=in_), R, W)

    def memset(self, eng, ap, val, W):
        e = self.engs[eng]
        self.op(eng, lambda: e.memset(ap, val), (), W)

    def load(self, eng, out, in_, W, owner=None, **kw):
        e = self.engs[eng]
        self.op(eng, lambda: e.dma_start(out=out, in_=in_, **kw), (), W, dma="w",
                owner=owner if owner is not None else W[0])

    def store(self, eng, out, in_, R, owner=None):
        e = self.engs[eng]
        self.op(eng, lambda: e.dma_start(out=out, in_=in_), R, (), dma="r",
                owner=owner if owner is not None else R[0])

    def finalize(self, es):
        nc = self.nc
        ops = self.ops
        n = len(ops)
        deps_ops = [None] * n
        deps_dma = [None] * n
        signal = [False] * n
        dma_ev = [None] * n
        last_op = {}
        dma_latest = {}
        pending_bar = {}
        semkeys = {}
        for i, (eng, fn, R, W, dma, owner) in enumerate(ops):
            d_ops = {}
            d_dma = {}
            if eng == "barrier":
                for e, idx in last_op.items():
                    if e != "sp":
                        d_ops[e] = idx
                d_dma = dict(dma_latest)
                deps_ops[i], deps_dma[i] = d_ops, d_dma
                ev = ("op", "sp", i)
                last_op["sp"] = i
                for e in self.engs:
                    if e != "sp":
                        pending_bar[e] = ev
                continue

            mykey = (id(owner), dma) if dma is not None else None

            def add(ev, eng=eng, d_ops=d_ops, d_dma=d_dma, mykey=mykey):
                if ev is None:
                    return
                if ev[0] == "dma" and ev[1] == mykey:
                    return
                if ev[0] == "op":
                    e, idx = ev[1], ev[2]
                    if e == "pe" and eng == "pe":
                        return
                    if d_ops.get(e, -1) < idx:
                        d_ops[e] = idx
                else:
                    k, v = ev[1], ev[2]
                    if d_dma.get(k, 0) < v:
                        d_dma[k] = v

            if eng in pending_bar:
                add(pending_bar.pop(eng))
            for t in R:
                add(t.w)
            for t in W:
                add(t.w)
                for ev in t.r.values():
                    add(ev)
            deps_ops[i], deps_dma[i] = d_ops, d_dma
            if dma is None:
                ev = ("op", eng, i)
                rkey = eng
            else:
                key = (id(owner), dma)
                if key not in semkeys:
                    semkeys[key] = None
                if dma == "w":
                    owner.wcnt += 1
                    val = 16 * owner.wcnt
                else:
                    owner.rcnt += 1
                    val = 16 * owner.rcnt
                ev = ("dma", key, val)
                dma_ev[i] = (key, val)
                dma_latest[key] = val
                rkey = key
            if dma is None:
                last_op[eng] = i
            for t in R:
                t.r[rkey] = ev
            for t in W:
                t.w = ev
                t.r = {}
        for i in range(n):
            for e, idx in deps_ops[i].items():
                signal[idx] = True
        esem = {e: es.enter_context(nc.semaphore("c_" + e)) for e in self.engs}
        for j, key in enumerate(semkeys):
            semkeys[key] = es.enter_context(nc.semaphore("d%d" % j))
        self.n_sems = len(esem) + len(semkeys)
        rank = [0] * n
        cnt = {e: 0 for e in self.engs}
        for i, o in enumerate(ops):
            eng = "sp" if o[0] == "barrier" else o[0]
            if o[4] is not None:
                signal[i] = False
            elif signal[i] or o[0] == "barrier":
                cnt[eng] += 1
                signal[i] = True
            rank[i] = cnt[eng]
        known = {e: {} for e in self.engs}
        for i, (eng, fn, R, W, dma, owner) in enumerate(ops):
            real = "sp" if eng == "barrier" else eng
            e = self.engs[real]
            kn = known[real]
            for pe_, idx in deps_ops[i].items():
                v = rank[idx]
                if kn.get(pe_, 0) < v:
                    e.wait_ge(esem[pe_], v)
                    kn[pe_] = v
            for key, v in deps_dma[i].items():
                if kn.get(key, 0) < v:
                    e.wait_ge(semkeys[key], v)
                    kn[key] = v
            if eng == "barrier":
                ins = nc.sync.nop()
                ins.then_inc(esem["sp"], 1)
                continue
            ins = fn()
            if dma is not None:
                ins.then_inc(semkeys[dma_ev[i][0]], 16)
            elif signal[i]:
                ins.then_inc(esem[real], 1)
        self.max_rank = dict(cnt)


def _host_consts():
    p = np.arange(128)
    same = (p[:, None] // 64) == (p[None, :] // 64)
    ident = np.eye(128, dtype=np.float32)
    tri = (same & (p[:, None] <= p[None, :])).astype(np.float32)
    upp = (same & (p[:, None] > p[None, :])).astype(np.float32)
    cf32 = np.concatenate([ident, tri, upp], axis=1)
    blockones = same.astype(np.float32)
    cm = []
    for h in range(8):
        slope = 2.0 ** (-(h + 1))
        s = p[:, None]
        q = p[None, :]
        c = np.zeros((128, 128), np.float32)
        c = np.where(same & (s > q), -2.0 * slope * (s - q), c)
        c = np.where((s // 64) > (q // 64), NEG, c)
        cm.append(c.astype(np.float32))
    cbf = np.concatenate([ident, blockones] + cm, axis=1)
    t = np.arange(S)
    aug = np.zeros((8, 2, 4, S), np.float32)
    for h in range(8):
        slope = 2.0 ** (-(h + 1))
        aug[h, 0, 0] = -slope * 64.0 * (t // 64)
        aug[h, 0, 1] = -slope * (t % 64)
        aug[h, 0, 2] = 1.0
        aug[h, 0, 3] = 1.0
        aug[h, 1, 0] = 1.0
        aug[h, 1, 1] = 1.0
        aug[h, 1, 2] = slope * 64.0 * (t // 64)
        aug[h, 1, 3] = slope * (t % 64)
    return cf32, cbf, aug


def _col(v):
    v = np.asarray(v, np.float32)
    return np.ascontiguousarray(v.reshape(-1, 128).T)


NCOL = 48 + 8 + 8 + 1 + 1 + 132 + 44 + 2 + 1
C_BADA, C_G1, C_G2, C_QG, C_KG, C_CW, C_CB, C_GLAG, C_DG = 0, 48, 56, 64, 65, 66, 198, 242, 244


def _host_inputs(b, inp, consts):
    cf32, cbf, aug = consts
    cols = np.zeros((128, NCOL), np.float32)
    cols[:, C_BADA:C_BADA + 48] = _col(inp["b_ada"][0])
    cols[:, C_G1:C_G1 + 8] = _col(inp["norm1_g"][0])
    cols[:, C_G2:C_G2 + 8] = _col(inp["norm2_g"][0])
    cols[:, C_QG] = np.tile(inp["q_norm_g"][0], 2)
    cols[:, C_KG] = np.tile(inp["k_norm_g"][0], 2)
    cw = inp["conv_w"][0]
    for j in range(3):
        cols[:, C_CW + 44 * j:C_CW + 44 * (j + 1)] = _col(cw[j])
    cols[:, C_CB:C_CB + 44] = _col(inp["conv_b"][0])
    cols[:, C_GLAG:C_GLAG + 2] = _col(inp["gla_norm_g"][0])
    cols[:, C_DG] = inp["diff_norm_g"][0]
    lam = np.stack([inp["lam_q1"][0], inp["lam_k1"][0], inp["lam_q2"][0], inp["lam_k2"][0]])
    walpha = np.zeros((33, 512), np.float32)
    walpha[0:16] = inp["w_alpha_up"][0]
    walpha[32] = inp["b_alpha"][0]
    return {
        "x": np.ascontiguousarray(inp["x"][b]),
        "c_col": _col(inp["c"][b]),
        "cols": cols,
        "lam": np.ascontiguousarray(lam.reshape(1, 256)),
        "walpha": walpha,
        "w_ada": np.ascontiguousarray(inp["w_ada"][0]),
        "w_in": np.ascontiguousarray(inp["w_in"][0]),
        "w_gla_o": np.ascontiguousarray(inp["w_gla_o"][0]),
        "w_diff_o": np.ascontiguousarray(inp["w_diff_o"][0]),
        "w_out": np.ascontiguousarray(inp["w_out"][0]),
        "w_up": np.ascontiguousarray(inp["w_up"][0]),
        "w_down": np.ascontiguousarray(inp["w_down"][0]),
        "cf32": cf32, "cbf": cbf, "aug": aug,
    }


def build(stage="full"):
    nc = bass.Bass("TRN2", target_bir_lowering=False)
    P = Prog(nc)
    dram = {}

    def din(name, shape):
        dram[name] = nc.dram_tensor(name, list(shape), F32, kind="ExternalInput").ap()
        return dram[name]

    x_d = din("x", [S, D])
    ccol_d = din("c_col", [128, 8])
    cols_d = din("cols", [128, NCOL])
    lam_d = din("lam", [1, 256])
    walpha_d = din("walpha", [33, 512])
    wada_d = din("w_ada", [D, 6 * D])
    win_d = din("w_in", [D, IN_DIM])
    wglao_d = din("w_gla_o", [D, D])
    wdiffo_d = din("w_diff_o", [D, D])
    wout_d = din("w_out", [D, D])
    wup_d = din("w_up", [D, 2 * F])
    wdown_d = din("w_down", [F, D])
    cf32_d = din("cf32", [128, 384])
    cbf_d = din("cbf", [128, 1280])
    aug_d = din("aug", [8, 2, 4, S])
    out_d = nc.dram_tensor("out", [S, D], F32, kind="ExternalOutput").ap()
    dbg = {}

    es = contextlib.ExitStack()

    def sb(name, shape, dt=F32, stack=None):
        return (stack or es).enter_context(nc.sbuf_tensor("s_" + name, list(shape), dt))

    def ps(name, shape, dt=F32, stack=None):
        return (stack or es).enter_context(nc.psum_tensor("p_" + name, list(shape), dt))

    cf32 = sb("cf32", [128, 384])
    cbf = sb("cbf", [128, 1280], BF16)
    cols = sb("cols", [128, NCOL])
    ccol = sb("ccol", [128, 8])
    lam_sb = sb("lam_sb", [1, 256])
    t_const = T("const")
    P.load("sp", cf32[:], cf32_d[:, :], [t_const])
    P.load("pool", cbf[:], cbf_d[:, :], [t_const], owner=t_const)
    P.load("sp", cols[:], cols_d[:, :], [t_const], owner=t_const)
    P.load("sp", ccol[:], ccol_d[:, :], [t_const], owner=t_const)
    P.load("sp", lam_sb[:], lam_d[:, :], [t_const], owner=t_const)
    ident = cf32[:, 0:128]
    tri = cf32[:, 128:256]
    upp = cf32[:, 256:384]
    ident_bf = cbf[:, 0:128]
    blockones = cbf[:, 128:256]

    if stage == "c0":
        dbg["cols"] = (cols, [128, NCOL], [t_const])
        pass
        return _finish(nc, P, es, dbg, out_d)
    modc = sb("modc", [128, 48])
    a1c = sb("a1c", [128, 8])
    a2c = sb("a2c", [128, 8])
    t_mod = T("mod")

    with contextlib.ExitStack() as s0:
        silc = sb("silc", [128, 8], stack=s0)
        t_silc = T("silc")
        P.act(silc[:], ccol[:], AF.Silu, [t_const], [t_silc])
        if stage == "c1":
            dbg["silc"] = (silc, [128, 8], [t_silc])
            return _finish(nc, P, es, dbg, out_d)
        wa = [sb("wa%d" % i, [128, 8, 512], stack=s0) for i in range(2)]
        t_wa = [T("wa0"), T("wa1")]
        pm = ps("pm", [128, 48], stack=s0)
        t_pm = T("pm")
        wada_v = wada_d.rearrange("(k p) n -> p k n", p=128)
        for nb in range(12):
            bi = nb % 2
            P.load("sp", wa[bi][:], wada_v[:, :, nb * 512:(nb + 1) * 512], [t_wa[bi]])
            for j in range(4):
                nchunk = nb * 4 + j
                for kc in range(8):
                    P.mm(pm[:, nchunk:nchunk + 1], wa[bi][:, kc, j * 128:(j + 1) * 128],
                         silc[:, kc:kc + 1], kc == 0, kc == 7, [t_wa[bi], t_silc], [t_pm])
        P.tt("dve", modc[:], pm[:], cols[:, C_BADA:C_BADA + 48], ALU.add, [t_pm, t_const], [t_mod])
        if stage == "c2":
            dbg["modc"] = (modc, [128, 48], [t_mod])
            return _finish(nc, P, es, dbg, out_d)
        P.stt(a1c[:], modc[:, 8:16], 1.0, cols[:, C_G1:C_G1 + 8], ALU.add, ALU.mult,
              [t_mod, t_const], [t_mod])
        P.stt(a2c[:], modc[:, 32:40], 1.0, cols[:, C_G2:C_G2 + 8], ALU.add, ALU.mult,
              [t_mod, t_const], [t_mod])
    if stage == "mod":
        dbg["modc"] = (modc, [128, 48], [t_mod])
        dbg["a1c"] = (a1c, [128, 8], [t_mod])
        return _finish(nc, P, es, dbg, out_d)

    nlam = sb("nlam", [128, 1])
    ones_row = sb("ones_row", [1, 128])
    qgs = sb("qgs", [128, 2])
    dgs = sb("dgs", [128, 1])
    mb_d = nc.dram_tensor("mb_scratch", [128, 8, S], BF16, kind="Internal").ap()
    s10 = contextlib.ExitStack()
    hT = sb("hT", [128, 8, S], BF16, stack=s10)
    t_hT = [T("hT%d" % i) for i in range(NT)]

    def norm_a(src_ap_dram, i, bufs, preloaded=False):
        xt, t_xt, ss, t_ss, xs, t_xs, ptr, t_ptr = bufs
        bi = i % len(xt)
        if not preloaded:
            P.load("sp", xt[bi][:], src_ap_dram[i * 128:(i + 1) * 128, :], [t_xt[bi]])
        b2 = i % 2
        P.act(xs[b2][:], xt[bi][:], AF.Square, [t_xt[bi]], [t_xs[b2], t_ss[b2]],
              accum_out=ss[b2][:, 0:1])
        P.act(ss[b2][:, 1:2], ss[b2][:, 0:1], AF.Ln, [t_ss[b2]], [t_ss[b2]],
              bias=EPS, scale=1.0 / D)
        P.act(ss[b2][:, 2:3], ss[b2][:, 1:2], AF.Exp, [t_ss[b2]], [t_ss[b2]], scale=-0.5)
        P.ts("dve", xs[b2][:], xt[bi][:], ss[b2][:, 2:3], None, ALU.mult, None,
             [t_xt[bi], t_ss[b2]], [t_xs[b2]])
        return bi

    def norm_b(i, dst, t_dst, ac, shc_off, bufs, out_is_tile=False, col=None):
        xt, t_xt, ss, t_ss, xs, t_xs, ptr, t_ptr = bufs
        b2 = i % 2
        for half in range(2):
            pt = ptr[half]
            for j in range(4):
                kc = half * 4 + j
                P.tr(pt[:, j * 128:(j + 1) * 128], xs[b2][:, kc * 128:(kc + 1) * 128], ident,
                     [t_xs[b2], t_const], [t_ptr[half]])
            for j in range(4):
                kc = half * 4 + j
                if out_is_tile:
                    o = dst[:, kc, :]
                elif col is not None:
                    o = dst[:, kc, col * 128:(col + 1) * 128]
                else:
                    o = dst[:, kc, i * 128:(i + 1) * 128]
                P.ts("dve", o, pt[:, j * 128:(j + 1) * 128], ac[:, kc:kc + 1],
                     modc[:, shc_off + kc:shc_off + kc + 1], ALU.mult, ALU.add,
                     [t_ptr[half], t_mod], [t_dst])

    def norm_tile(src_ap_dram, i, dst, t_dst, ac, shc_off, stk, bufs, out_is_tile=False, col=None,
                  preloaded=False):
        bi = norm_a(src_ap_dram, i, bufs, preloaded)
        norm_b(i, dst, t_dst, ac, shc_off, bufs, out_is_tile, col)
        return bi

    with contextlib.ExitStack() as s1:
        xt = [sb("xt%d" % i, [128, D], stack=s1) for i in range(3)]
        t_xt = [T("xt%d" % i) for i in range(3)]
        ss = [sb("ss%d" % i, [128, 4], stack=s1) for i in range(2)]
        t_ss = [T("ss0"), T("ss1")]
        xs = [sb("xs%d" % i, [128, D], stack=s1) for i in range(2)]
        t_xs = [T("xs0"), T("xs1")]
        ptr = [ps("ptr%d" % i, [128, 512], stack=s1) for i in range(2)]
        t_ptr = [T("ptr0"), T("ptr1")]
        bufs = (xt, t_xt, ss, t_ss, xs, t_xs, ptr, t_ptr)
        for i in range(NT):
            norm_tile(x_d, i, hT, t_hT[i], a1c, 0, s1, bufs)
        P.barrier()
    if stage == "hT":
        dbg["hT"] = (hT, [128, 8, S], t_hT)
        return _finish(nc, P, es, dbg, out_d)

    bank = [ps("bank%d" % i, [128, 512]) for i in range(8)]
    t_bank = [T("bank%d" % i) for i in range(8)]

    t_nlam = T("nlam")
    with contextlib.ExitStack() as sl:
        ltmp = sb("ltmp", [1, 136], stack=sl)
        t_l = T("ltmp")
        P.memset("dve", ones_row[:], 1.0, [t_l])
        P.tt("dve", ltmp[:, 0:64], lam_sb[:, 0:64], lam_sb[:, 64:128], ALU.mult, [t_const], [t_l])
        P.tt("dve", ltmp[:, 64:128], lam_sb[:, 128:192], lam_sb[:, 192:256], ALU.mult, [t_const], [t_l])
        for j in range(2):
            P.op("dve", (lambda j=j: nc.vector.reduce_sum(out=ltmp[:, 128 + j:129 + j],
                                                          in_=ltmp[:, 64 * j:64 * j + 64],
                                                          axis=mybir.AxisListType.X)), [t_l], [t_l])
        P.act(ltmp[:, 130:132], ltmp[:, 128:130], AF.Exp, [t_l], [t_l])
        P.tt("dve", ltmp[:, 132:133], ltmp[:, 130:131], ltmp[:, 131:132], ALU.subtract, [t_l], [t_l])
        P.ts("dve", ltmp[:, 133:134], ltmp[:, 132:133], LAMBDA_INIT, -1.0, ALU.add, ALU.mult, [t_l], [t_l])
        P.mm(bank[0][:, 0:1], ones_row[:, :], ltmp[:, 133:134], True, True, [t_l], [t_bank[0]])
        P.copy("dve", nlam[:], bank[0][:, 0:1], [t_bank[0]], [t_nlam])
        P.ts("dve", qgs[:, 0:1], cols[:, C_QG:C_QG + 1], 0.125, None, ALU.mult, None, [t_const], [t_nlam])
        P.copy("dve", qgs[:, 1:2], cols[:, C_KG:C_KG + 1], [t_const], [t_nlam])
        P.ts("dve", dgs[:], cols[:, C_DG:C_DG + 1], 1.0 - LAMBDA_INIT, None, ALU.mult, None,
             [t_const], [t_nlam])
        P.barrier()

    ob_d = nc.dram_tensor("ob_scratch", [128, 8, S], BF16, kind="Internal").ap()
    wg_s = nc.dram_tensor("wg_s", [D, 3088], BF16, kind="Internal").ap()
    wga_s = nc.dram_tensor("wga_s", [D, D], BF16, kind="Internal").ap()
    wgo_s = nc.dram_tensor("wgo_s", [D, D], BF16, kind="Internal").ap()
    wo_s = nc.dram_tensor("wo_s", [D, D], BF16, kind="Internal").ap()
    wup_s = nc.dram_tensor("wup_s", [D, 2 * F], BF16, kind="Internal").ap()
    wdn_s = nc.dram_tensor("wdn_s", [F, D], BF16, kind="Internal").ap()
    wgb_s = nc.dram_tensor("wgb_s", [D, D], BF16, kind="Internal").ap()
    wdo_s = nc.dram_tensor("wdo_s", [D, D], BF16, kind="Internal").ap()
    t_pre = T("precast")
    precast = [
        (wgb_s[:, :], win_d[:, OFF_GATEB:OFF_GATEB + D]), (wdo_s[:, :], wdiffo_d[:, :]),
        (wg_s[:, 0:1544], win_d[:, 0:1544]), (wg_s[:, 1544:3088], win_d[:, 1544:3088]),
        (wga_s[:, :], win_d[:, OFF_GATEA:OFF_GATEA + D]), (wgo_s[:, :], wglao_d[:, :]),
        (wo_s[:, :], wout_d[:, :]),
    ] + [(wup_s[:, q4 * 1408:(q4 + 1) * 1408], wup_d[:, q4 * 1408:(q4 + 1) * 1408]) for q4 in range(4)] + [
        (wdn_s[0:1024, :], wdown_d[0:1024, :]), (wdn_s[1024:2048, :], wdown_d[1024:2048, :]),
        (wdn_s[2048:2816, :], wdown_d[2048:2816, :]),
    ]

    def emit_precast(k):
        for o_, i_ in precast[:k]:
            P.op("pool", (lambda o_=o_, i_=i_: nc.gpsimd.dma_start(out=o_, in_=i_)), (), [t_pre],
                 dma="w", owner=t_pre)
        del precast[:k]
    win_v = win_d.rearrange("(k p) n -> p k n", p=128)
    with contextlib.ExitStack() as sa:
        QKb = [sb("QK%d" % i, [128, 4, S], BF16, stack=sa) for i in range(2)]
        t_q = [[T("q%d_%d" % (b_, i)) for i in range(8)] for b_ in range(2)]
        t_k = [[T("k%d_%d" % (b_, i)) for i in range(8)] for b_ in range(2)]
        t_aug = T("aug")
        Vb = [sb("Vaug%d" % i, [128, NT, 130], BF16, stack=sa) for i in range(2)]
        t_v = [[T("v%d_%d" % (b_, i)) for i in range(8)] for b_ in range(2)]
        wqkv = [sb("wqkv%d" % i, [128, 3, 8, 128], BF16, stack=sa) for i in range(2)]
        t_w = [T("wqkv0"), T("wqkv1")]
        PT = [sb("PT%d" % i, [128, 512], BF16, stack=sa) for i in range(3)]
        t_PT = [T("PT%d" % i) for i in range(3)]
        sq = [sb("sq%d" % i, [128, 512], BF16, stack=sa) for i in range(2)]
        t_sq = [T("sq0"), T("sq1")]
        rs = [sb("rs%d" % i, [128, 512], stack=sa) for i in range(2)]
        t_rs = [T("rs0"), T("rs1")]
        qraw = [sb("qraw%d" % i, [128, 512], stack=sa) for i in range(2)]
        t_qraw = [T("qraw0"), T("qraw1")]
        OTs = [[sb("OTs%d_%d" % (a, m_), [128, 512], stack=sa) for m_ in range(2)] for a in range(2)]
        SMs = [[sb("SMs%d_%d" % (a, m_), [128, 512], stack=sa) for m_ in range(2)] for a in range(2)]
        t_OTs = [[T("OTs%d_%d" % (a, m_)) for m_ in range(2)] for a in range(2)]
        t_SMs = [[T("SMs%d_%d" % (a, m_)) for m_ in range(2)] for a in range(2)]
        sqb = [sb("sqb%d" % i, [128, 512], BF16, stack=sa) for i in range(2)]
        t_sqb = [T("sqb0"), T("sqb1")]
        onb = [sb("onb%d" % i, [128, 512], BF16, stack=sa) for i in range(2)]
        t_onb = [T("onb0"), T("onb1")]
        rstd = sb("rstd", [128, 512], stack=sa)
        t_rstd = T("rstd")
        rscr = sb("rscr", [128, 512], stack=sa)
        t_rscr = T("rscr")
        ones_bf = sb("ones_bf", [128, 128], BF16, stack=sa)
        t_onesbf = T("ones_bf")
        P.memset("pool", ones_bf[:], 1.0, [t_onesbf])
        for b_ in range(2):
            P.memset("pool", QKb[b_][64:128, 0, :], 0.0, t_q[b_])
            P.memset("pool", QKb[b_][0:64, 1, :], 0.0, t_q[b_])
            P.memset("pool", QKb[b_][64:128, 2, :], 0.0, t_k[b_])
            P.memset("pool", QKb[b_][0:64, 3, :], 0.0, t_k[b_])
            P.memset("pool", Vb[b_][:, :, 128:130], 1.0, t_v[b_])

        def load_w(hd):
            hp = hd % 2
            wb, tw = wqkv[hp], t_w[hp]
            for j, off in enumerate((OFF_DQ, OFF_DK, OFF_DV)):
                P.load("pool", wb[:, j, :, :], win_v[:, :, off + hd * 128: off + (hd + 1) * 128],
                       [tw], owner=tw)

        def load_aug(hd):
            hp = hd % 2
            QK = QKb[hp]
            P.load("pool", QK[64:68, 0, :], aug_d[hd, 0, :, :], t_q[hp], owner=t_aug)
            P.load("pool", QK[0:4, 1, :], aug_d[hd, 0, :, :], t_q[hp], owner=t_aug)
            P.load("pool", QK[64:68, 2, :], aug_d[hd, 1, :, :], t_k[hp], owner=t_aug)
            P.load("pool", QK[0:4, 3, :], aug_d[hd, 1, :, :], t_k[hp], owner=t_aug)

        def projection(hd):
            hp = hd % 2
            QK, Vaug = QKb[hp], Vb[hp]
            wb, tw = wqkv[hp], t_w[hp]
            pq, tpq = bank[7], t_bank[7]
            it = 0
            for which in range(2):
                for tb in range(8):
                    b2 = it % 2
                    it += 1
                    tsl = slice(tb * 512, (tb + 1) * 512)
                    for kc in range(8):
                        P.mm(pq[:, :], wb[:, which, kc, :], hT[:, kc, tsl], kc == 0, kc == 7,
                             [tw] + t_hT[tb * 4:tb * 4 + 4], [tpq])
                    P.copy("dve", qraw[b2][:], pq[:, :], [tpq], [t_qraw[b2]])
                    P.tt("dve", sq[b2][:], qraw[b2][:], qraw[b2][:], ALU.mult, [t_qraw[b2]], [t_sq[b2]])
                    P.mm(pq[:, :], blockones, sq[b2][:], True, True, [t_sq[b2], t_const], [tpq])
                    P.act(rs[b2][:], pq[:, :], AF.Ln, [tpq], [t_rs[b2]], bias=EPS, scale=1.0 / 64)
                    P.act(rs[b2][:], rs[b2][:], AF.Exp, [t_rs[b2]], [t_rs[b2]], scale=-0.5)
                    tdst = t_q[hp][tb] if which == 0 else t_k[hp][tb]
                    P.stt(QK[0:64, 2 * which, tsl], qraw[b2][0:64, :], qgs[0:64, which:which + 1],
                          rs[b2][0:64, :], ALU.mult, ALU.mult, [t_qraw[b2], t_rs[b2], t_nlam], [tdst])
                    P.stt(QK[64:128, 2 * which + 1, tsl], qraw[b2][64:128, :], qgs[64:128, which:which + 1],
                          rs[b2][64:128, :], ALU.mult, ALU.mult, [t_qraw[b2], t_rs[b2], t_nlam], [tdst])
            for g in range(8):
                for tt_ in range(4):
                    ti = g * 4 + tt_
                    for kc in range(8):
                        P.mm(pq[:, tt_ * 128:(tt_ + 1) * 128], hT[:, kc, ti * 128:(ti + 1) * 128],
                             wb[:, 2, kc, :], kc == 0, kc == 7, [tw, t_hT[ti]], [tpq])
                for tt_ in range(4):
                    ti = g * 4 + tt_
                    P.copy("dve", Vaug[:, ti, 0:128], pq[:, tt_ * 128:(tt_ + 1) * 128], [tpq], [t_v[hp][g]])

        def attention(hd):
            hp = hd % 2
            QK, Vaug = QKb[hp], Vb[hp]
            cm_h = cbf[:, 256 + hd * 128: 256 + (hd + 1) * 128]
            items = []
            for qb in range(8):
                for mp in range(2):
                    for kt in range(4 * qb + 4):
                        items.append(("qk", qb, mp, kt))
                if qb >= 1:
                    items.insert(len(items) - 6, ("ssq", qb - 1))
            items.append(("ssq", 7))
            items.insert(len(items) - 1, ("nop",))
            items.insert(len(items) - 1, ("nop",))
            nst = len(items)

            def c0_of(it):
                _, qb, mp, kt = it
                jj = kt - 4 * qb
                return 0 if jj < 0 else jj * 128

            def stage1(n):
                it = items[n]
                pl, tpl = bank[4 + n % 3], t_bank[4 + n % 3]
                if it[0] == "qk":
                    _, qb, mp, kt = it
                    jj = kt - 4 * qb
                    c0 = c0_of(it)
                    P.mm(pl[:, c0:512], QK[:, 2 + mp, kt * 128:(kt + 1) * 128],
                         QK[:, mp, qb * 512 + c0:(qb + 1) * 512], True, jj < 0,
                         [t_k[hp][kt // 4], t_q[hp][qb]], [tpl])
                    if jj >= 0:
                        P.mm(pl[:, c0:c0 + 128], ident_bf, cm_h, False, True, [t_const], [tpl])
                elif it[0] == "ssq":
                    qb = it[1]
                    P.mm(pl[:, :], ones_bf[:], sqb[qb % 2][:], True, True, [t_sqb[qb % 2], t_onesbf], [tpl])

            def stage2(n):
                it = items[n]
                pl, tpl = bank[4 + n % 3], t_bank[4 + n % 3]
                if it[0] == "qk":
                    c0 = c0_of(it)
                    P.act(PT[n % 3][:, c0:512], pl[:, c0:512], AF.Exp, [tpl], [t_PT[n % 3]])
                elif it[0] == "ssq":
                    qb = it[1]
                    P.act(rstd[:], pl[:, :], AF.Ln, [tpl], [t_rstd], bias=EPS, scale=1.0 / 128)
                    P.act(rstd[:], rstd[:], AF.Exp, [t_rstd], [t_rstd], scale=-0.5)

            def stage3(n):
                it = items[n]
                if it[0] == "qk":
                    _, qb, mp, kt = it
                    c0 = c0_of(it)
                    last = kt == 4 * qb + 3
                    bo, tbo = bank[2 * mp], t_bank[2 * mp]
                    bs, tbs = bank[2 * mp + 1], t_bank[2 * mp + 1]
                    P.mm(bo[:, c0:512], Vaug[:, kt, 0:128], PT[n % 3][:, c0:512], kt == 0, last,
                         [t_PT[n % 3], t_v[hp][kt // 4]], [tbo])
                    P.mm(bs[:, c0:512], ones_bf[:], PT[n % 3][:, c0:512], kt == 0, last,
                         [t_PT[n % 3], t_onesbf], [tbs])
                    if last:
                        e_ = qb % 2
                        sm_, tsm_ = SMs[e_][mp], t_SMs[e_][mp]
                        P.act(sm_[:], bs[:, :], AF.Ln, [tbs], [tsm_])
                        P.act(sm_[:], sm_[:], AF.Exp, [tsm_], [tsm_], scale=-1.0)
                        oa, ta = OTs[e_][0], t_OTs[e_][0]
                        if mp == 0:
                            P.tt("dve", oa[:], bo[:, :], sm_[:], ALU.mult, [tbo, tsm_], [ta])
                        else:
                            obb, tb_ = OTs[e_][1], t_OTs[e_][1]
                            P.stt(obb[:], bo[:, :], nlam[:, 0:1], sm_[:], ALU.mult, ALU.mult, [tbo, tsm_, t_nlam], [tb_])
                            P.tt("dve", oa[:], oa[:], obb[:], ALU.add, [ta, tb_], [ta])
                            P.tt("dve", sqb[e_][:], oa[:], oa[:], ALU.mult, [ta], [t_sqb[e_]])
                elif it[0] == "ssq":
                    qb = it[1]
                    e_ = qb % 2
                    P.tt("dve", onb[e_][:], OTs[e_][0][:], rstd[:], ALU.mult, [t_OTs[e_][0], t_rstd], [t_onb[e_]])
                    P.store("sp", ob_d[:, hd, qb * 512:(qb + 1) * 512], onb[e_][:], [t_onb[e_]])

            for n in range(nst + 2):
                if n < nst:
                    stage1(n)
                if 0 <= n - 1 < nst:
                    stage2(n - 1)
                if n - 2 >= 0:
                    stage3(n - 2)

        load_w(0)
        load_aug(0)
        load_w(1)
        projection(0)
        for hd in range(8):
            main = P.capture(lambda: attention(hd))

            def side_fn():
                if hd + 1 < 8:
                    load_aug(hd + 1)
                if hd + 2 < 8:
                    load_w(hd + 2)
                emit_precast(2)
                if hd + 1 < 8:
                    projection(hd + 1)
            side = P.capture(side_fn)
            P.emit_merged(main, side)
        emit_precast(len(precast))
        P.barrier()

    with contextlib.ExitStack() as sb2:
        wgb = sb("wgb", [128, 8, D], BF16, stack=sb2)
        wdo = sb("wdo", [128, 8, D], BF16, stack=sb2)
        t_wgb, t_wdo = T("wgb"), T("wdo")
        sig = sb("sig", [128, 8, 512], stack=sb2)
        t_sig = [T("sig%d" % i) for i in range(8)]
        mst = [sb("mst%d" % i, [128, 8, 512], BF16, stack=sb2) for i in range(2)]
        t_mst = [T("mst0"), T("mst1")]
        obl = [sb("obl%d" % i, [128, 8, 512], BF16, stack=sb2) for i in range(2)]
        t_obl = [T("obl0"), T("obl1")]
        P.load("sp", wgb[:], wgb_s.rearrange("(k p) n -> p k n", p=128), [t_wgb])
        P.load("sp", wdo[:], wdo_s.rearrange("(k p) n -> p k n", p=128), [t_wdo])
        for kc in range(8):
            P.ts("dve", wdo[:, kc, :], wdo[:, kc, :], dgs[:, 0:1], None, ALU.mult, None, [t_wdo, t_nlam], [t_wdo])
        n_ = 0
        for tb in range(8):
            tsl = slice(tb * 512, (tb + 1) * 512)
            th = t_hT[tb * 4:tb * 4 + 4]
            ol, tol = obl[tb % 2], t_obl[tb % 2]
            P.load("sp", ol[:], ob_d[:, :, tsl], [tol])
            for dc in range(8):
                pg, tpg = bank[n_ % 4], t_bank[n_ % 4]
                n_ += 1
                for kc in range(8):
                    P.mm(pg[:, :], wgb[:, kc, dc * 128:(dc + 1) * 128], hT[:, kc, tsl], kc == 0, kc == 7,
                         [t_wgb] + th, [tpg])
                P.act(sig[:, dc, :], pg[:, :], AF.Sigmoid, [tpg], [t_sig[dc]])
            for dc in range(8):
                pg, tpg = bank[n_ % 4], t_bank[n_ % 4]
                n_ += 1
                for kc in range(8):
                    P.mm(pg[:, :], wdo[:, kc, dc * 128:(dc + 1) * 128], ol[:, kc, :], kc == 0, kc == 7,
                         [t_wdo, tol], [tpg])
                P.tt("dve", mst[tb % 2][:, dc, :], pg[:, :], sig[:, dc, :], ALU.mult,
                     [tpg, t_sig[dc]], [t_mst[tb % 2]])
            P.store("sp", mb_d[:, :, tsl], mst[tb % 2][:], [t_mst[tb % 2]])
        P.barrier()
    s10.close()
    if stage == "mb":
        with contextlib.ExitStack() as sd:
            mdbg = sb("mdbg", [128, 8, S], BF16, stack=sd)
            t_md = T("mdbg")
            P.load("sp", mdbg[:], mb_d[:, :, :], [t_md])
            dbg["mT"] = (mdbg, [128, 8, S], [t_md])
            return _finish(nc, P, es, dbg, out_d)

    s13 = contextlib.ExitStack()
    wg = sb("wg", [128, 8, 3088], BF16, stack=s13)
    wga = sb("wga", [128, 8, D], BF16, stack=s13)
    wgo = sb("wgo", [128, 8, D], BF16, stack=s13)
    wo = sb("wo", [128, 8, D], BF16, stack=s13)
    walpha_sb = sb("walpha_sb", [33, 512], stack=s13)
    t_wg, t_wga, t_wgo, t_wo, t_wal = T("wg"), T("wga"), T("wgo"), T("wo"), T("wal")
    P.load("sp", wg[:], wg_s.rearrange("(k p) n -> p k n", p=128), [t_wg])
    P.load("sp", walpha_sb[:], walpha_d[:, :], [t_wal])
    P.load("sp", wga[:], wga_s.rearrange("(k p) n -> p k n", p=128), [t_wga])
    P.load("sp", wgo[:], wgo_s.rearrange("(k p) n -> p k n", p=128), [t_wgo])
    P.load("sp", wo[:], wo_s.rearrange("(k p) n -> p k n", p=128), [t_wo])
    for kc in range(8):
        P.ts("dve", wgo[:, kc, :], wgo[:, kc, :], cols[:, C_GLAG + kc % 2:C_GLAG + kc % 2 + 1], None,
             ALU.mult, None, [t_wgo, t_const], [t_wgo])
    ones128 = sb("ones128", [128, 128], stack=s13)
    Gt = sb("Gt", [128, 128], stack=s13)
    gt_bc = sb("gt_bc", [128, D], stack=s13)
    t_ones, t_G, t_gtbc = T("ones128"), T("G"), T("gtbc")
    P.memset("dve", ones128[:], 1.0, [t_ones])

    def bcast_cols(col0, dst, t_dst):
        for j in range(8):
            P.ts("dve", Gt[:], ones128[:], modc[:, col0 + j:col0 + j + 1], None, ALU.mult, None,
                 [t_ones, t_mod], [t_G])
            P.mm(bank[j // 4][:, (j % 4) * 128:(j % 4 + 1) * 128], Gt[:], ident, True, True,
                 [t_G, t_const], [t_bank[j // 4]])
        for hf in range(2):
            P.copy("dve", dst[:, hf * 512:(hf + 1) * 512], bank[hf][:, :], [t_bank[hf]], [t_dst])

    bcast_cols(16, gt_bc, t_gtbc)
    for kc in range(8):
        P.tt("dve", wo[:, kc, :], wo[:, kc, :], gt_bc[:], ALU.mult, [t_wo, t_gtbc], [t_wo])

    Sst = sb("Sst", [128, 4, 256], stack=s13)
    Sbf = sb("Sbf", [128, 4, 256], BF16, stack=s13)
    t_S = [T("S%d" % h) for h in range(4)]
    t_Sbf = [T("Sbf%d" % h) for h in range(4)]
    P.memset("pool", Sst[:], 0.0, t_S)
    P.memset("pool", Sbf[:], 0.0, t_Sbf)
    MF4 = sb("MF4", [128, 512], BF16, stack=s13)
    MB4 = sb("MB4", [128, 512], BF16, stack=s13)
    t_msk = T("msk")
    for h in range(4):
        P.copy("dve", MF4[:, h * 128:(h + 1) * 128], tri, [t_const], [t_msk])
        P.copy("dve", MB4[:, h * 128:(h + 1) * 128], upp, [t_const], [t_msk])
    acT = sb("acT", [33, 128], stack=s13)
    t_acT = T("acT")
    P.memset("dve", acT[:], 1.0, [t_acT])
    xt3 = [sb("xt3_%d" % i, [128, D], stack=s13) for i in range(2)]
    t_xt3 = [T("xt3_0"), T("xt3_1")]
    ss3 = [sb("ss3_%d" % i, [128, 4], stack=s13) for i in range(2)]
    t_ss3 = [T("ss3_0"), T("ss3_1")]
    scrA = sb("scrA", [128, D], stack=s13)
    scrB = sb("scrB", [128, D], stack=s13)
    t_scrA, t_scrB = T("scrA"), T("scrB")
    hTt = [sb("hTt%d" % i, [128, 8, 128], BF16, stack=s13) for i in range(2)]
    t_hTt = [T("hTt0"), T("hTt1")]
    v_sb = sb("v_sb", [128, D], BF16, stack=s13)
    sg = sb("sg", [128, D], BF16, stack=s13)
    Ep = sb("Ep", [128, 512], stack=s13)
    Em = sb("Em", [128, 512], stack=s13)
    scrC = sb("scrC", [128, 512], stack=s13)
    la = sb("la", [128, 512], stack=s13)
    qf = sb("qf", [128, 512], BF16, stack=s13)
    qbw = sb("qbw", [128, 512], BF16, stack=s13)
    kf = sb("kf", [128, 512], BF16, stack=s13)
    kbw = sb("kbw", [128, 512], BF16, stack=s13)
    kd = sb("kd", [128, 512], BF16, stack=s13)
    tmpF = sb("tmpF", [128, 512], BF16, stack=s13)
    tmpB = sb("tmpB", [128, 512], BF16, stack=s13)
    scT = sb("scT", [128, 512], BF16, stack=s13)
    oaT = sb("oaT", [128, D], BF16, stack=s13)
    m_t = sb("m_t", [128, D], BF16, stack=s13)
    mbt = [sb("mbt%d" % i, [128, D], BF16, stack=s13) for i in range(2)]
    t_mbt = [T("mbt0"), T("mbt1")]
    scrD = [sb("scrD%d" % i, [128, 512], stack=s13) for i in range(2)]
    t_scrD = [T("scrD0"), T("scrD1")]
    junk = sb("junk", [128, 256], BF16, stack=s13)
    smg = [sb("smg%d" % i, [128, 4], stack=s13) for i in range(4)]
    t_smg = [T("smg%d" % i) for i in range(4)]
    (t_v, t_sg, t_Ep, t_Em, t_scrC, t_la, t_qf, t_qbw, t_kf, t_kbw, t_kd, t_tF, t_tB, t_scT, t_oaT,
     t_mt, t_junk) = [T(n) for n in ("v", "sg", "Ep", "Em", "scrC", "la", "qf", "qbw", "kf", "kbw", "kd",
                                     "tF", "tB", "scT", "oaT", "mt", "junk")]
    t_dl = [T("dl%d" % h) for h in range(4)]
    xs3 = sb("xs3", [128, D], stack=s13)
    t_xs3 = T("xs3")
    bufs13 = (xt3, t_xt3, ss3, t_ss3, [xs3, xs3], [t_xs3, t_xs3], [bank[0], bank[1]],
              [t_bank[0], t_bank[1]])
    QS = 128.0 ** -0.5
    NTILES = NT if stage != "x1s" else 2
    def gate_chain(hb, th):
        for kc in range(8):
            P.mm(bank[4][0:16, 0:128], wg[:, kc, OFF_GA:OFF_GA + 16], hb[:, kc, :], kc == 0, kc == 7,
                 [t_wg, th], [t_bank[4]])
        P.copy("dve", acT[0:16, :], bank[4][0:16, 0:128], [t_bank[4]], [t_acT])
        P.mm(bank[5][:, :], acT[:, :], walpha_sb[:, :], True, True, [t_acT, t_wal], [t_bank[5]])
        P.act(scrC[:], bank[5][:, :], AF.Exp, [t_bank[5]], [t_scrC], scale=-1.0)
        P.act(la[:], scrC[:], AF.Ln, [t_scrC], [t_la], bias=1.0)
        for h in range(4):
            P.mm(bank[4][:, h * 128:(h + 1) * 128], la[:, h * 128:(h + 1) * 128], tri, True, True,
                 [t_la, t_const], [t_bank[4]])
        P.mm(bank[5][:, :], upp, la[:, :], True, True, [t_const, t_la], [t_bank[5]])
        P.act(Ep[:], bank[4][:, :], AF.Exp, [t_bank[4]], [t_Ep], scale=-1.0 / 16)
        P.act(Em[:], bank[4][:, :], AF.Exp, [t_bank[4]], [t_Em], scale=1.0 / 16)
        P.act(scrC[:], bank[5][:, :], AF.Exp, [t_bank[5]], [t_scrC], scale=-1.0 / 16)

    def gla_loads(i):
        P.load("sp", xt3[i % 2][:], x_d[i * 128:(i + 1) * 128, :], [t_xt3[i % 2]])
        P.load("sp", mbt[i % 2][:, :].rearrange("p (a b) -> p a b", a=8), mb_d[:, :, i * 128:(i + 1) * 128],
               [t_mbt[i % 2]])

    gla_loads(0)
    for i in range(NTILES):
        if i + 1 < NTILES:
            gla_loads(i + 1)
        bi = i % 2
        if i == 0:
            norm_a(x_d, 0, bufs13, preloaded=True)
            norm_b(0, hTt[0], t_hTt[0], a1c, 0, bufs13, out_is_tile=True)
        hb, th = hTt[i % 2], t_hTt[i % 2]
        mb_ = mbt[i % 2]
        if i == 0:
            gate_chain(hb, th)
        for h in range(4):
            for kc in range(8):
                P.mm(bank[2][:, h * 128:(h + 1) * 128], wg[:, kc, OFF_GQ + h * 128:OFF_GQ + (h + 1) * 128],
                     hb[:, kc, :], kc == 0, kc == 7, [t_wg, th], [t_bank[2]])
        for h in range(4):
            for kc in range(8):
                P.mm(bank[3][:, h * 128:(h + 1) * 128], wg[:, kc, OFF_GK + h * 128:OFF_GK + (h + 1) * 128],
                     hb[:, kc, :], kc == 0, kc == 7, [t_wg, th], [t_bank[3]])
        for hf in range(2):
            for kc in range(8):
                P.mm(bank[6 + hf][:, :], hb[:, kc, :], wg[:, kc, OFF_GV + hf * 512:OFF_GV + (hf + 1) * 512],
                     kc == 0, kc == 7, [t_wg, th], [t_bank[6 + hf]])
        P.stt(qf[:], bank[2][:, :], QS, Ep[:], ALU.mult, ALU.mult, [t_bank[2], t_Ep], [t_qf])
        P.stt(qbw[:], bank[2][:, :], QS, Em[:], ALU.mult, ALU.mult, [t_bank[2], t_Em], [t_qbw])
        P.tt("dve", kf[:], bank[3][:, :], Em[:], ALU.mult, [t_bank[3], t_Em], [t_kf])
        P.tt("dve", kbw[:], bank[3][:, :], Ep[:], ALU.mult, [t_bank[3], t_Ep], [t_kbw])
        for hf in range(2):
            P.copy("act", v_sb[:, hf * 512:(hf + 1) * 512], bank[6 + hf][:, :], [t_bank[6 + hf]], [t_v])
        for dc in range(8):
            for kc in range(8):
                P.mm(bank[dc // 4][:, (dc % 4) * 128:(dc % 4 + 1) * 128], wga[:, kc, dc * 128:(dc + 1) * 128],
                     hb[:, kc, :], kc == 0, kc == 7, [t_wga, th], [t_bank[dc // 4]])
        for kc in range(8):
            P.mm(bank[5][:, :], hb[:, kc, :], wg[:, kc, OFF_GK:OFF_GK + 512], kc == 0, kc == 7,
                 [t_wg, th], [t_bank[5]])
        sga = scrA
        for hf in range(2):
            sl_ = sga[:, hf * 512:(hf + 1) * 512]
            P.act(sl_, bank[hf][:, :], AF.Exp, [t_bank[hf]], [t_scrA], scale=-1.0)
        P.tt("dve", kd[:], bank[5][:, :], scrC[:], ALU.mult, [t_bank[5], t_scrC], [t_kd])
        for hf in range(2):
            for kc in range(8):
                P.mm(bank[6 + hf][:, :], hb[:, kc, :], wg[:, kc, OFF_GG + hf * 512:OFF_GG + (hf + 1) * 512],
                     kc == 0, kc == 7, [t_wg, th], [t_bank[6 + hf]])
            sd = scrD[hf]
            P.act(sd[:], bank[6 + hf][:, :], AF.Exp, [t_bank[6 + hf]], [t_scrD[hf]], scale=-1.0)
            P.act(sd[:], sd[:], AF.Ln, [t_scrD[hf]], [t_scrD[hf]], bias=1.0)
            P.act(sd[:], sd[:], AF.Exp, [t_scrD[hf]], [t_scrD[hf]], scale=-1.0)
        for h in range(4):
            hs = slice(h * 128, (h + 1) * 128)
            P.mm(bank[0][:, hs], kf[:, hs], qf[:, hs], True, True, [t_kf, t_qf], [t_bank[0]])
        for h in range(4):
            hs = slice(h * 128, (h + 1) * 128)
            P.mm(bank[1][:, hs], kbw[:, hs], qbw[:, hs], True, True, [t_kbw, t_qbw], [t_bank[1]])
        P.tt("dve", tmpF[:], bank[0][:, :], MF4[:], ALU.mult, [t_bank[0], t_msk], [t_tF])
        P.tt("dve", tmpB[:], bank[1][:, :], MB4[:], ALU.mult, [t_bank[1], t_msk], [t_tB])
        P.tt("dve", scT[:], tmpF[:], tmpB[:], ALU.add, [t_tF, t_tB], [t_scT])
        for hf in range(2):
            P.tt("dve", sg[:, hf * 512:(hf + 1) * 512], bank[6 + hf][:, :], scrD[hf][:], ALU.mult,
                 [t_bank[6 + hf], t_scrD[hf]], [t_sg])
        obanks = [2, 3, 6, 7]
        og = scrB
        for h in range(4):
            hs = slice(h * 128, (h + 1) * 128)
            es_ = slice(h * 256, (h + 1) * 256)
            ob_, tob = bank[obanks[h]], t_bank[obanks[h]]
            P.mm(ob_[:, 0:256], scT[:, hs], v_sb[:, es_], True, False, [t_scT, t_v], [tob])
        for c in range(2):
            cs = slice(c * 64, (c + 1) * 64)
            for h in range(4):
                hs = slice(h * 128, (h + 1) * 128)
                es_ = slice(h * 256, (h + 1) * 256)
                ob_, tob = bank[obanks[h]], t_bank[obanks[h]]
                P.mm(ob_[cs, 0:256], qf[:, h * 128 + c * 64:h * 128 + (c + 1) * 64], Sbf[:, h, :],
                     False, c == 1, [t_qf, t_Sbf[h]], [tob])
            for h in range(4):
                hs = slice(h * 128, (h + 1) * 128)
                es_ = slice(h * 256, (h + 1) * 256)
                db_ = bank[4 + h // 2][:, (h % 2) * 256:(h % 2 + 1) * 256]
                tdb = t_bank[4 + h // 2]
                P.mm(db_, kd[cs, hs], v_sb[cs, es_], True, True, [t_kd, t_v], [tdb])
            if c == 1:
                for h in range(4):
                    es_ = slice(h * 256, (h + 1) * 256)
                    ob_, tob = bank[obanks[h]], t_bank[obanks[h]]
                    smb, tsm = smg[h], t_smg[h]
                    P.act(junk[:], ob_[:, 0:256], AF.Square, [tob], [t_junk, tsm], accum_out=smb[:, 0:1])
                    P.act(smb[:, 1:2], smb[:, 0:1], AF.Ln, [tsm], [tsm], bias=EPS, scale=1.0 / 256)
                    P.act(smb[:, 2:3], smb[:, 1:2], AF.Exp, [tsm], [tsm], scale=-0.5)
                    P.stt(og[:, es_], ob_[:, 0:256], smb[:, 2:3], sg[:, es_], ALU.mult, ALU.mult,
                          [tob, tsm, t_sg], [t_scrB])
            for h in range(4):
                db_ = bank[4 + h // 2][:, (h % 2) * 256:(h % 2 + 1) * 256]
                tdb = t_bank[4 + h // 2]
                dcol = h * 128 + 63 + 64 * c
                P.stt(Sst[:, h, :], Sst[:, h, :], Ep[:, dcol:dcol + 1], db_, ALU.mult, ALU.add,
                      [t_S[h], t_Ep, tdb], [t_S[h]])
                P.copy("act", Sbf[:, h, :], Sst[:, h, :], [t_S[h]], [t_Sbf[h]])
        for kc in range(8):
            P.tr(bank[kc // 4][:, (kc % 4) * 128:(kc % 4 + 1) * 128], og[:, kc * 128:(kc + 1) * 128], ident,
                 [t_scrB, t_const], [t_bank[kc // 4]])
        for hf in range(2):
            P.copy("act", oaT[:, hf * 512:(hf + 1) * 512], bank[hf][:, :], [t_bank[hf]], [t_oaT])
        for hf in range(2):
            sl_ = sga[:, hf * 512:(hf + 1) * 512]
            P.act(sl_, sl_, AF.Ln, [t_scrA], [t_scrA], bias=1.0)
            P.act(sl_, sl_, AF.Exp, [t_scrA], [t_scrA], scale=-1.0)
        if i + 1 < NTILES:
            norm_a(x_d, i + 1, bufs13, preloaded=True)
            norm_b(i + 1, hTt[(i + 1) % 2], t_hTt[(i + 1) % 2], a1c, 0, bufs13, out_is_tile=True)
        for dc in range(8):
            for kc in range(8):
                P.mm(bank[2 + dc // 4][:, (dc % 4) * 128:(dc % 4 + 1) * 128], wgo[:, kc, dc * 128:(dc + 1) * 128],
                     oaT[:, kc * 128:(kc + 1) * 128], kc == 0, kc == 7, [t_wgo, t_oaT], [t_bank[2 + dc // 4]])
        if i + 1 < NTILES:
            gate_chain(hTt[(i + 1) % 2], t_hTt[(i + 1) % 2])
        for hf in range(2):
            P.tt("dve", sga[:, hf * 512:(hf + 1) * 512], bank[2 + hf][:, :], sga[:, hf * 512:(hf + 1) * 512],
                 ALU.mult, [t_bank[2 + hf], t_scrA], [t_scrA])
        P.tt("dve", m_t[:], sga[:], mb_[:], ALU.add, [t_scrA, t_mbt[i % 2]], [t_mt])
        for hf in range(2):
            for kc in range(8):
                P.mm(bank[6 + hf][:, :], m_t[:, kc * 128:(kc + 1) * 128], wo[:, kc, hf * 512:(hf + 1) * 512],
                     kc == 0, kc == 7, [t_mt, t_wo], [t_bank[6 + hf]])
        for hf in range(2):
            P.tt("dve", xt3[bi][:, hf * 512:(hf + 1) * 512], bank[6 + hf][:, :],
                 xt3[bi][:, hf * 512:(hf + 1) * 512], ALU.add, [t_bank[6 + hf], t_xt3[bi]], [t_xt3[bi]])
        P.store("sp", out_d[i * 128:(i + 1) * 128, :], xt3[bi][:], [t_xt3[bi]])
    P.barrier()
    s13.close()
    if stage in ("x1", "x1s"):
        return _finish(nc, P, es, dbg, out_d)

    s2 = contextlib.ExitStack()
    wup = sb("wup", [128, 8, 2 * F], BF16, stack=s2)
    wdn = sb("wdn", [128, NFC, D], BF16, stack=s2)
    t_wup, t_wdn = T("wup"), T("wdn")
    wup_v = wup_s.rearrange("(k p) n -> p k n", p=128)
    for q4 in range(2):
        P.load("sp", wup[:, :, q4 * 2816:(q4 + 1) * 2816], wup_v[:, :, q4 * 2816:(q4 + 1) * 2816], [t_wup])
    wdn_v = wdn_s.rearrange("(c p) n -> p c n", p=128)
    P.load("sp", wdn[:, 0:11, :], wdn_v[:, 0:11, :], [t_wdn])
    P.load("sp", wdn[:, 11:22, :], wdn_v[:, 11:22, :], [t_wdn])
    xt2 = [sb("xt2_%d" % i, [128, D], stack=s2) for i in range(2)]
    t_xt2 = [T("xt2_0"), T("xt2_1")]
    xr = [sb("xr%d" % i, [128, D], stack=s2) for i in range(2)]
    t_xr = [T("xr0"), T("xr1")]
    ss2 = [sb("ss2_%d" % i, [128, 4], stack=s2) for i in range(2)]
    t_ss2 = [T("ss2_0"), T("ss2_1")]
    xs2 = sb("xs2", [128, D], stack=s2)
    t_xs2 = T("xs2")
    h2T = sb("h2T", [128, 8, 512], BF16, stack=s2)
    t_h2T = [T("h2T%d" % i) for i in range(4)]
    Uab = [[sb("U%d_%d" % (p_, i), [128, 514], stack=s2) for p_ in range(2)] for i in range(2)]
    accab = [[sb("acc%d_%d" % (p_, i), [128, 512], stack=s2) for p_ in range(2)] for i in range(2)]
    t_Uab = [[T("U%d_%d" % (p_, i)) for p_ in range(2)] for i in range(2)]
    t_accab = [[T("acc%d_%d" % (p_, i)) for p_ in range(2)] for i in range(2)]
    t_Ucab = [[T("Uc%d_%d" % (p_, i)) for p_ in range(2)] for i in range(2)]
    actT = sb("actT", [128, NFC, 512], BF16, stack=s2)
    t_actT = [T("actT%d" % i) for i in range(NFC)]
    car = sb("car", [128, 2 * NFC, 2], stack=s2)
    t_car = [T("car%d" % i) for i in range(2 * NFC)]
    P.memset("dve", car[:], 0.0, t_car)
    ones2, t_o2 = Uab[0][0][:, 0:128], t_Uab[0][0]
    Gt2, t_G2 = Uab[0][1][:, 0:128], t_Uab[0][1]
    P.memset("dve", ones2, 1.0, [t_o2])
    for j in range(8):
        P.ts("dve", Gt2, ones2, modc[:, 40 + j:41 + j], None, ALU.mult, None, [t_o2, t_mod], [t_G2])
        P.mm(bank[j // 4][:, (j % 4) * 128:(j % 4 + 1) * 128], Gt2, ident, True, True,
             [t_G2, t_const], [t_bank[j // 4]])
    for hf in range(2):
        P.copy("dve", accab[0][hf][:], bank[hf][:, :], [t_bank[hf]], [t_accab[0][hf]])
    for fc in range(NFC):
        for hf in range(2):
            P.tt("dve", wdn[:, fc, hf * 512:(hf + 1) * 512], wdn[:, fc, hf * 512:(hf + 1) * 512],
                 accab[0][hf][:], ALU.mult, [t_wdn, t_accab[0][hf]], [t_wdn])
    bufs2 = (xt2, t_xt2, ss2, t_ss2, [xs2, xs2], [t_xs2, t_xs2], [bank[6], bank[7]], [t_bank[6], t_bank[7]])
    NB = 8 if stage != "ffns" else 1

    def norm_block(tb):
        for tt_ in range(4):
            ti = tb * 4 + tt_
            norm_tile(out_d, ti, h2T, t_h2T[tt_], a2c, 24, None, bufs2, col=tt_)

    cwc = lambda j, ch: cols[:, C_CW + 44 * j + ch:C_CW + 44 * j + ch + 1]
    cbc = lambda ch: cols[:, C_CB + ch:C_CB + ch + 1]
    norm_block(0)
    for tb in range(NB):
        for fc in range(NFC):
            db = fc % 2
            parts = tuple((Uab[db][p_], t_Uab[db][p_], accab[db][p_], t_accab[db][p_]) for p_ in range(2))
            acca, t_acca, accb, t_accb = parts[0][2], parts[0][3], parts[1][2], parts[1][3]
            for part, (U_, tU, acc_, tacc) in enumerate(parts):
                ch = part * NFC + fc
                pu, tpu = bank[(2 * fc + part) % 4], t_bank[(2 * fc + part) % 4]
                for kc in range(8):
                    P.mm(pu[:, :], wup[:, kc, ch * 128:(ch + 1) * 128], h2T[:, kc, :], kc == 0, kc == 7,
                         [t_wup] + t_h2T, [tpu])
            for part, (U_, tU, acc_, tacc) in enumerate(parts):
                ch = part * NFC + fc
                pu, tpu = bank[(2 * fc + part) % 4], t_bank[(2 * fc + part) % 4]
                P.act(acc_[:], pu[:, :], AF.Identity, [tpu, t_const], [tacc], scale=cwc(2, ch), bias=cbc(ch))
            for tap in (1, 0):
                sh = 2 - tap
                for part, (U_, tU, acc_, tacc) in enumerate(parts):
                    ch = part * NFC + fc
                    pu, tpu = bank[(2 * fc + part) % 4], t_bank[(2 * fc + part) % 4]
                    P.stt(acc_[:, sh:512], pu[:, 0:512 - sh], cwc(tap, ch), acc_[:, sh:512], ALU.mult, ALU.add,
                          [tpu, t_const, tacc], [tacc])
            for part, (U_, tU, acc_, tacc) in enumerate(parts):
                ch = part * NFC + fc
                pu, tpu = bank[(2 * fc + part) % 4], t_bank[(2 * fc + part) % 4]
                P.stt(acc_[:, 0:1], car[:, ch, 1:2], cwc(1, ch), acc_[:, 0:1], ALU.mult, ALU.add,
                      [t_car[ch], t_const, tacc], [tacc])
                P.stt(acc_[:, 0:2], car[:, ch, 0:2], cwc(0, ch), acc_[:, 0:2], ALU.mult, ALU.add,
                      [t_car[ch], t_const, tacc], [tacc])
                P.copy("dve", car[:, ch, :], pu[:, 510:512], [tpu], [t_car[ch]])
            P.act(acca[:], acca[:], AF.Silu, [t_acca], [t_acca])
            P.tt("pool", actT[:, fc, :], acca[:], accb[:], ALU.mult, [t_acca, t_accb], [t_actT[fc]])
        for tt_ in range(4):
            ti = tb * 4 + tt_
            r_ = xr[ti % 2]
            tr_ = t_xr[ti % 2]
            P.load("sp", r_[:], out_d[ti * 128:(ti + 1) * 128, :], [tr_])
            b0 = 4
            for hf in range(2):
                for fc in range(NFC):
                    P.mm(bank[b0 + hf][:, :], actT[:, fc, tt_ * 128:(tt_ + 1) * 128],
                         wdn[:, fc, hf * 512:(hf + 1) * 512], fc == 0, fc == NFC - 1,
                         [t_actT[fc], t_wdn], [t_bank[b0 + hf]])
            nti = (tb + 1) * 4 + tt_
            if tb + 1 < NB:
                norm_a(out_d, nti, bufs2)
            for hf in range(2):
                P.tt("dve", r_[:, hf * 512:(hf + 1) * 512], bank[b0 + hf][:, :], r_[:, hf * 512:(hf + 1) * 512],
                     ALU.add, [t_bank[b0 + hf], tr_], [tr_])
            P.store("sp", out_d[ti * 128:(ti + 1) * 128, :], r_[:], [tr_])
            if tb + 1 < NB:
                norm_b(nti, h2T, t_h2T[tt_], a2c, 24, bufs2, col=tt_)
    P.barrier()
    P.finalize(es)
    return nc, {}


def _finish(nc, P, es, dbg, out_d):
    outs = {}
    for name, (tile_, shape, deps) in dbg.items():
        d = nc.dram_tensor("dbg_" + name, list(shape), tile_.dtype if hasattr(tile_, "dtype") else F32,
                           kind="ExternalOutput").ap()
        outs[name] = d
        P.store("pool", d, tile_[:], list(deps))
    P.barrier()
    P.finalize(es)
    return nc, outs


_CONSTS = None


def _in_maps(inputs, cores):
    global _CONSTS
    if _CONSTS is None:
        _CONSTS = _host_consts()
    inp = {k: np.asarray(v) for k, v in inputs.items()}
    return [_host_inputs(b, inp, _CONSTS) for b in cores]


_NC_CACHE = {}


def kernel(**inputs):
    if "nc" not in _NC_CACHE:
        _NC_CACHE["nc"] = build("full")[0]
    nc = _NC_CACHE["nc"]
    maps = _in_maps(inputs, list(range(8)))
    res = run_bass_kernel_spmd(nc, maps, core_ids=list(range(8)))
    out = np.stack([np.asarray(res.results[b]["out"]) for b in range(8)], axis=0)
    return out.astype(np.float32)
```

```python
import contextlib
import numpy as np
import concourse.bass as bass
import concourse.mybir as mybir
from concourse.bass_utils import run_bass_kernel_spmd

F32 = mybir.dt.float32
BF16 = mybir.dt.bfloat16
AF = mybir.ActivationFunctionType
ALU = mybir.AluOpType

D = 1024
S = 4096
NT = S // 128
EPS = 1e-6
F = 2816
NFC = F // 128
IN_DIM = 8208
OFF_GQ, OFF_GK, OFF_GV, OFF_GG, OFF_GA = 0, 512, 1024, 2048, 3072
OFF_DQ, OFF_DK, OFF_DV, OFF_GATEA, OFF_GATEB = 3088, 4112, 5136, 6160, 7184
LAMBDA_INIT = 0.8 - 0.6 * 1.0
NEG = -30000.0
NO_MERGE = False


class T:
    __slots__ = ("name", "w", "r", "wcnt", "rcnt")

    def __init__(self, name):
        self.name = name
        self.w = None
        self.r = {}
        self.wcnt = 0
        self.rcnt = 0


class Prog:
    def __init__(self, nc):
        self.nc = nc
        self.ops = []
        self.engs = {"pe": nc.tensor, "act": nc.scalar, "dve": nc.vector,
                     "pool": nc.gpsimd, "sp": nc.sync}

    def op(self, eng, fn, R=(), W=(), dma=None, owner=None):
        self.ops.append((eng, fn, tuple(R), tuple(W), dma, owner))

    def barrier(self):
        self.ops.append(("barrier", None, (), (), None, None))

    def capture(self, fn):
        saved, self.ops = self.ops, []
        fn()
        out, self.ops = self.ops, saved
        return out

    def emit_merged(self, main, side):
        if not side or NO_MERGE:
            self.ops.extend(main)
            self.ops.extend(side)
            return
        nm, ns = len(main), len(side)
        j = 0
        for i, o in enumerate(main):
            self.ops.append(o)
            tgt = (i + 1) * ns // nm
            while j < tgt:
                self.ops.append(side[j])
                j += 1
        self.ops.extend(side[j:])

    def mm(self, out, lhsT, rhs, start, stop, R, W, **kw):
        nc = self.nc
        self.op("pe", lambda: nc.tensor.matmul(out, lhsT=lhsT, rhs=rhs, start=start,
                                               stop=stop, **kw), R, W)

    def tr(self, out, in_, ident, R, W):
        nc = self.nc
        self.op("pe", lambda: nc.tensor.transpose(out, in_, ident), R, W)

    def act(self, out, in_, func, R, W, bias=None, scale=None, accum_out=None):
        nc = self.nc
        kw = {}
        if bias is not None:
            kw["bias"] = bias
        if scale is not None:
            kw["scale"] = scale
        if accum_out is not None:
            kw["accum_out"] = accum_out
        self.op("act", lambda: nc.scalar.activation(out=out, in_=in_, func=func, **kw), R, W)

    def ts(self, eng, out, in0, s1, s2, op0, op1, R, W):
        e = self.engs[eng]
        if op1 is None:
            self.op(eng, lambda: e.tensor_scalar(out=out, in0=in0, scalar1=s1, scalar2=None,
                                                 op0=op0), R, W)
        else:
            self.op(eng, lambda: e.tensor_scalar(out=out, in0=in0, scalar1=s1, scalar2=s2,
                                                 op0=op0, op1=op1), R, W)

    def stt(self, out, in0, scalar, in1, op0, op1, R, W):
        nc = self.nc
        self.op("dve", lambda: nc.vector.scalar_tensor_tensor(out=out, in0=in0, scalar=scalar,
                                                              in1=in1, op0=op0, op1=op1), R, W)

    def tt(self, eng, out, in0, in1, op, R, W):
        e = self.engs[eng]
        self.op(eng, lambda: e.tensor_tensor(out=out, in0=in0, in1=in1, op=op), R, W)

    def copy(self, eng, out, in_, R, W):
        if eng == "act":
            nc = self.nc
            self.op("act", lambda: nc.scalar.copy(out=out, in_=in_), R, W)
        else:
            e = self.engs[eng]
            self.op(eng, lambda: e.tensor_copy(out=out, in_=in_), R, W)

    def recip(self, out, in_, R, W):
        nc = self.nc
        self.op("dve", lambda: nc.vector.reciprocal(out=out, in_=in_), R, W)

    def memset(self, eng, ap, val, W):
        e = self.engs[eng]
        self.op(eng, lambda: e.memset(ap, val), (), W)

    def load(self, eng, out, in_, W, owner=None, **kw):
        e = self.engs[eng]
        self.op(eng, lambda: e.dma_start(out=out, in_=in_, **kw), (), W, dma="w",
                owner=owner if owner is not None else W[0])

    def store(self, eng, out, in_, R, owner=None):
        e = self.engs[eng]
        self.op(eng, lambda: e.dma_start(out=out, in_=in_), R, (), dma="r",
                owner=owner if owner is not None else R[0])

    def finalize(self, es):
        nc = self.nc
        ops = self.ops
        n = len(ops)
        deps_ops = [None] * n
        deps_dma = [None] * n
        signal = [False] * n
        dma_ev = [None] * n
        last_op = {}
        dma_latest = {}
        pending_bar = {}
        semkeys = {}
        for i, (eng, fn, R, W, dma, owner) in enumerate(ops):
            d_ops = {}
            d_dma = {}
            if eng == "barrier":
                for e, idx in last_op.items():
                    if e != "sp":
                        d_ops[e] = idx
                d_dma = dict(dma_latest)
                deps_ops[i], deps_dma[i] = d_ops, d_dma
                ev = ("op", "sp", i)
                last_op["sp"] = i
                for e in self.engs:
                    if e != "sp":
                        pending_bar[e] = ev
                continue

            mykey = (id(owner), dma) if dma is not None else None

            def add(ev, eng=eng, d_ops=d_ops, d_dma=d_dma, mykey=mykey):
                if ev is None:
                    return
                if ev[0] == "dma" and ev[1] == mykey:
                    return
                if ev[0] == "op":
                    e, idx = ev[1], ev[2]
                    if e == "pe" and eng == "pe":
                        return
                    if d_ops.get(e, -1) < idx:
                        d_ops[e] = idx
                else:
                    k, v = ev[1], ev[2]
                    if d_dma.get(k, 0) < v:
                        d_dma[k] = v

            if eng in pending_bar:
                add(pending_bar.pop(eng))
            for t in R:
                add(t.w)
            for t in W:
                add(t.w)
                for ev in t.r.values():
                    add(ev)
            deps_ops[i], deps_dma[i] = d_ops, d_dma
            if dma is None:
                ev = ("op", eng, i)
                rkey = eng
            else:
                key = (id(owner), dma)
                if key not in semkeys:
                    semkeys[key] = None
                if dma == "w":
                    owner.wcnt += 1
                    val = 16 * owner.wcnt
                else:
                    owner.rcnt += 1
                    val = 16 * owner.rcnt
                ev = ("dma", key, val)
                dma_ev[i] = (key, val)
                dma_latest[key] = val
                rkey = key
            if dma is None:
                last_op[eng] = i
            for t in R:
                t.r[rkey] = ev
            for t in W:
                t.w = ev
                t.r = {}
        for i in range(n):
            for e, idx in deps_ops[i].items():
                signal[idx] = True
        esem = {e: es.enter_context(nc.semaphore("c_" + e)) for e in self.engs}
        for j, key in enumerate(semkeys):
            semkeys[key] = es.enter_context(nc.semaphore("d%d" % j))
        self.n_sems = len(esem) + len(semkeys)
        rank = [0] * n
        cnt = {e: 0 for e in self.engs}
        for i, o in enumerate(ops):
            eng = "sp" if o[0] == "barrier" else o[0]
            if o[4] is not None:
                signal[i] = False
            elif signal[i] or o[0] == "barrier":
                cnt[eng] += 1
                signal[i] = True
            rank[i] = cnt[eng]
        known = {e: {} for e in self.engs}
        for i, (eng, fn, R, W, dma, owner) in enumerate(ops):
            real = "sp" if eng == "barrier" else eng
            e = self.engs[real]
            kn = known[real]
            for pe_, idx in deps_ops[i].items():
                v = rank[idx]
                if kn.get(pe_, 0) < v:
                    e.wait_ge(esem[pe_], v)
                    kn[pe_] = v
            for key, v in deps_dma[i].items():
                if kn.get(key, 0) < v:
                    e.wait_ge(semkeys[key], v)
                    kn[key] = v
            if eng == "barrier":
                ins = nc.sync.nop()
                ins.then_inc(esem["sp"], 1)
                continue
            ins = fn()
            if dma is not None:
                ins.then_inc(semkeys[dma_ev[i][0]], 16)
            elif signal[i]:
                ins.then_inc(esem[real], 1)
        self.max_rank = dict(cnt)


def _host_consts():
    p = np.arange(128)
    same = (p[:, None] // 64) == (p[None, :] // 64)
    ident = np.eye(128, dtype=np.float32)
    tri = (same & (p[:, None] <= p[None, :])).astype(np.float32)
    upp = (same & (p[:, None] > p[None, :])).astype(np.float32)
    cf32 = np.concatenate([ident, tri, upp], axis=1)
    blockones = same.astype(np.float32)
    cm = []
    for h in range(8):
        slope = 2.0 ** (-(h + 1))
        s = p[:, None]
        q = p[None, :]
        c = np.zeros((128, 128), np.float32)
        c = np.where(same & (s > q), -2.0 * slope * (s - q), c)
        c = np.where((s // 64) > (q // 64), NEG, c)
        cm.append(c.astype(np.float32))
    cbf = np.concatenate([ident, blockones] + cm, axis=1)
    t = np.arange(S)
    aug = np.zeros((8, 2, 4, S), np.float32)
    for h in range(8):
        slope = 2.0 ** (-(h + 1))
        aug[h, 0, 0] = -slope * 64.0 * (t // 64)
        aug[h, 0, 1] = -slope * (t % 64)
        aug[h, 0, 2] = 1.0
        aug[h, 0, 3] = 1.0
        aug[h, 1, 0] = 1.0
        aug[h, 1, 1] = 1.0
        aug[h, 1, 2] = slope * 64.0 * (t // 64)
        aug[h, 1, 3] = slope * (t % 64)
    return cf32, cbf, aug


def _col(v):
    v = np.asarray(v, np.float32)
    return np.ascontiguousarray(v.reshape(-1, 128).T)


NCOL = 48 + 8 + 8 + 1 + 1 + 132 + 44 + 2 + 1
C_BADA, C_G1, C_G2, C_QG, C_KG, C_CW, C_CB, C_GLAG, C_DG = 0, 48, 56, 64, 65, 66, 198, 242, 244


def _host_inputs(b, inp, consts):
    cf32, cbf, aug = consts
    cols = np.zeros((128, NCOL), np.float32)
    cols[:, C_BADA:C_BADA + 48] = _col(inp["b_ada"][0])
    cols[:, C_G1:C_G1 + 8] = _col(inp["norm1_g"][0])
    cols[:, C_G2:C_G2 + 8] = _col(inp["norm2_g"][0])
    cols[:, C_QG] = np.tile(inp["q_norm_g"][0], 2)
    cols[:, C_KG] = np.tile(inp["k_norm_g"][0], 2)
    cw = inp["conv_w"][0]
    for j in range(3):
        cols[:, C_CW + 44 * j:C_CW + 44 * (j + 1)] = _col(cw[j])
    cols[:, C_CB:C_CB + 44] = _col(inp["conv_b"][0])
    cols[:, C_GLAG:C_GLAG + 2] = _col(inp["gla_norm_g"][0])
    cols[:, C_DG] = inp["diff_norm_g"][0]
    lam = np.stack([inp["lam_q1"][0], inp["lam_k1"][0], inp["lam_q2"][0], inp["lam_k2"][0]])
    walpha = np.zeros((33, 512), np.float32)
    walpha[0:16] = inp["w_alpha_up"][0]
    walpha[32] = inp["b_alpha"][0]
    return {
        "x": np.ascontiguousarray(inp["x"][b]),
        "c_col": _col(inp["c"][b]),
        "cols": cols,
        "lam": np.ascontiguousarray(lam.reshape(1, 256)),
        "walpha": walpha,
        "w_ada": np.ascontiguousarray(inp["w_ada"][0]),
        "w_in": np.ascontiguousarray(inp["w_in"][0]),
        "w_gla_o": np.ascontiguousarray(inp["w_gla_o"][0]),
        "w_diff_o": np.ascontiguousarray(inp["w_diff_o"][0]),
        "w_out": np.ascontiguousarray(inp["w_out"][0]),
        "w_up": np.ascontiguousarray(inp["w_up"][0]),
        "w_down": np.ascontiguousarray(inp["w_down"][0]),
        "cf32": cf32, "cbf": cbf, "aug": aug,
    }


def build(stage="full"):
    nc = bass.Bass("TRN2", target_bir_lowering=False)
    P = Prog(nc)
    dram = {}

    def din(name, shape):
        dram[name] = nc.dram_tensor(name, list(shape), F32, kind="ExternalInput").ap()
        return dram[name]

    x_d = din("x", [S, D])
    ccol_d = din("c_col", [128, 8])
    cols_d = din("cols", [128, NCOL])
    lam_d = din("lam", [1, 256])
    walpha_d = din("walpha", [33, 512])
    wada_d = din("w_ada", [D, 6 * D])
    win_d = din("w_in", [D, IN_DIM])
    wglao_d = din("w_gla_o", [D, D])
    wdiffo_d = din("w_diff_o", [D, D])
    wout_d = din("w_out", [D, D])
    wup_d = din("w_up", [D, 2 * F])
    wdown_d = din("w_down", [F, D])
    cf32_d = din("cf32", [128, 384])
    cbf_d = din("cbf", [128, 1280])
    aug_d = din("aug", [8, 2, 4, S])
    out_d = nc.dram_tensor("out", [S, D], F32, kind="ExternalOutput").ap()
    dbg = {}

    es = contextlib.ExitStack()

    def sb(name, shape, dt=F32, stack=None):
        return (stack or es).enter_context(nc.sbuf_tensor("s_" + name, list(shape), dt))

    def ps(name, shape, dt=F32, stack=None):
        return (stack or es).enter_context(nc.psum_tensor("p_" + name, list(shape), dt))

    cf32 = sb("cf32", [128, 384])
    cbf = sb("cbf", [128, 1280], BF16)
    cols = sb("cols", [128, NCOL])
    ccol = sb("ccol", [128, 8])
    lam_sb = sb("lam_sb", [1, 256])
    t_const = T("const")
    P.load("sp", cf32[:], cf32_d[:, :], [t_const])
    P.load("pool", cbf[:], cbf_d[:, :], [t_const], owner=t_const)
    P.load("sp", cols[:], cols_d[:, :], [t_const], owner=t_const)
    P.load("sp", ccol[:], ccol_d[:, :], [t_const], owner=t_const)
    P.load("sp", lam_sb[:], lam_d[:, :], [t_const], owner=t_const)
    ident = cf32[:, 0:128]
    tri = cf32[:, 128:256]
    upp = cf32[:, 256:384]
    ident_bf = cbf[:, 0:128]
    blockones = cbf[:, 128:256]

    if stage == "c0":
        dbg["cols"] = (cols, [128, NCOL], [t_const])
        pass
        return _finish(nc, P, es, dbg, out_d)
    modc = sb("modc", [128, 48])
    a1c = sb("a1c", [128, 8])
    a2c = sb("a2c", [128, 8])
    t_mod = T("mod")

    with contextlib.ExitStack() as s0:
        silc = sb("silc", [128, 8], stack=s0)
        t_silc = T("silc")
        P.act(silc[:], ccol[:], AF.Silu, [t_const], [t_silc])
        if stage == "c1":
            dbg["silc"] = (silc, [128, 8], [t_silc])
            return _finish(nc, P, es, dbg, out_d)
        wa = [sb("wa%d" % i, [128, 8, 512], stack=s0) for i in range(2)]
        t_wa = [T("wa0"), T("wa1")]
        pm = ps("pm", [128, 48], stack=s0)
        t_pm = T("pm")
        wada_v = wada_d.rearrange("(k p) n -> p k n", p=128)
        for nb in range(12):
            bi = nb % 2
            P.load("sp", wa[bi][:], wada_v[:, :, nb * 512:(nb + 1) * 512], [t_wa[bi]])
            for j in range(4):
                nchunk = nb * 4 + j
                for kc in range(8):
                    P.mm(pm[:, nchunk:nchunk + 1], wa[bi][:, kc, j * 128:(j + 1) * 128],
                         silc[:, kc:kc + 1], kc == 0, kc == 7, [t_wa[bi], t_silc], [t_pm])
        P.tt("dve", modc[:], pm[:], cols[:, C_BADA:C_BADA + 48], ALU.add, [t_pm, t_const], [t_mod])
        if stage == "c2":
            dbg["modc"] = (modc, [128, 48], [t_mod])
            return _finish(nc, P, es, dbg, out_d)
        P.stt(a1c[:], modc[:, 8:16], 1.0, cols[:, C_G1:C_G1 + 8], ALU.add, ALU.mult,
              [t_mod, t_const], [t_mod])
        P.stt(a2c[:], modc[:, 32:40], 1.0, cols[:, C_G2:C_G2 + 8], ALU.add, ALU.mult,
              [t_mod, t_const], [t_mod])
    if stage == "mod":
        dbg["modc"] = (modc, [128, 48], [t_mod])
        dbg["a1c"] = (a1c, [128, 8], [t_mod])
        return _finish(nc, P, es, dbg, out_d)

    nlam = sb("nlam", [128, 1])
    ones_row = sb("ones_row", [1, 128])
    qgs = sb("qgs", [128, 2])
    dgs = sb("dgs", [128, 1])
    mb_d = nc.dram_tensor("mb_scratch", [128, 8, S], BF16, kind="Internal").ap()
    s10 = contextlib.ExitStack()
    hT = sb("hT", [128, 8, S], BF16, stack=s10)
    t_hT = [T("hT%d" % i) for i in range(NT)]

    def norm_a(src_ap_dram, i, bufs, preloaded=False):
        xt, t_xt, ss, t_ss, xs, t_xs, ptr, t_ptr = bufs
        bi = i % len(xt)
        if not preloaded:
            P.load("sp", xt[bi][:], src_ap_dram[i * 128:(i + 1) * 128, :], [t_xt[bi]])
        b2 = i % 2
        P.act(xs[b2][:], xt[bi][:], AF.Square, [t_xt[bi]], [t_xs[b2], t_ss[b2]],
              accum_out=ss[b2][:, 0:1])
        P.act(ss[b2][:, 1:2], ss[b2][:, 0:1], AF.Ln, [t_ss[b2]], [t_ss[b2]],
              bias=EPS, scale=1.0 / D)
        P.act(ss[b2][:, 2:3], ss[b2][:, 1:2], AF.Exp, [t_ss[b2]], [t_ss[b2]], scale=-0.5)
        P.ts("dve", xs[b2][:], xt[bi][:], ss[b2][:, 2:3], None, ALU.mult, None,
             [t_xt[bi], t_ss[b2]], [t_xs[b2]])
        return bi

    def norm_b(i, dst, t_dst, ac, shc_off, bufs, out_is_tile=False, col=None):
        xt, t_xt, ss, t_ss, xs, t_xs, ptr, t_ptr = bufs
        b2 = i % 2
        for half in range(2):
            pt = ptr[half]
            for j in range(4):
                kc = half * 4 + j
                P.tr(pt[:, j * 128:(j + 1) * 128], xs[b2][:, kc * 128:(kc + 1) * 128], ident,
                     [t_xs[b2], t_const], [t_ptr[half]])
            for j in range(4):
                kc = half * 4 + j
                if out_is_tile:
                    o = dst[:, kc, :]
                elif col is not None:
                    o = dst[:, kc, col * 128:(col + 1) * 128]
                else:
                    o = dst[:, kc, i * 128:(i + 1) * 128]
                P.ts("dve", o, pt[:, j * 128:(j + 1) * 128], ac[:, kc:kc + 1],
                     modc[:, shc_off + kc:shc_off + kc + 1], ALU.mult, ALU.add,
                     [t_ptr[half], t_mod], [t_dst])

    def norm_tile(src_ap_dram, i, dst, t_dst, ac, shc_off, stk, bufs, out_is_tile=False, col=None,
                  preloaded=False):
        bi = norm_a(src_ap_dram, i, bufs, preloaded)
        norm_b(i, dst, t_dst, ac, shc_off, bufs, out_is_tile, col)
        return bi

    with contextlib.ExitStack() as s1:
        xt = [sb("xt%d" % i, [128, D], stack=s1) for i in range(3)]
        t_xt = [T("xt%d" % i) for i in range(3)]
        ss = [sb("ss%d" % i, [128, 4], stack=s1) for i in range(2)]
        t_ss = [T("ss0"), T("ss1")]
        xs = [sb("xs%d" % i, [128, D], stack=s1) for i in range(2)]
        t_xs = [T("xs0"), T("xs1")]
        ptr = [ps("ptr%d" % i, [128, 512], stack=s1) for i in range(2)]
        t_ptr = [T("ptr0"), T("ptr1")]
        bufs = (xt, t_xt, ss, t_ss, xs, t_xs, ptr, t_ptr)
        for i in range(NT):
            norm_tile(x_d, i, hT, t_hT[i], a1c, 0, s1, bufs)
        P.barrier()
    if stage == "hT":
        dbg["hT"] = (hT, [128, 8, S], t_hT)
        return _finish(nc, P, es, dbg, out_d)

    bank = [ps("bank%d" % i, [128, 512]) for i in range(8)]
    t_bank = [T("bank%d" % i) for i in range(8)]

    t_nlam = T("nlam")
    with contextlib.ExitStack() as sl:
        ltmp = sb("ltmp", [1, 136], stack=sl)
        t_l = T("ltmp")
        P.memset("dve", ones_row[:], 1.0, [t_l])
        P.tt("dve", ltmp[:, 0:64], lam_sb[:, 0:64], lam_sb[:, 64:128], ALU.mult, [t_const], [t_l])
        P.tt("dve", ltmp[:, 64:128], lam_sb[:, 128:192], lam_sb[:, 192:256], ALU.mult, [t_const], [t_l])
        for j in range(2):
            P.op("dve", (lambda j=j: nc.vector.reduce_sum(out=ltmp[:, 128 + j:129 + j],
                                                          in_=ltmp[:, 64 * j:64 * j + 64],
                                                          axis=mybir.AxisListType.X)), [t_l], [t_l])
        P.act(ltmp[:, 130:132], ltmp[:, 128:130], AF.Exp, [t_l], [t_l])
        P.tt("dve", ltmp[:, 132:133], ltmp[:, 130:131], ltmp[:, 131:132], ALU.subtract, [t_l], [t_l])
        P.ts("dve", ltmp[:, 133:134], ltmp[:, 132:133], LAMBDA_INIT, -1.0, ALU.add, ALU.mult, [t_l], [t_l])
        P.mm(bank[0][:, 0:1], ones_row[:, :], ltmp[:, 133:134], True, True, [t_l], [t_bank[0]])
        P.copy("dve", nlam[:], bank[0][:, 0:1], [t_bank[0]], [t_nlam])
        P.ts("dve", qgs[:, 0:1], cols[:, C_QG:C_QG + 1], 0.125, None, ALU.mult, None, [t_const], [t_nlam])
        P.copy("dve", qgs[:, 1:2], cols[:, C_KG:C_KG + 1], [t_const], [t_nlam])
        P.ts("dve", dgs[:], cols[:, C_DG:C_DG + 1], 1.0 - LAMBDA_INIT, None, ALU.mult, None,
             [t_const], [t_nlam])
        P.barrier()

    ob_d = nc.dram_tensor("ob_scratch", [128, 8, S], BF16, kind="Internal").ap()
    wg_s = nc.dram_tensor("wg_s", [D, 3088], BF16, kind="Internal").ap()
    wga_s = nc.dram_tensor("wga_s", [D, D], BF16, kind="Internal").ap()
    wgo_s = nc.dram_tensor("wgo_s", [D, D], BF16, kind="Internal").ap()
    wo_s = nc.dram_tensor("wo_s", [D, D], BF16, kind="Internal").ap()
    wup_s = nc.dram_tensor("wup_s", [D, 2 * F], BF16, kind="Internal").ap()
    wdn_s = nc.dram_tensor("wdn_s", [F, D], BF16, kind="Internal").ap()
    wgb_s = nc.dram_tensor("wgb_s", [D, D], BF16, kind="Internal").ap()
    wdo_s = nc.dram_tensor("wdo_s", [D, D], BF16, kind="Internal").ap()
    t_pre = T("precast")
    precast = [
        (wgb_s[:, :], win_d[:, OFF_GATEB:OFF_GATEB + D]), (wdo_s[:, :], wdiffo_d[:, :]),
        (wg_s[:, 0:1544], win_d[:, 0:1544]), (wg_s[:, 1544:3088], win_d[:, 1544:3088]),
        (wga_s[:, :], win_d[:, OFF_GATEA:OFF_GATEA + D]), (wgo_s[:, :], wglao_d[:, :]),
        (wo_s[:, :], wout_d[:, :]),
    ] + [(wup_s[:, q4 * 1408:(q4 + 1) * 1408], wup_d[:, q4 * 1408:(q4 + 1) * 1408]) for q4 in range(4)] + [
        (wdn_s[0:1024, :], wdown_d[0:1024, :]), (wdn_s[1024:2048, :], wdown_d[1024:2048, :]),
        (wdn_s[2048:2816, :], wdown_d[2048:2816, :]),
    ]

    def emit_precast(k):
        for o_, i_ in precast[:k]:
            P.op("pool", (lambda o_=o_, i_=i_: nc.gpsimd.dma_start(out=o_, in_=i_)), (), [t_pre],
                 dma="w", owner=t_pre)
        del precast[:k]
    win_v = win_d.rearrange("(k p) n -> p k n", p=128)
    with contextlib.ExitStack() as sa:
        QKb = [sb("QK%d" % i, [128, 4, S], BF16, stack=sa) for i in range(2)]
        t_q = [[T("q%d_%d" % (b_, i)) for i in range(8)] for b_ in range(2)]
        t_k = [[T("k%d_%d" % (b_, i)) for i in range(8)] for b_ in range(2)]
        t_aug = T("aug")
        Vb = [sb("Vaug%d" % i, [128, NT, 130], BF16, stack=sa) for i in range(2)]
        t_v = [[T("v%d_%d" % (b_, i)) for i in range(8)] for b_ in range(2)]
        wqkv = [sb("wqkv%d" % i, [128, 3, 8, 128], BF16, stack=sa) for i in range(2)]
        t_w = [T("wqkv0"), T("wqkv1")]
        NPT = 4
        PT = [sb("PT%d" % i, [128, 512], BF16, stack=sa) for i in range(NPT)]
        t_PT = [T("PT%d" % i) for i in range(NPT)]
        sq = [sb("sq%d" % i, [128, 512], BF16, stack=sa) for i in range(2)]
        t_sq = [T("sq0"), T("sq1")]
        rs = [sb("rs%d" % i, [128, 512], stack=sa) for i in range(2)]
        t_rs = [T("rs0"), T("rs1")]
        qraw = [sb("qraw%d" % i, [128, 512], stack=sa) for i in range(2)]
        t_qraw = [T("qraw0"), T("qraw1")]
        OTs = [[sb("OTs%d_%d" % (a, m_), [128, 512], stack=sa) for m_ in range(2)] for a in range(2)]
        SMs = [[sb("SMs%d_%d" % (a, m_), [128, 512], stack=sa) for m_ in range(2)] for a in range(2)]
        t_OTs = [[T("OTs%d_%d" % (a, m_)) for m_ in range(2)] for a in range(2)]
        t_SMs = [[T("SMs%d_%d" % (a, m_)) for m_ in range(2)] for a in range(2)]
        sqb = [sb("sqb%d" % i, [128, 512], BF16, stack=sa) for i in range(2)]
        t_sqb = [T("sqb0"), T("sqb1")]
        onb = [sb("onb%d" % i, [128, 512], BF16, stack=sa) for i in range(2)]
        t_onb = [T("onb0"), T("onb1")]
        rstd = sb("rstd", [128, 512], stack=sa)
        t_rstd = T("rstd")
        rscr = sb("rscr", [128, 512], stack=sa)
        t_rscr = T("rscr")
        ones_bf = sb("ones_bf", [128, 128], BF16, stack=sa)
        t_onesbf = T("ones_bf")
        P.memset("pool", ones_bf[:], 1.0, [t_onesbf])
        for b_ in range(2):
            P.memset("pool", QKb[b_][64:128, 0, :], 0.0, t_q[b_])
            P.memset("pool", QKb[b_][0:64, 1, :], 0.0, t_q[b_])
            P.memset("pool", QKb[b_][64:128, 2, :], 0.0, t_k[b_])
            P.memset("pool", QKb[b_][0:64, 3, :], 0.0, t_k[b_])
            P.memset("pool", Vb[b_][:, :, 128:130], 1.0, t_v[b_])

        def load_w(hd):
            hp = hd % 2
            wb, tw = wqkv[hp], t_w[hp]
            for j, off in enumerate((OFF_DQ, OFF_DK, OFF_DV)):
                P.load("pool", wb[:, j, :, :], win_v[:, :, off + hd * 128: off + (hd + 1) * 128],
                       [tw], owner=tw)

        def load_aug(hd):
            hp = hd % 2
            QK = QKb[hp]
            P.load("pool", QK[64:68, 0, :], aug_d[hd, 0, :, :], t_q[hp], owner=t_aug)
            P.load("pool", QK[0:4, 1, :], aug_d[hd, 0, :, :], t_q[hp], owner=t_aug)
            P.load("pool", QK[64:68, 2, :], aug_d[hd, 1, :, :], t_k[hp], owner=t_aug)
            P.load("pool", QK[0:4, 3, :], aug_d[hd, 1, :, :], t_k[hp], owner=t_aug)

        def projection(hd):
            hp = hd % 2
            QK, Vaug = QKb[hp], Vb[hp]
            wb, tw = wqkv[hp], t_w[hp]
            pq, tpq = bank[7], t_bank[7]
            it = 0
            for which in range(2):
                for tb in range(8):
                    b2 = it % 2
                    it += 1
                    tsl = slice(tb * 512, (tb + 1) * 512)
                    for kc in range(8):
                        P.mm(pq[:, :], wb[:, which, kc, :], hT[:, kc, tsl], kc == 0, kc == 7,
                             [tw] + t_hT[tb * 4:tb * 4 + 4], [tpq])
                    P.copy("dve", qraw[b2][:], pq[:, :], [tpq], [t_qraw[b2]])
                    P.tt("dve", sq[b2][:], qraw[b2][:], qraw[b2][:], ALU.mult, [t_qraw[b2]], [t_sq[b2]])
                    P.mm(pq[:, :], blockones, sq[b2][:], True, True, [t_sq[b2], t_const], [tpq])
                    P.act(rs[b2][:], pq[:, :], AF.Ln, [tpq], [t_rs[b2]], bias=EPS, scale=1.0 / 64)
                    P.act(rs[b2][:], rs[b2][:], AF.Exp, [t_rs[b2]], [t_rs[b2]], scale=-0.5)
                    tdst = t_q[hp][tb] if which == 0 else t_k[hp][tb]
                    P.stt(QK[0:64, 2 * which, tsl], qraw[b2][0:64, :], qgs[0:64, which:which + 1],
                          rs[b2][0:64, :], ALU.mult, ALU.mult, [t_qraw[b2], t_rs[b2], t_nlam], [tdst])
                    P.stt(QK[64:128, 2 * which + 1, tsl], qraw[b2][64:128, :], qgs[64:128, which:which + 1],
                          rs[b2][64:128, :], ALU.mult, ALU.mult, [t_qraw[b2], t_rs[b2], t_nlam], [tdst])
            for g in range(8):
                for tt_ in range(4):
                    ti = g * 4 + tt_
                    for kc in range(8):
                        P.mm(pq[:, tt_ * 128:(tt_ + 1) * 128], hT[:, kc, ti * 128:(ti + 1) * 128],
                             wb[:, 2, kc, :], kc == 0, kc == 7, [tw, t_hT[ti]], [tpq])
                for tt_ in range(4):
                    ti = g * 4 + tt_
                    P.copy("dve", Vaug[:, ti, 0:128], pq[:, tt_ * 128:(tt_ + 1) * 128], [tpq], [t_v[hp][g]])

        def attention(hd):
            hp = hd % 2
            QK, Vaug = QKb[hp], Vb[hp]
            cm_h = cbf[:, 256 + hd * 128: 256 + (hd + 1) * 128]
            items = []
            for qb in range(8):
                for mp in range(2):
                    for kt in range(4 * qb + 4):
                        items.append(("qk", qb, mp, kt))
                if qb >= 1:
                    items.insert(len(items) - 6, ("ssq", qb - 1))
            items.append(("ssq", 7))
            items.insert(len(items) - 1, ("nop",))
            items.insert(len(items) - 1, ("nop",))
            nst = len(items)

            def c0_of(it):
                _, qb, mp, kt = it
                jj = kt - 4 * qb
                return 0 if jj < 0 else jj * 128

            def stage1(n):
                it = items[n]
                pl, tpl = bank[4 + n % 3], t_bank[4 + n % 3]
                if it[0] == "qk":
                    _, qb, mp, kt = it
                    jj = kt - 4 * qb
                    c0 = c0_of(it)
                    P.mm(pl[:, c0:512], QK[:, 2 + mp, kt * 128:(kt + 1) * 128],
                         QK[:, mp, qb * 512 + c0:(qb + 1) * 512], True, jj < 0,
                         [t_k[hp][kt // 4], t_q[hp][qb]], [tpl])
                    if jj >= 0:
                        P.mm(pl[:, c0:c0 + 128], ident_bf, cm_h, False, True, [t_const], [tpl])
                elif it[0] == "ssq":
                    qb = it[1]
                    P.mm(pl[:, :], ones_bf[:], sqb[qb % 2][:], True, True, [t_sqb[qb % 2], t_onesbf], [tpl])

            def stage2(n):
                it = items[n]
                pl, tpl = bank[4 + n % 3], t_bank[4 + n % 3]
                if it[0] == "qk":
                    c0 = c0_of(it)
                    P.act(PT[n % NPT][:, c0:512], pl[:, c0:512], AF.Exp, [tpl], [t_PT[n % NPT]])
                elif it[0] == "ssq":
                    qb = it[1]
                    P.act(rstd[:], pl[:, :], AF.Ln, [tpl], [t_rstd], bias=EPS, scale=1.0 / 128)
                    P.act(rstd[:], rstd[:], AF.Exp, [t_rstd], [t_rstd], scale=-0.5)

            def stage3(n):
                it = items[n]
                if it[0] == "qk":
                    _, qb, mp, kt = it
                    c0 = c0_of(it)
                    last = kt == 4 * qb + 3
                    bo, tbo = bank[2 * mp], t_bank[2 * mp]
                    bs, tbs = bank[2 * mp + 1], t_bank[2 * mp + 1]
                    P.mm(bo[:, c0:512], Vaug[:, kt, 0:128], PT[n % NPT][:, c0:512], kt == 0, last,
                         [t_PT[n % NPT], t_v[hp][kt // 4]], [tbo])
                    P.mm(bs[:, c0:512], ones_bf[:], PT[n % NPT][:, c0:512], kt == 0, last,
                         [t_PT[n % NPT], t_onesbf], [tbs])
                    if last:
                        e_ = qb % 2
                        sm_, tsm_ = SMs[e_][mp], t_SMs[e_][mp]
                        P.act(sm_[:], bs[:, :], AF.Ln, [tbs], [tsm_])
                        P.act(sm_[:], sm_[:], AF.Exp, [tsm_], [tsm_], scale=-1.0)
                        oa, ta = OTs[e_][0], t_OTs[e_][0]
                        if mp == 0:
                            P.tt("dve", oa[:], bo[:, :], sm_[:], ALU.mult, [tbo, tsm_], [ta])
                        else:
                            obb, tb_ = OTs[e_][1], t_OTs[e_][1]
                            P.stt(obb[:], bo[:, :], nlam[:, 0:1], sm_[:], ALU.mult, ALU.mult, [tbo, tsm_, t_nlam], [tb_])
                            P.tt("dve", oa[:], oa[:], obb[:], ALU.add, [ta, tb_], [ta])
                            P.tt("dve", sqb[e_][:], oa[:], oa[:], ALU.mult, [ta], [t_sqb[e_]])
                elif it[0] == "ssq":
                    qb = it[1]
                    e_ = qb % 2
                    P.tt("dve", onb[e_][:], OTs[e_][0][:], rstd[:], ALU.mult, [t_OTs[e_][0], t_rstd], [t_onb[e_]])
                    P.store("sp", ob_d[:, hd, qb * 512:(qb + 1) * 512], onb[e_][:], [t_onb[e_]])

            for n in range(nst + 2):
                if n < nst:
                    stage1(n)
                if 0 <= n - 1 < nst:
                    stage2(n - 1)
                if n - 2 >= 0:
                    stage3(n - 2)

        load_w(0)
        load_aug(0)
        load_w(1)
        projection(0)
        for hd in range(8):
            main = P.capture(lambda: attention(hd))

            def side_fn():
                if hd + 1 < 8:
                    load_aug(hd + 1)
                if hd + 2 < 8:
                    load_w(hd + 2)
                emit_precast(2)
                if hd + 1 < 8:
                    projection(hd + 1)
            side = P.capture(side_fn)
            P.emit_merged(main, side)
        emit_precast(len(precast))
        P.barrier()

    with contextlib.ExitStack() as sb2:
        wgb = sb("wgb", [128, 8, D], BF16, stack=sb2)
        wdo = sb("wdo", [128, 8, D], BF16, stack=sb2)
        t_wgb, t_wdo = T("wgb"), T("wdo")
        sig = sb("sig", [128, 8, 512], stack=sb2)
        t_sig = [T("sig%d" % i) for i in range(8)]
        mst = [sb("mst%d" % i, [128, 8, 512], BF16, stack=sb2) for i in range(2)]
        t_mst = [T("mst0"), T("mst1")]
        obl = [sb("obl%d" % i, [128, 8, 512], BF16, stack=sb2) for i in range(2)]
        t_obl = [T("obl0"), T("obl1")]
        P.load("sp", wgb[:], wgb_s.rearrange("(k p) n -> p k n", p=128), [t_wgb])
        P.load("sp", wdo[:], wdo_s.rearrange("(k p) n -> p k n", p=128), [t_wdo])
        for kc in range(8):
            P.ts("dve", wdo[:, kc, :], wdo[:, kc, :], dgs[:, 0:1], None, ALU.mult, None, [t_wdo, t_nlam], [t_wdo])
        n_ = 0
        for tb in range(8):
            tsl = slice(tb * 512, (tb + 1) * 512)
            th = t_hT[tb * 4:tb * 4 + 4]
            ol, tol = obl[tb % 2], t_obl[tb % 2]
            P.load("sp", ol[:], ob_d[:, :, tsl], [tol])
            for dc in range(8):
                pg, tpg = bank[n_ % 4], t_bank[n_ % 4]
                n_ += 1
                for kc in range(8):
                    P.mm(pg[:, :], wgb[:, kc, dc * 128:(dc + 1) * 128], hT[:, kc, tsl], kc == 0, kc == 7,
                         [t_wgb] + th, [tpg])
                P.act(sig[:, dc, :], pg[:, :], AF.Sigmoid, [tpg], [t_sig[dc]])
            for dc in range(8):
                pg, tpg = bank[n_ % 4], t_bank[n_ % 4]
                n_ += 1
                for kc in range(8):
                    P.mm(pg[:, :], wdo[:, kc, dc * 128:(dc + 1) * 128], ol[:, kc, :], kc == 0, kc == 7,
                         [t_wdo, tol], [tpg])
                P.tt("dve", mst[tb % 2][:, dc, :], pg[:, :], sig[:, dc, :], ALU.mult,
                     [tpg, t_sig[dc]], [t_mst[tb % 2]])
            P.store("sp", mb_d[:, :, tsl], mst[tb % 2][:], [t_mst[tb % 2]])
        P.barrier()
    s10.close()
    if stage == "mb":
        with contextlib.ExitStack() as sd:
            mdbg = sb("mdbg", [128, 8, S], BF16, stack=sd)
            t_md = T("mdbg")
            P.load("sp", mdbg[:], mb_d[:, :, :], [t_md])
            dbg["mT"] = (mdbg, [128, 8, S], [t_md])
            return _finish(nc, P, es, dbg, out_d)

    s13 = contextlib.ExitStack()
    wg = sb("wg", [128, 8, 3088], BF16, stack=s13)
    wga = sb("wga", [128, 8, D], BF16, stack=s13)
    wgo = sb("wgo", [128, 8, D], BF16, stack=s13)
    wo = sb("wo", [128, 8, D], BF16, stack=s13)
    walpha_sb = sb("walpha_sb", [33, 512], stack=s13)
    t_wg, t_wga, t_wgo, t_wo, t_wal = T("wg"), T("wga"), T("wgo"), T("wo"), T("wal")
    P.load("sp", wg[:], wg_s.rearrange("(k p) n -> p k n", p=128), [t_wg])
    P.load("sp", walpha_sb[:], walpha_d[:, :], [t_wal])
    P.load("sp", wga[:], wga_s.rearrange("(k p) n -> p k n", p=128), [t_wga])
    P.load("sp", wgo[:], wgo_s.rearrange("(k p) n -> p k n", p=128), [t_wgo])
    P.load("sp", wo[:], wo_s.rearrange("(k p) n -> p k n", p=128), [t_wo])
    for kc in range(8):
        P.ts("dve", wgo[:, kc, :], wgo[:, kc, :], cols[:, C_GLAG + kc % 2:C_GLAG + kc % 2 + 1], None,
             ALU.mult, None, [t_wgo, t_const], [t_wgo])
    ones128 = sb("ones128", [128, 128], stack=s13)
    Gt = sb("Gt", [128, 128], stack=s13)
    gt_bc = sb("gt_bc", [128, D], stack=s13)
    t_ones, t_G, t_gtbc = T("ones128"), T("G"), T("gtbc")
    P.memset("dve", ones128[:], 1.0, [t_ones])

    def bcast_cols(col0, dst, t_dst):
        for j in range(8):
            P.ts("dve", Gt[:], ones128[:], modc[:, col0 + j:col0 + j + 1], None, ALU.mult, None,
                 [t_ones, t_mod], [t_G])
            P.mm(bank[j // 4][:, (j % 4) * 128:(j % 4 + 1) * 128], Gt[:], ident, True, True,
                 [t_G, t_const], [t_bank[j // 4]])
        for hf in range(2):
            P.copy("dve", dst[:, hf * 512:(hf + 1) * 512], bank[hf][:, :], [t_bank[hf]], [t_dst])

    bcast_cols(16, gt_bc, t_gtbc)
    for kc in range(8):
        P.tt("dve", wo[:, kc, :], wo[:, kc, :], gt_bc[:], ALU.mult, [t_wo, t_gtbc], [t_wo])

    Sst = sb("Sst", [128, 4, 256], stack=s13)
    Sbf = sb("Sbf", [128, 4, 256], BF16, stack=s13)
    t_S = [T("S%d" % h) for h in range(4)]
    t_Sbf = [T("Sbf%d" % h) for h in range(4)]
    P.memset("pool", Sst[:], 0.0, t_S)
    P.memset("pool", Sbf[:], 0.0, t_Sbf)
    MF4 = sb("MF4", [128, 512], BF16, stack=s13)
    MB4 = sb("MB4", [128, 512], BF16, stack=s13)
    t_msk = T("msk")
    for h in range(4):
        P.copy("dve", MF4[:, h * 128:(h + 1) * 128], tri, [t_const], [t_msk])
        P.copy("dve", MB4[:, h * 128:(h + 1) * 128], upp, [t_const], [t_msk])
    acT = sb("acT", [33, 128], stack=s13)
    t_acT = T("acT")
    P.memset("dve", acT[:], 1.0, [t_acT])
    xt3 = [sb("xt3_%d" % i, [128, D], stack=s13) for i in range(2)]
    t_xt3 = [T("xt3_0"), T("xt3_1")]
    ss3 = [sb("ss3_%d" % i, [128, 4], stack=s13) for i in range(2)]
    t_ss3 = [T("ss3_0"), T("ss3_1")]
    scrA = sb("scrA", [128, D], stack=s13)
    scrB = sb("scrB", [128, D], stack=s13)
    t_scrA, t_scrB = T("scrA"), T("scrB")
    hTt = [sb("hTt%d" % i, [128, 8, 128], BF16, stack=s13) for i in range(2)]
    t_hTt = [T("hTt0"), T("hTt1")]
    v_sb = sb("v_sb", [128, D], BF16, stack=s13)
    sg = sb("sg", [128, D], BF16, stack=s13)
    Ep = sb("Ep", [128, 512], stack=s13)
    Em = sb("Em", [128, 512], stack=s13)
    scrC = sb("scrC", [128, 512], stack=s13)
    la = sb("la", [128, 512], stack=s13)
    qf = sb("qf", [128, 512], BF16, stack=s13)
    qbw = sb("qbw", [128, 512], BF16, stack=s13)
    kf = sb("kf", [128, 512], BF16, stack=s13)
    kbw = sb("kbw", [128, 512], BF16, stack=s13)
    kd = sb("kd", [128, 512], BF16, stack=s13)
    tmpF = sb("tmpF", [128, 512], BF16, stack=s13)
    tmpB = sb("tmpB", [128, 512], BF16, stack=s13)
    scT = sb("scT", [128, 512], BF16, stack=s13)
    oaT = sb("oaT", [128, D], BF16, stack=s13)
    m_t = sb("m_t", [128, D], BF16, stack=s13)
    mbt = [sb("mbt%d" % i, [128, D], BF16, stack=s13) for i in range(2)]
    t_mbt = [T("mbt0"), T("mbt1")]
    scrD = [sb("scrD%d" % i, [128, 512], stack=s13) for i in range(2)]
    t_scrD = [T("scrD0"), T("scrD1")]
    junk = sb("junk", [128, 256], BF16, stack=s13)
    smg = [sb("smg%d" % i, [128, 4], stack=s13) for i in range(4)]
    t_smg = [T("smg%d" % i) for i in range(4)]
    (t_v, t_sg, t_Ep, t_Em, t_scrC, t_la, t_qf, t_qbw, t_kf, t_kbw, t_kd, t_tF, t_tB, t_scT, t_oaT,
     t_mt, t_junk) = [T(n) for n in ("v", "sg", "Ep", "Em", "scrC", "la", "qf", "qbw", "kf", "kbw", "kd",
                                     "tF", "tB", "scT", "oaT", "mt", "junk")]
    t_dl = [T("dl%d" % h) for h in range(4)]
    xs3 = sb("xs3", [128, D], stack=s13)
    t_xs3 = T("xs3")
    bufs13 = (xt3, t_xt3, ss3, t_ss3, [xs3, xs3], [t_xs3, t_xs3], [bank[0], bank[1]],
              [t_bank[0], t_bank[1]])
    QS = 128.0 ** -0.5
    NTILES = NT if stage != "x1s" else 2
    def gate_chain(hb, th):
        for kc in range(8):
            P.mm(bank[4][0:16, 0:128], wg[:, kc, OFF_GA:OFF_GA + 16], hb[:, kc, :], kc == 0, kc == 7,
                 [t_wg, th], [t_bank[4]])
        P.copy("dve", acT[0:16, :], bank[4][0:16, 0:128], [t_bank[4]], [t_acT])
        P.mm(bank[5][:, :], acT[:, :], walpha_sb[:, :], True, True, [t_acT, t_wal], [t_bank[5]])
        P.act(scrC[:], bank[5][:, :], AF.Exp, [t_bank[5]], [t_scrC], scale=-1.0)
        P.act(la[:], scrC[:], AF.Ln, [t_scrC], [t_la], bias=1.0)
        for h in range(4):
            P.mm(bank[4][:, h * 128:(h + 1) * 128], la[:, h * 128:(h + 1) * 128], tri, True, True,
                 [t_la, t_const], [t_bank[4]])
        P.mm(bank[5][:, :], upp, la[:, :], True, True, [t_const, t_la], [t_bank[5]])
        P.act(Ep[:], bank[4][:, :], AF.Exp, [t_bank[4]], [t_Ep], scale=-1.0 / 16)
        P.act(Em[:], bank[4][:, :], AF.Exp, [t_bank[4]], [t_Em], scale=1.0 / 16)
        P.act(scrC[:], bank[5][:, :], AF.Exp, [t_bank[5]], [t_scrC], scale=-1.0 / 16)

    def gla_loads(i):
        P.load("sp", xt3[i % 2][:], x_d[i * 128:(i + 1) * 128, :], [t_xt3[i % 2]])
        P.load("sp", mbt[i % 2][:, :].rearrange("p (a b) -> p a b", a=8), mb_d[:, :, i * 128:(i + 1) * 128],
               [t_mbt[i % 2]])

    gla_loads(0)
    for i in range(NTILES):
        if i + 1 < NTILES:
            gla_loads(i + 1)
        bi = i % 2
        if i == 0:
            norm_a(x_d, 0, bufs13, preloaded=True)
            norm_b(0, hTt[0], t_hTt[0], a1c, 0, bufs13, out_is_tile=True)
        hb, th = hTt[i % 2], t_hTt[i % 2]
        mb_ = mbt[i % 2]
        if i == 0:
            gate_chain(hb, th)
        for h in range(4):
            for kc in range(8):
                P.mm(bank[2][:, h * 128:(h + 1) * 128], wg[:, kc, OFF_GQ + h * 128:OFF_GQ + (h + 1) * 128],
                     hb[:, kc, :], kc == 0, kc == 7, [t_wg, th], [t_bank[2]])
        for h in range(4):
            for kc in range(8):
                P.mm(bank[3][:, h * 128:(h + 1) * 128], wg[:, kc, OFF_GK + h * 128:OFF_GK + (h + 1) * 128],
                     hb[:, kc, :], kc == 0, kc == 7, [t_wg, th], [t_bank[3]])
        for hf in range(2):
            for kc in range(8):
                P.mm(bank[6 + hf][:, :], hb[:, kc, :], wg[:, kc, OFF_GV + hf * 512:OFF_GV + (hf + 1) * 512],
                     kc == 0, kc == 7, [t_wg, th], [t_bank[6 + hf]])
        P.stt(qf[:], bank[2][:, :], QS, Ep[:], ALU.mult, ALU.mult, [t_bank[2], t_Ep], [t_qf])
        P.stt(qbw[:], bank[2][:, :], QS, Em[:], ALU.mult, ALU.mult, [t_bank[2], t_Em], [t_qbw])
        P.tt("dve", kf[:], bank[3][:, :], Em[:], ALU.mult, [t_bank[3], t_Em], [t_kf])
        P.tt("dve", kbw[:], bank[3][:, :], Ep[:], ALU.mult, [t_bank[3], t_Ep], [t_kbw])
        for hf in range(2):
            P.copy("act", v_sb[:, hf * 512:(hf + 1) * 512], bank[6 + hf][:, :], [t_bank[6 + hf]], [t_v])
        for dc in range(8):
            for kc in range(8):
                P.mm(bank[dc // 4][:, (dc % 4) * 128:(dc % 4 + 1) * 128], wga[:, kc, dc * 128:(dc + 1) * 128],
                     hb[:, kc, :], kc == 0, kc == 7, [t_wga, th], [t_bank[dc // 4]])
        for kc in range(8):
            P.mm(bank[5][:, :], hb[:, kc, :], wg[:, kc, OFF_GK:OFF_GK + 512], kc == 0, kc == 7,
                 [t_wg, th], [t_bank[5]])
        sga = scrA
        for hf in range(2):
            sl_ = sga[:, hf * 512:(hf + 1) * 512]
            P.act(sl_, bank[hf][:, :], AF.Exp, [t_bank[hf]], [t_scrA], scale=-1.0)
            P.act(sl_, sl_, AF.Ln, [t_scrA], [t_scrA], bias=1.0)
            P.act(sl_, sl_, AF.Exp, [t_scrA], [t_scrA], scale=-1.0)
        P.tt("dve", kd[:], bank[5][:, :], scrC[:], ALU.mult, [t_bank[5], t_scrC], [t_kd])
        for hf in range(2):
            for kc in range(8):
                P.mm(bank[6 + hf][:, :], hb[:, kc, :], wg[:, kc, OFF_GG + hf * 512:OFF_GG + (hf + 1) * 512],
                     kc == 0, kc == 7, [t_wg, th], [t_bank[6 + hf]])
            sd = scrD[hf]
            P.act(sd[:], bank[6 + hf][:, :], AF.Exp, [t_bank[6 + hf]], [t_scrD[hf]], scale=-1.0)
            P.act(sd[:], sd[:], AF.Ln, [t_scrD[hf]], [t_scrD[hf]], bias=1.0)
            P.act(sd[:], sd[:], AF.Exp, [t_scrD[hf]], [t_scrD[hf]], scale=-1.0)
        for h in range(4):
            hs = slice(h * 128, (h + 1) * 128)
            P.mm(bank[0][:, hs], kf[:, hs], qf[:, hs], True, True, [t_kf, t_qf], [t_bank[0]])
        for h in range(4):
            hs = slice(h * 128, (h + 1) * 128)
            P.mm(bank[1][:, hs], kbw[:, hs], qbw[:, hs], True, True, [t_kbw, t_qbw], [t_bank[1]])
        P.tt("dve", tmpF[:], bank[0][:, :], MF4[:], ALU.mult, [t_bank[0], t_msk], [t_tF])
        P.tt("dve", tmpB[:], bank[1][:, :], MB4[:], ALU.mult, [t_bank[1], t_msk], [t_tB])
        P.tt("dve", scT[:], tmpF[:], tmpB[:], ALU.add, [t_tF, t_tB], [t_scT])
        for hf in range(2):
            P.tt("dve", sg[:, hf * 512:(hf + 1) * 512], bank[6 + hf][:, :], scrD[hf][:], ALU.mult,
                 [t_bank[6 + hf], t_scrD[hf]], [t_sg])
        obanks = [2, 3, 6, 7]
        og = scrB
        for h in range(4):
            hs = slice(h * 128, (h + 1) * 128)
            es_ = slice(h * 256, (h + 1) * 256)
            ob_, tob = bank[obanks[h]], t_bank[obanks[h]]
            P.mm(ob_[:, 0:256], scT[:, hs], v_sb[:, es_], True, False, [t_scT, t_v], [tob])
        for c in range(2):
            cs = slice(c * 64, (c + 1) * 64)
            for h in range(4):
                hs = slice(h * 128, (h + 1) * 128)
                es_ = slice(h * 256, (h + 1) * 256)
                ob_, tob = bank[obanks[h]], t_bank[obanks[h]]
                P.mm(ob_[cs, 0:256], qf[:, h * 128 + c * 64:h * 128 + (c + 1) * 64], Sbf[:, h, :],
                     False, c == 1, [t_qf, t_Sbf[h]], [tob])
            for h in range(4):
                hs = slice(h * 128, (h + 1) * 128)
                es_ = slice(h * 256, (h + 1) * 256)
                db_ = bank[4 + h // 2][:, (h % 2) * 256:(h % 2 + 1) * 256]
                tdb = t_bank[4 + h // 2]
                P.mm(db_, kd[cs, hs], v_sb[cs, es_], True, True, [t_kd, t_v], [tdb])
            if c == 1:
                for h in range(4):
                    es_ = slice(h * 256, (h + 1) * 256)
                    ob_, tob = bank[obanks[h]], t_bank[obanks[h]]
                    smb, tsm = smg[h], t_smg[h]
                    P.act(junk[:], ob_[:, 0:256], AF.Square, [tob], [t_junk, tsm], accum_out=smb[:, 0:1])
                    P.act(smb[:, 1:2], smb[:, 0:1], AF.Ln, [tsm], [tsm], bias=EPS, scale=1.0 / 256)
                    P.act(smb[:, 2:3], smb[:, 1:2], AF.Exp, [tsm], [tsm], scale=-0.5)
                    P.stt(og[:, es_], ob_[:, 0:256], smb[:, 2:3], sg[:, es_], ALU.mult, ALU.mult,
                          [tob, tsm, t_sg], [t_scrB])
            for h in range(4):
                db_ = bank[4 + h // 2][:, (h % 2) * 256:(h % 2 + 1) * 256]
                tdb = t_bank[4 + h // 2]
                dcol = h * 128 + 63 + 64 * c
                P.stt(Sst[:, h, :], Sst[:, h, :], Ep[:, dcol:dcol + 1], db_, ALU.mult, ALU.add,
                      [t_S[h], t_Ep, tdb], [t_S[h]])
                P.copy("act", Sbf[:, h, :], Sst[:, h, :], [t_S[h]], [t_Sbf[h]])
        for kc in range(8):
            P.tr(bank[kc // 4][:, (kc % 4) * 128:(kc % 4 + 1) * 128], og[:, kc * 128:(kc + 1) * 128], ident,
                 [t_scrB, t_const], [t_bank[kc // 4]])
        for hf in range(2):
            P.copy("act", oaT[:, hf * 512:(hf + 1) * 512], bank[hf][:, :], [t_bank[hf]], [t_oaT])
        if i + 1 < NTILES:
            norm_a(x_d, i + 1, bufs13, preloaded=True)
            norm_b(i + 1, hTt[(i + 1) % 2], t_hTt[(i + 1) % 2], a1c, 0, bufs13, out_is_tile=True)
        for dc in range(8):
            for kc in range(8):
                P.mm(bank[2 + dc // 4][:, (dc % 4) * 128:(dc % 4 + 1) * 128], wgo[:, kc, dc * 128:(dc + 1) * 128],
                     oaT[:, kc * 128:(kc + 1) * 128], kc == 0, kc == 7, [t_wgo, t_oaT], [t_bank[2 + dc // 4]])
        if i + 1 < NTILES:
            gate_chain(hTt[(i + 1) % 2], t_hTt[(i + 1) % 2])
        for hf in range(2):
            P.tt("dve", sga[:, hf * 512:(hf + 1) * 512], bank[2 + hf][:, :], sga[:, hf * 512:(hf + 1) * 512],
                 ALU.mult, [t_bank[2 + hf], t_scrA], [t_scrA])
        P.tt("dve", m_t[:], sga[:], mb_[:], ALU.add, [t_scrA, t_mbt[i % 2]], [t_mt])
        for hf in range(2):
            for kc in range(8):
                P.mm(bank[6 + hf][:, :], m_t[:, kc * 128:(kc + 1) * 128], wo[:, kc, hf * 512:(hf + 1) * 512],
                     kc == 0, kc == 7, [t_mt, t_wo], [t_bank[6 + hf]])
        for hf in range(2):
            P.tt("dve", xt3[bi][:, hf * 512:(hf + 1) * 512], bank[6 + hf][:, :],
                 xt3[bi][:, hf * 512:(hf + 1) * 512], ALU.add, [t_bank[6 + hf], t_xt3[bi]], [t_xt3[bi]])
        P.store("sp", out_d[i * 128:(i + 1) * 128, :], xt3[bi][:], [t_xt3[bi]])
    P.barrier()
    s13.close()
    if stage in ("x1", "x1s"):
        return _finish(nc, P, es, dbg, out_d)

    s2 = contextlib.ExitStack()
    wup = sb("wup", [128, 8, 2 * F], BF16, stack=s2)
    wdn = sb("wdn", [128, NFC, D], BF16, stack=s2)
    t_wup, t_wdn = T("wup"), T("wdn")
    wup_v = wup_s.rearrange("(k p) n -> p k n", p=128)
    for q4 in range(2):
        P.load("sp", wup[:, :, q4 * 2816:(q4 + 1) * 2816], wup_v[:, :, q4 * 2816:(q4 + 1) * 2816], [t_wup])
    wdn_v = wdn_s.rearrange("(c p) n -> p c n", p=128)
    P.load("sp", wdn[:, 0:11, :], wdn_v[:, 0:11, :], [t_wdn])
    P.load("sp", wdn[:, 11:22, :], wdn_v[:, 11:22, :], [t_wdn])
    xt2 = [sb("xt2_%d" % i, [128, D], stack=s2) for i in range(2)]
    t_xt2 = [T("xt2_0"), T("xt2_1")]
    xr = [sb("xr%d" % i, [128, D], stack=s2) for i in range(2)]
    t_xr = [T("xr0"), T("xr1")]
    ss2 = [sb("ss2_%d" % i, [128, 4], stack=s2) for i in range(2)]
    t_ss2 = [T("ss2_0"), T("ss2_1")]
    xs2 = sb("xs2", [128, D], stack=s2)
    t_xs2 = T("xs2")
    h2T = sb("h2T", [128, 8, 512], BF16, stack=s2)
    t_h2T = [T("h2T%d" % i) for i in range(4)]
    Uab = [[sb("U%d_%d" % (p_, i), [128, 514], stack=s2) for p_ in range(2)] for i in range(2)]
    accab = [[sb("acc%d_%d" % (p_, i), [128, 512], stack=s2) for p_ in range(2)] for i in range(2)]
    t_Uab = [[T("U%d_%d" % (p_, i)) for p_ in range(2)] for i in range(2)]
    t_accab = [[T("acc%d_%d" % (p_, i)) for p_ in range(2)] for i in range(2)]
    t_Ucab = [[T("Uc%d_%d" % (p_, i)) for p_ in range(2)] for i in range(2)]
    actT = sb("actT", [128, NFC, 512], BF16, stack=s2)
    t_actT = [T("actT%d" % i) for i in range(NFC)]
    car = sb("car", [128, 2 * NFC, 2], stack=s2)
    t_car = [T("car%d" % i) for i in range(2 * NFC)]
    P.memset("dve", car[:], 0.0, t_car)
    ones2, t_o2 = Uab[0][0][:, 0:128], t_Uab[0][0]
    Gt2, t_G2 = Uab[0][1][:, 0:128], t_Uab[0][1]
    P.memset("dve", ones2, 1.0, [t_o2])
    for j in range(8):
        P.ts("dve", Gt2, ones2, modc[:, 40 + j:41 + j], None, ALU.mult, None, [t_o2, t_mod], [t_G2])
        P.mm(bank[j // 4][:, (j % 4) * 128:(j % 4 + 1) * 128], Gt2, ident, True, True,
             [t_G2, t_const], [t_bank[j // 4]])
    for hf in range(2):
        P.copy("dve", accab[0][hf][:], bank[hf][:, :], [t_bank[hf]], [t_accab[0][hf]])
    for fc in range(NFC):
        for hf in range(2):
            P.tt("dve", wdn[:, fc, hf * 512:(hf + 1) * 512], wdn[:, fc, hf * 512:(hf + 1) * 512],
                 accab[0][hf][:], ALU.mult, [t_wdn, t_accab[0][hf]], [t_wdn])
    bufs2 = (xt2, t_xt2, ss2, t_ss2, [xs2, xs2], [t_xs2, t_xs2], [bank[6], bank[7]], [t_bank[6], t_bank[7]])
    NB = 8 if stage != "ffns" else 1

    def norm_block(tb):
        for tt_ in range(4):
            ti = tb * 4 + tt_
            norm_tile(out_d, ti, h2T, t_h2T[tt_], a2c, 24, None, bufs2, col=tt_)

    cwc = lambda j, ch: cols[:, C_CW + 44 * j + ch:C_CW + 44 * j + ch + 1]
    cbc = lambda ch: cols[:, C_CB + ch:C_CB + ch + 1]
    norm_block(0)
    for tb in range(NB):
        for fc in range(NFC):
            db = fc % 2
            parts = tuple((Uab[db][p_], t_Uab[db][p_], accab[db][p_], t_accab[db][p_]) for p_ in range(2))
            acca, t_acca, accb, t_accb = parts[0][2], parts[0][3], parts[1][2], parts[1][3]
            for part, (U_, tU, acc_, tacc) in enumerate(parts):
                ch = part * NFC + fc
                pu, tpu = bank[(2 * fc + part) % 4], t_bank[(2 * fc + part) % 4]
                for kc in range(8):
                    P.mm(pu[:, :], wup[:, kc, ch * 128:(ch + 1) * 128], h2T[:, kc, :], kc == 0, kc == 7,
                         [t_wup] + t_h2T, [tpu])
            for part, (U_, tU, acc_, tacc) in enumerate(parts):
                ch = part * NFC + fc
                pu, tpu = bank[(2 * fc + part) % 4], t_bank[(2 * fc + part) % 4]
                P.act(acc_[:], pu[:, :], AF.Identity, [tpu, t_const], [tacc], scale=cwc(2, ch), bias=cbc(ch))
            for tap in (1, 0):
                sh = 2 - tap
                for part, (U_, tU, acc_, tacc) in enumerate(parts):
                    ch = part * NFC + fc
                    pu, tpu = bank[(2 * fc + part) % 4], t_bank[(2 * fc + part) % 4]
                    P.stt(acc_[:, sh:512], pu[:, 0:512 - sh], cwc(tap, ch), acc_[:, sh:512], ALU.mult, ALU.add,
                          [tpu, t_const, tacc], [tacc])
            for part, (U_, tU, acc_, tacc) in enumerate(parts):
                ch = part * NFC + fc
                pu, tpu = bank[(2 * fc + part) % 4], t_bank[(2 * fc + part) % 4]
                P.stt(acc_[:, 0:1], car[:, ch, 1:2], cwc(1, ch), acc_[:, 0:1], ALU.mult, ALU.add,
                      [t_car[ch], t_const, tacc], [tacc])
                P.stt(acc_[:, 0:2], car[:, ch, 0:2], cwc(0, ch), acc_[:, 0:2], ALU.mult, ALU.add,
                      [t_car[ch], t_const, tacc], [tacc])
                P.copy("dve", car[:, ch, :], pu[:, 510:512], [tpu], [t_car[ch]])
            P.act(acca[:], acca[:], AF.Silu, [t_acca], [t_acca])
            P.tt("pool", actT[:, fc, :], acca[:], accb[:], ALU.mult, [t_acca, t_accb], [t_actT[fc]])
        for tt_ in range(4):
            ti = tb * 4 + tt_
            r_ = xr[ti % 2]
            tr_ = t_xr[ti % 2]
            P.load("sp", r_[:], out_d[ti * 128:(ti + 1) * 128, :], [tr_])
            b0 = 4
            for hf in range(2):
                for fc in range(NFC):
                    P.mm(bank[b0 + hf][:, :], actT[:, fc, tt_ * 128:(tt_ + 1) * 128],
                         wdn[:, fc, hf * 512:(hf + 1) * 512], fc == 0, fc == NFC - 1,
                         [t_actT[fc], t_wdn], [t_bank[b0 + hf]])
            nti = (tb + 1) * 4 + tt_
            if tb + 1 < NB:
                norm_a(out_d, nti, bufs2)
            for hf in range(2):
                P.tt("dve", r_[:, hf * 512:(hf + 1) * 512], bank[b0 + hf][:, :], r_[:, hf * 512:(hf + 1) * 512],
                     ALU.add, [t_bank[b0 + hf], tr_], [tr_])
            P.store("sp", out_d[ti * 128:(ti + 1) * 128, :], r_[:], [tr_])
            if tb + 1 < NB:
                norm_b(nti, h2T, t_h2T[tt_], a2c, 24, bufs2, col=tt_)
    P.barrier()
    P.finalize(es)
    return nc, {}


def _finish(nc, P, es, dbg, out_d):
    outs = {}
    for name, (tile_, shape, deps) in dbg.items():
        d = nc.dram_tensor("dbg_" + name, list(shape), tile_.dtype if hasattr(tile_, "dtype") else F32,
                           kind="ExternalOutput").ap()
        outs[name] = d
        P.store("pool", d, tile_[:], list(deps))
    P.barrier()
    P.finalize(es)
    return nc, outs


_CONSTS = None


def _in_maps(inputs, cores):
    global _CONSTS
    if _CONSTS is None:
        _CONSTS = _host_consts()
    inp = {k: np.asarray(v) for k, v in inputs.items()}
    return [_host_inputs(b, inp, _CONSTS) for b in cores]


_NC_CACHE = {}


def kernel(**inputs):
    if "nc" not in _NC_CACHE:
        _NC_CACHE["nc"] = build("full")[0]
    nc = _NC_CACHE["nc"]
    maps = _in_maps(inputs, list(range(8)))
    res = run_bass_kernel_spmd(nc, maps, core_ids=list(range(8)))
    out = np.stack([np.asarray(res.results[b]["out"]) for b in range(8)], axis=0)
    return out.astype(np.float32)
```
